# Optimizing a Trainium2 kernel written in Bass

```python
import math
import jax, jax.numpy as jnp
from jax import lax
import numpy as np

D_MODEL = 1024
BATCH = 16
SEQ = 2048
DEPTH = 1

DIFF_HEADS = 4
DIFF_HEAD_DIM = 64
DIFF_QK = DIFF_HEADS * 2 * DIFF_HEAD_DIM
DIFF_V = DIFF_HEADS * 2 * DIFF_HEAD_DIM
RWKV_HEAD_DIM = 64
RWKV_WIDTH = D_MODEL // 2
RWKV_HEADS = RWKV_WIDTH // RWKV_HEAD_DIM
DECAY_RANK = 64
ICLR_RANK = 64
GATE_RANK = 128
FFN_HIDDEN = -(-8 * D_MODEL // (3 * 256)) * 256
NUM_BUCKETS = 32
MAX_DISTANCE = 128
Q_BLOCK = 128
LN_EPS = 1e-5
LNX_EPS = 64e-5
NEG_BIG = -1e30
DEEPNORM_ALPHA = (2.0 * DEPTH) ** 0.25
DEEPNORM_BETA = (8.0 * DEPTH) ** -0.25

RWKV_SHIFT_WIDTH = 3 * RWKV_WIDTH + DECAY_RANK + ICLR_RANK + GATE_RANK
IN_SIZES = (DIFF_QK, DIFF_QK, DIFF_V, RWKV_SHIFT_WIDTH, D_MODEL, D_MODEL)
IN_SPLIT = tuple(int(s) for s in np.cumsum(IN_SIZES)[:-1])
W_IN_COLS = int(sum(IN_SIZES))
RWKV_SIZES = (RWKV_WIDTH, RWKV_WIDTH, RWKV_WIDTH, DECAY_RANK, ICLR_RANK, GATE_RANK)
RWKV_SPLIT = tuple(int(s) for s in np.cumsum(RWKV_SIZES)[:-1])

kernel_name = 'hybrid_diffattn_rwkv7_gated_deepnorm'


def _layer_norm(x, g, b, eps=LN_EPS):
    xf = x.astype(jnp.float32)
    mu = jnp.mean(xf, -1, keepdims=True)
    var = jnp.mean(jnp.square(xf - mu), -1, keepdims=True)
    return ((xf - mu) * lax.rsqrt(var + eps) * g + b).astype(x.dtype)


def _rms_norm(x, g, eps=LN_EPS):
    xf = x.astype(jnp.float32)
    return (xf * lax.rsqrt(jnp.mean(xf * xf, -1, keepdims=True) + eps) * g).astype(x.dtype)


def _t5_bucket(dist):
    max_exact = NUM_BUCKETS // 2
    d = jnp.maximum(dist, 1).astype(jnp.float32)
    large = max_exact + (jnp.log(d / max_exact) / math.log(MAX_DISTANCE / max_exact)
                         * (NUM_BUCKETS - max_exact)).astype(jnp.int32)
    large = jnp.minimum(large, NUM_BUCKETS - 1)
    return jnp.where(dist < max_exact, dist, large)


def _diff_attention(q, k, v, lam, rel_bias):
    B, S = q.shape[0], q.shape[1]
    nblk = S // Q_BLOCK
    scale = DIFF_HEAD_DIM ** -0.5
    qb = q.reshape(B, nblk, Q_BLOCK, DIFF_HEADS, 2, DIFF_HEAD_DIM).transpose(1, 0, 2, 3, 4, 5)
    k_pos = jnp.arange(S, dtype=jnp.int32)

    def block(args):
        i, q_blk = args
        q_pos = i * Q_BLOCK + jnp.arange(Q_BLOCK, dtype=jnp.int32)
        dist = q_pos[:, None] - k_pos[None, :]
        bias = rel_bias[_t5_bucket(jnp.maximum(dist, 0))].astype(jnp.float32)
        logits = jnp.einsum('bqhcd,bkhcd->bhcqk', q_blk, k).astype(jnp.float32) * scale
        logits = logits + bias.transpose(2, 0, 1)[None, :, None]
        logits = jnp.where((dist >= 0)[None, None, None], logits, NEG_BIG)
        p = jax.nn.softmax(logits, axis=-1)
        p = p[:, :, 0] - lam * p[:, :, 1]
        return jnp.einsum('bhqk,bkhe->bqhe', p.astype(v.dtype), v)

    out = lax.map(block, (jnp.arange(nblk, dtype=jnp.int32), qb))
    return out.transpose(1, 0, 2, 3, 4).reshape(B, S, DIFF_HEADS, 2 * DIFF_HEAD_DIM)


def _diff_branch(p_q, p_k, p_v, lq1, lk1, lq2, lk2, subln_g, rel_bias, lambda_init):
    B, S = p_q.shape[0], p_q.shape[1]
    q = p_q.reshape(B, S, DIFF_HEADS, 2, DIFF_HEAD_DIM)
    k = p_k.reshape(B, S, DIFF_HEADS, 2, DIFF_HEAD_DIM)
    v = p_v.reshape(B, S, DIFF_HEADS, 2 * DIFF_HEAD_DIM)
    f32 = jnp.float32
    lam = (jnp.exp(jnp.sum(lq1.astype(f32) * lk1.astype(f32)))
           - jnp.exp(jnp.sum(lq2.astype(f32) * lk2.astype(f32))) + lambda_init)
    o = _diff_attention(q, k, v, lam, rel_bias)
    o = _rms_norm(o, subln_g) * (1.0 - lambda_init)
    return o.reshape(B, S, DIFF_HEADS * 2 * DIFF_HEAD_DIM)


def _rwkv7_scan(r, w, k, v, kk, a):
    B, S, H, N = r.shape

    def step(state, inp):
        r_t, w_t, k_t, v_t, kk_t, a_t = inp
        sa = jnp.einsum('bhij,bhj->bhi', state, -kk_t)
        state = (state * w_t[:, :, None, :]
                 + sa[..., None] * (kk_t * a_t)[:, :, None, :]
                 + v_t[..., None] * k_t[:, :, None, :])
        y = jnp.einsum('bhij,bhj->bhi', state, r_t)
        return state, y

    xs = tuple(t.transpose(1, 0, 2, 3) for t in (r, w, k, v, kk, a))
    state0 = jnp.zeros((B, H, N, N), jnp.float32)
    _, ys = lax.scan(step, state0, xs)
    return ys.transpose(1, 0, 2, 3)


def _rwkv_branch(p, mu, w0, w2, a0, a2, g2, k_k, k_a, r_k, lnx_g, lnx_b):
    B, S = p.shape[0], p.shape[1]
    f32 = jnp.float32
    p_prev = jnp.pad(p[:, :-1], ((0, 0), (1, 0), (0, 0)))
    p = p + (p_prev - p) * mu
    r, k, v, wl, al, gl = jnp.split(p, RWKV_SPLIT, axis=-1)
    w = -jax.nn.softplus(-(w0 + jnp.tanh(wl) @ w2)) - 0.5
    decay = jnp.exp(-jnp.exp(w.astype(f32)))
    a = jax.nn.sigmoid(a0 + al @ a2)
    g = jax.nn.sigmoid(gl) @ g2
    heads = lambda t: t.reshape(B, S, RWKV_HEADS, RWKV_HEAD_DIM)
    kk = heads(k * k_k).astype(f32)
    kk = kk * lax.rsqrt(jnp.maximum(jnp.sum(kk * kk, -1, keepdims=True), 1e-24))
    k = k * (1.0 + (a - 1.0) * k_a)
    r_h, k_h, v_h, a_h = heads(r), heads(k), heads(v), heads(a)
    y = _rwkv7_scan(r_h.astype(f32), heads(decay), k_h.astype(f32), v_h.astype(f32),
                    kk, a_h.astype(f32))
    mean = jnp.mean(y, -1, keepdims=True)
    var = jnp.mean(jnp.square(y - mean), -1, keepdims=True)
    y = ((y - mean) * lax.rsqrt(var + LNX_EPS)).reshape(B, S, RWKV_WIDTH) * lnx_g + lnx_b
    bonus = jnp.sum(r_h.astype(f32) * k_h.astype(f32) * r_k, -1, keepdims=True) * v_h.astype(f32)
    y = y + bonus.reshape(B, S, RWKV_WIDTH)
    return (y * g).astype(p.dtype)


def setup_inputs(seed: int = 0) -> dict:
    key = jax.random.key(seed)
    ks = iter(jax.random.split(key, 48))
    L, D, beta = DEPTH, D_MODEL, DEEPNORM_BETA

    def nrm(shape, scale):
        return scale * jax.random.normal(next(ks), shape, jnp.float32)

    x = nrm((BATCH, SEQ, D), 1.0)
    ln_in_g = 1.0 + nrm((D,), 0.02)
    ln_in_b = nrm((D,), 0.02)
    rel_bias = nrm((NUM_BUCKETS, DIFF_HEADS), 0.3)
    w_in = jnp.concatenate([
        nrm((L, D, DIFF_QK), D ** -0.5),
        nrm((L, D, DIFF_QK), D ** -0.5),
        nrm((L, D, DIFF_V), beta * D ** -0.5),
        nrm((L, D, 2 * RWKV_WIDTH), D ** -0.5),
        nrm((L, D, RWKV_WIDTH), beta * D ** -0.5),
        nrm((L, D, DECAY_RANK + ICLR_RANK + GATE_RANK), D ** -0.5),
        nrm((L, D, 2 * D), D ** -0.5),
    ], axis=-1)
    diff_lam_q1 = nrm((L, DIFF_HEAD_DIM), 0.1)
    diff_lam_k1 = nrm((L, DIFF_HEAD_DIM), 0.1)
    diff_lam_q2 = nrm((L, DIFF_HEAD_DIM), 0.1)
    diff_lam_k2 = nrm((L, DIFF_HEAD_DIM), 0.1)
    diff_subln_g = 1.0 + nrm((L, 2 * DIFF_HEAD_DIM), 0.02)
    rwkv_mu = jax.random.uniform(next(ks), (L, RWKV_SHIFT_WIDTH), jnp.float32)
    rwkv_w0 = jnp.broadcast_to(jnp.linspace(-6.0, -1.0, RWKV_WIDTH, dtype=jnp.float32),
                               (L, RWKV_WIDTH)) + nrm((L, RWKV_WIDTH), 0.1)
    rwkv_w2 = nrm((L, DECAY_RANK, RWKV_WIDTH), 0.5 * DECAY_RANK ** -0.5)
    rwkv_a0 = nrm((L, RWKV_WIDTH), 0.1)
    rwkv_a2 = nrm((L, ICLR_RANK, RWKV_WIDTH), 0.5 * ICLR_RANK ** -0.5)
    rwkv_g2 = nrm((L, GATE_RANK, RWKV_WIDTH), GATE_RANK ** -0.5)
    rwkv_k_k = 0.85 + nrm((L, RWKV_WIDTH), 0.02)
    rwkv_k_a = 1.0 + nrm((L, RWKV_WIDTH), 0.02)
    rwkv_r_k = nrm((L, RWKV_HEADS, RWKV_HEAD_DIM), 0.1)
    rwkv_lnx_g = 1.0 + nrm((L, RWKV_WIDTH), 0.02)
    rwkv_lnx_b = nrm((L, RWKV_WIDTH), 0.02)
    w_up_a = nrm((L, DIFF_V, D), beta * DIFF_V ** -0.5)
    w_up_b = nrm((L, RWKV_WIDTH, D), beta * RWKV_WIDTH ** -0.5)
    w_out = nrm((L, D, D), beta * D ** -0.5)
    ln1_g = 1.0 + nrm((L, D), 0.02)
    ln1_b = nrm((L, D), 0.02)
    ffn_w_gate = nrm((L, D, FFN_HIDDEN), beta * D ** -0.5)
    ffn_w_up = nrm((L, D, FFN_HIDDEN), beta * D ** -0.5)
    ffn_w_down = nrm((L, FFN_HIDDEN, D), beta * FFN_HIDDEN ** -0.5)
    ln2_g = 1.0 + nrm((L, D), 0.02)
    ln2_b = nrm((L, D), 0.02)
    return {'x': x, 'ln_in_g': ln_in_g, 'ln_in_b': ln_in_b, 'rel_bias': rel_bias,
            'w_in': w_in, 'diff_lam_q1': diff_lam_q1, 'diff_lam_k1': diff_lam_k1,
            'diff_lam_q2': diff_lam_q2, 'diff_lam_k2': diff_lam_k2,
            'diff_subln_g': diff_subln_g, 'rwkv_mu': rwkv_mu, 'rwkv_w0': rwkv_w0,
            'rwkv_w2': rwkv_w2, 'rwkv_a0': rwkv_a0, 'rwkv_a2': rwkv_a2, 'rwkv_g2': rwkv_g2,
            'rwkv_k_k': rwkv_k_k, 'rwkv_k_a': rwkv_k_a, 'rwkv_r_k': rwkv_r_k,
            'rwkv_lnx_g': rwkv_lnx_g, 'rwkv_lnx_b': rwkv_lnx_b, 'w_up_a': w_up_a,
            'w_up_b': w_up_b, 'w_out': w_out, 'ln1_g': ln1_g, 'ln1_b': ln1_b,
            'ffn_w_gate': ffn_w_gate, 'ffn_w_up': ffn_w_up, 'ffn_w_down': ffn_w_down,
            'ln2_g': ln2_g, 'ln2_b': ln2_b}


def reference(x, ln_in_g, ln_in_b, rel_bias, w_in, diff_lam_q1, diff_lam_k1, diff_lam_q2,
              diff_lam_k2, diff_subln_g, rwkv_mu, rwkv_w0, rwkv_w2, rwkv_a0, rwkv_a2, rwkv_g2,
              rwkv_k_k, rwkv_k_a, rwkv_r_k, rwkv_lnx_g, rwkv_lnx_b, w_up_a, w_up_b, w_out,
              ln1_g, ln1_b, ffn_w_gate, ffn_w_up, ffn_w_down, ln2_g, ln2_b):
    h = _layer_norm(x, ln_in_g, ln_in_b)
    for l in range(DEPTH):
        lambda_init = 0.8 - 0.6 * math.exp(-0.3 * l)
        proj = h @ w_in[l]
        p_q, p_k, p_v, p_rwkv, p_ga, p_gb = jnp.split(proj, IN_SPLIT, axis=-1)
        y_a = _diff_branch(p_q, p_k, p_v, diff_lam_q1[l], diff_lam_k1[l], diff_lam_q2[l],
                           diff_lam_k2[l], diff_subln_g[l], rel_bias, lambda_init)
        y_b = _rwkv_branch(p_rwkv, rwkv_mu[l], rwkv_w0[l], rwkv_w2[l], rwkv_a0[l],
                           rwkv_a2[l], rwkv_g2[l], rwkv_k_k[l], rwkv_k_a[l], rwkv_r_k[l],
                           rwkv_lnx_g[l], rwkv_lnx_b[l])
        merged = (jax.nn.sigmoid(p_ga) * (y_a @ w_up_a[l])
                  + jax.nn.sigmoid(p_gb) * (y_b @ w_up_b[l]))
        mix = merged @ w_out[l]
        h = _layer_norm(DEEPNORM_ALPHA * h + mix, ln1_g[l], ln1_b[l])
        ffn = (jax.nn.silu(h @ ffn_w_gate[l]) * (h @ ffn_w_up[l])) @ ffn_w_down[l]
        h = _layer_norm(DEEPNORM_ALPHA * h + ffn, ln2_g[l], ln2_b[l])
    return h
```

```python
import math
import numpy as np
import concourse.bass as bass
import concourse.mybir as mybir
from concourse.bass_utils import run_bass_kernel_spmd

F32 = mybir.dt.float32
BF16 = mybir.dt.bfloat16
ALU = mybir.AluOpType
AF = mybir.ActivationFunctionType
AX = mybir.AxisListType

NCORES = 8
D = 1024
SEQ = 2048
BL = 2
TOK = 512
NSLAB = SEQ // TOK
ALPHA = 2.0 ** 0.25
LAMBDA_INIT = 0.2
FFN = 2816
NHC = FFN // 128
C = 64
NCH = TOK // C
MASKVAL = -1.0e9
EXPM05 = math.exp(-0.5)

PB_LNIN, PB_LN1, PB_LN2, PB_SUB, PB_LAM, PB_RB31, PB_N = 0, 2048, 4096, 6144, 6272, 6528, 6532
PC_LNIN_G, PC_LNIN_B, PC_LN1_G, PC_LN1_B, PC_MU, PC_W0, PC_A0, PC_KK, PC_KA, PC_RK, PC_LXG, PC_LXB, PC_N = \
    0, 8, 16, 24, 32, 46, 50, 54, 58, 62, 66, 70, 74


class T:
    __slots__ = ("w", "r", "excl", "tw", "tr")

    def __init__(self, init_reads=None, excl=False, t0=0.0):
        self.w = None
        self.r = dict(init_reads) if init_reads else {}
        self.excl = excl
        self.tw = 0.0
        self.tr = t0


class Buf:
    def __init__(self, t, tr=None):
        self.t = t
        self.T = tr if tr is not None else T()

    def __getitem__(self, idx):
        return self.t[idx]


def _T(x):
    return x.T if isinstance(x, Buf) else x


class Prog:
    ENG = ("pe", "dve", "act", "pool", "sp")

    def __init__(self, nc, n_dma_sems=10):
        self.nc = nc
        self.sems = {}
        self.cnt = {}
        for e in ("pe", "dve", "act", "pool"):
            self.sems[e] = nc.alloc_semaphore("s_" + e)
            self.cnt[e] = 0
        self.known = {e: {} for e in self.ENG}
        self.lists = {e: [] for e in self.ENG}
        self.dsem = {}
        for q in ("sp", "pool", "poolc"):
            lst = []
            for i in range(n_dma_sems):
                k = "d_%s_%d" % (q, i)
                self.sems[k] = nc.alloc_semaphore(k)
                self.cnt[k] = 0
                lst.append(k)
            self.dsem[q] = lst
        self.dnext = {"sp": 0, "pool": 0, "poolc": 0}
        self.efree = {e: 0.0 for e in self.ENG}
        self.step_fin = 0.0

    def now(self):
        return max(self.efree.values())

    def _time(self, e, reads, writes, cost, issue=None):
        ready = 0.0
        for t in reads:
            ready = max(ready, t.tw)
        for t in writes:
            ready = max(ready, t.tw, t.tr)
        start = max(self.efree[e], ready + 0.2)
        if issue is None:
            fin = start + cost
            self.efree[e] = fin
        else:
            self.efree[e] = start + issue
            fin = start + cost
        for t in reads:
            t.tr = max(t.tr, fin)
        for t in writes:
            t.tw = fin
            t.tr = 0.0
        self.step_fin = max(self.step_fin, fin)

    def snapshot(self):
        return {k: v for k, v in self.cnt.items() if v > 0 and not k.startswith("d_poolc")}

    def _deps(self, e, reads, writes, skip_self=False):
        deps = {}

        def add(k, v):
            if deps.get(k, 0) < v:
                deps[k] = v
        for t in reads:
            if t.w is not None:
                add(*t.w)
        for t in writes:
            if t.w is not None:
                add(*t.w)
            for k, v in t.r.items():
                add(k, v)
        waits = []
        kn = self.known[e]
        for k, v in deps.items():
            if skip_self and k == e:
                continue
            if kn.get(k, 0) < v:
                kn[k] = v
                waits.append((k, v))
        return waits

    def _mark(self, ev, reads, writes):
        k, v = ev
        for t in reads:
            if t.r.get(k, 0) < v:
                t.r[k] = v
        for t in writes:
            t.w = ev
            t.r = {}

    def op(self, e, fn, reads=(), writes=(), cost=0.3):
        reads = [_T(x) for x in reads]
        writes = [_T(x) for x in writes]
        writes = writes + [t for t in reads if t.excl]
        reads = [t for t in reads if not t.excl]
        self._time(e, reads, writes, cost)
        waits = self._deps(e, reads, writes, skip_self=(e == "pe"))
        self.cnt[e] += 1
        ev = (e, self.cnt[e])
        self.lists[e].append((waits, fn, (e, 1)))
        self._mark(ev, reads, writes)

    def dma(self, q, fn, reads=(), writes=(), cost=4.0):
        reads = [_T(x) for x in reads]
        writes = [_T(x) for x in writes]
        eng = "pool" if q == "poolc" else q
        self._time(eng, reads, writes, cost, issue=(0.1 if eng == "sp" else 1.0))
        waits = self._deps(eng, reads, writes)
        lst = self.dsem[q]
        k = lst[self.dnext[q] % len(lst)]
        self.dnext[q] += 1
        prev = self.cnt[k]
        if prev > 0 and self.known[eng].get(k, 0) < prev:
            self.known[eng][k] = prev
            waits.append((k, prev))
        self.cnt[k] += 16
        ev = (k, self.cnt[k])
        self.lists[eng].append((waits, fn, (k, 16)))
        self._mark(ev, reads, writes)

    def final_wait(self, e, tiles):
        waits = self._deps(e, [_T(x) for x in tiles], ())
        self.lists[e].append((waits, None, None))

    def emit(self):
        nc = self.nc
        with nc.Block() as block:
            def run(name):
                def body(engh):
                    for waits, fn, inc in self.lists[name]:
                        for k, v in waits:
                            engh.wait_ge(self.sems[k], v)
                        if fn is not None:
                            fn(engh).then_inc(self.sems[inc[0]], inc[1])
                return body
            block.sync(run("sp"))
            block.scalar(run("act"))
            block.vector(run("dve"))
            block.gpsimd(run("pool"))
            block.tensor(run("pe"))


def bc_last(ap, n):
    return bass.AP(tensor=ap.tensor, offset=ap.offset, ap=[list(x) for x in ap.ap] + [[0, n]])


def _merge(ga, gb, na, nb):
    ia = ib = 0
    da = db = False
    while not (da and db):
        if not da and (db or ia * nb <= ib * na):
            try:
                next(ga)
                ia += 1
            except StopIteration:
                da = True
        elif not db:
            try:
                next(gb)
                ib += 1
            except StopIteration:
                db = True
        yield
    return


def _sched(P, gens, stop_when=None, bias=None):
    clocks = [0.0] * len(gens)
    live = list(range(len(gens)))
    while live:
        i = min(live, key=lambda j: (clocks[j] - (bias[j] if bias else 0.0), j))
        P.step_fin = 0.0
        try:
            next(gens[i])
        except StopIteration:
            live.remove(i)
            if stop_when is not None and i == stop_when:
                return
            continue
        if P.step_fin > 0.0:
            clocks[i] = P.step_fin
        yield


def _drain(g):
    n = 0
    for _ in g:
        n += 1
    return n


def _chain(*gens):
    for g_ in gens:
        for _ in g_:
            yield


NRING = 3


def build_program(n_slabs_total=BL * NSLAB, dbg=None):
    nc = bass.Bass("TRN2", target_bir_lowering=False)
    P = Prog(nc)

    def din(name, shape):
        return Buf(nc.dram_tensor(name, list(shape), F32, kind="ExternalInput").ap())

    x_d = din("x", [BL, SEQ, D])
    w_in_d = din("w_in", [D, 5376])
    w_upa_d = din("w_up_a", [512, D])
    w_upb_d = din("w_up_b", [512, D])
    w_out_d = din("w_out", [D, D])
    wg_d = din("wg", [D, FFN])
    wu_d = din("wu", [D, FFN])
    wd_d = din("wd", [FFN, D])
    pbc_d = din("pbc", [128, PB_N])
    pcol_d = din("pcol", [128, PC_N])
    w2a2_d = din("w2a2", [128, 512])
    g2_d = din("g2", [128, 512])
    rbaug_d = din("rbaug", [33, 4])
    ohu_d = din("ohu", [33, 383])
    masks_d = din("masks", [128, 4, 512])
    ident_d = din("ident", [128, 128])
    jmat_d = din("jmat", [128, 128])
    bones_d = din("bones", [128, 128])
    cmask_d = din("cmask", [128, 512])
    y_d = Buf(nc.dram_tensor("y", [BL, SEQ, D], F32, kind="ExternalOutput").ap())
    scr_d = Buf(nc.dram_tensor("scr", [4, 383], F32, kind="Internal").ap())
    dbg_outs = {}

    off = [16384 + 256]
    SB_END = 16384 + 212863 - 256

    def nbytes(shape, dt):
        nb = int(np.prod(shape[1:])) * (2 if dt == BF16 else 4)
        return (nb + 63) // 64 * 64

    def S(name, shape, dt):
        nb = nbytes(shape, dt)
        o = off[0]
        off[0] += nb
        assert o + nb <= SB_END, (name, o, nb)
        return Buf(nc.alloc_sbuf_tensor_at(name, list(shape), dt, offset=o))

    ps = [Buf(nc.alloc_psum_tensor("ps%d" % i, [128, 512], F32), T(excl=True)) for i in range(8)]

    def _n(ap):
        return int(np.prod(ap.shape[1:]))

    def mm(out, lhsT, rhs, start, stop, reads, writes):
        c_ = 0.035 + 0.00036 * _n(rhs)
        if rhs.tensor.dtype == F32:
            c_ *= 4.0
        P.op("pe", lambda e: e.matmul(out=out, lhsT=lhsT, rhs=rhs, start=start, stop=stop), reads, writes, cost=c_)

    def trp(out, in_, ident, reads, writes):
        P.op("pe", lambda e: e.transpose(out=out, in_=in_, identity=ident), reads, writes, cost=0.07)

    def act(out, in_, func, reads, writes, bias=None, scale=None):
        kw = {}
        if bias is not None:
            kw["bias"] = bias
        if scale is not None:
            kw["scale"] = scale
        P.op("act", lambda e: e.activation(out=out, in_=in_, func=func, **kw), reads, writes, cost=0.15 + 0.00105 * _n(out))

    def _vc(eng, out):
        return (0.3 + 0.002 * _n(out)) if eng == "pool" else (0.08 + 0.0012 * _n(out))

    def tt(eng, out, in0, in1, op, reads, writes):
        P.op(eng, lambda e: e.tensor_tensor(out=out, in0=in0, in1=in1, op=op), reads, writes, cost=_vc(eng, out))

    def ts(eng, out, in0, s1, s2, op0, op1, reads, writes):
        if s2 is None:
            P.op(eng, lambda e: e.tensor_scalar(out=out, in0=in0, scalar1=s1, scalar2=None, op0=op0), reads, writes,
                 cost=_vc(eng, out))
        else:
            P.op(eng, lambda e: e.tensor_scalar(out=out, in0=in0, scalar1=s1, scalar2=s2, op0=op0, op1=op1), reads, writes,
                 cost=_vc(eng, out))

    def stt(out, in0, scalar, in1, op0, op1, reads, writes):
        P.op("dve", lambda e: e.scalar_tensor_tensor(out=out, in0=in0, scalar=scalar, in1=in1, op0=op0, op1=op1), reads, writes,
             cost=_vc("dve", out))

    def cp(eng, out, in_, reads, writes):
        if eng == "act":
            P.op("act", lambda e: e.activation(out=out, in_=in_, func=AF.Copy), reads, writes, cost=0.15 + 0.00105 * _n(out))
        else:
            P.op(eng, lambda e: e.tensor_copy(out=out, in_=in_), reads, writes, cost=_vc(eng, out))

    def dma(q, out, in_, reads, writes):
        P.dma(q, lambda e: e.dma_start(out=out, in_=in_), reads, writes)

    def dump(name, buf, shape, dt=F32, reads=None):
        if dbg is None or name not in dbg:
            return
        d = Buf(nc.dram_tensor("dbg_" + name, list(shape), dt, kind="ExternalOutput").ap())
        dbg_outs[name] = d
        dma("sp", d[:], buf[:], reads if reads is not None else [buf], [d])

    pc = S("pc", [128, PC_N], F32)
    csm = S("csm", [128, 388], F32)
    identf = S("identf", [128, 128], F32)
    identb = S("identb", [128, 128], BF16)
    bonesf = S("bonesf", [128, 128], F32)
    jm = S("jm", [128, 128], F32)
    masks = S("masks", [128, 4, 512], BF16)
    cmask = S("cmask", [128, 512], F32)
    mb = S("mb", [128, 8, 128], BF16)
    w2a2b = S("w2a2b", [128, 512], BF16)
    g2b = S("g2b", [128, 512], BF16)
    zb = S("zb", [128, 264], BF16)
    neglam = S("neglam", [128, 1], F32)
    eps5 = S("eps5", [128, 1], F32)
    epsx = S("epsx", [128, 1], F32)
    one1 = S("one1", [128, 1], F32)
    mhalf = S("mhalf", [128, 1], F32)
    ln2x20 = S("ln2x20", [128, 1], F32)
    npc = S("npc", [128, PC_N], F32)
    vaug = S("vaug", [128, 16, 4, 129], BF16)
    kT = S("kT", [128, 4, SEQ], BF16)
    S32 = S("S32", [128, 4, 64], F32)
    Sbf = S("Sbf", [128, 4, 64], BF16)
    S32T = [T() for _ in range(4)]
    SbfT = [T() for _ in range(4)]
    prevcol = S("prevcol", [128, 14], F32)
    ring = [S("ring%d" % i, [128, 4096], BF16) for i in range(NRING)]
    hres_t = S("hres", [128, 4, D], F32)
    hresT = [T() for _ in range(4)]
    hT = S("hT", [128, 8, TOK], BF16)
    yab = S("yab", [128, 8, TOK], BF16)
    stt_ = [S("st%d" % i, [128, 2, 6], F32) for i in range(2)]
    mv_ = [S("mv%d" % i, [128, 2], F32) for i in range(2)]
    rs_ = [S("rs%d" % i, [128, 1], F32) for i in range(2)]
    RBYTES = (SB_END - off[0]) // 256 * 256
    arena = S("arena", [128, RBYTES // 4], F32)

    def RV(o, shape, dt, fence=True):
        nb = nbytes(shape, dt)
        assert o % 64 == 0 and o + nb <= RBYTES, (o, nb, RBYTES)
        a = arena.t[0:shape[0], o // 4:(o + nb) // 4]
        if dt == BF16:
            a = a.bitcast(BF16)
        n = int(np.prod(shape[1:]))
        a = a[:, 0:n]
        if len(shape) == 3:
            a = a.rearrange("p (a b) -> p a b", a=shape[1])
        elif len(shape) == 4:
            a = a.rearrange("p (a b c) -> p a b c", a=shape[1], b=shape[2])
        return Buf(a, T(init_reads=P.snapshot(), t0=P.now()) if fence else T())

    class RegionAlloc:
        def __init__(self, base):
            self.o = base

        def __call__(self, shape, dt):
            b_ = RV(self.o, shape, dt)
            self.o += nbytes(shape, dt)
            return b_

    units = []
    ws = {"load": 0, "use": 0}

    def ws_load_next():
        u = ws["load"]
        if u < len(units):
            units[u](ring[u % NRING])
            ws["load"] += 1

    def ws_get():
        u = ws["use"]
        assert u < ws["load"], "unit not loaded"
        return ring[u % NRING]

    def ws_release():
        ws["use"] += 1
        ws_load_next()

    released = set()

    def ws_release_unit(u):
        released.add(u)
        while ws["use"] in released:
            released.discard(ws["use"])
            ws["use"] += 1
            ws_load_next()

    w_in_r = w_in_d.t.rearrange("(c p) n -> p c n", p=128)
    upa_r = w_upa_d.t.rearrange("(c p) n -> p c n", p=128)
    upb_r = w_upb_d.t.rearrange("(c p) n -> p c n", p=128)
    wout_r = w_out_d.t.rearrange("(c p) n -> p c n", p=128)
    wg_r = wg_d.t.rearrange("(c p) n -> p c n", p=128)
    wu_r = wu_d.t.rearrange("(c p) n -> p c n", p=128)
    wd_r = wd_d.t.rearrange("(c p) n -> p c n", p=128)

    unit_parts = []

    def u_multi(parts):
        unit_parts.append(parts)

    def u_cols(src_r, wbuf, kc, col0, ncols):
        u_multi([(src_r, wbuf, kc, col0, ncols)])

    u_cols(w_in_r, w_in_d, 8, 0, 512)
    u_cols(w_in_r, w_in_d, 8, 512, 512)
    u_cols(w_in_r, w_in_d, 8, 1024, 512)
    u_cols(w_in_r, w_in_d, 8, 3072, 256)
    for c in range(4):
        u_multi([(w_in_r, w_in_d, 8, 1536 + 512 * i + 128 * c, 128) for i in range(3)])
    for f in range(8):
        u_multi([(w_in_r, w_in_d, 8, 3328 + 128 * f, 128),
                 (w_in_r, w_in_d, 8, 4352 + 128 * f, 128),
                 (upa_r, w_upa_d, 4, 128 * f, 128),
                 (upb_r, w_upb_d, 4, 128 * f, 128)])
    for half in range(2):
        u_cols(wout_r, w_out_d, 8, 512 * half, 512)
    for u in range(11):
        u_multi([(wg_r, wg_d, 8, 256 * u, 256), (wu_r, wu_d, 8, 256 * u, 256)])
    for u in range(8):
        u_cols(wd_r, wd_d, 22, 128 * u, 128)
    NU = len(unit_parts)
    wbf = nc.dram_tensor("wbf", [NU, 128, 4096], BF16, kind="Internal").ap()
    wbfT = [T() for _ in range(NU)]
    usize = []
    for u, parts in enumerate(unit_parts):
        usize.append(sum(kc * ncols for (_, _, kc, _, ncols) in parts))

    def emit_conversions():
        for u, parts in enumerate(unit_parts):
            o = 0
            for (src_r, wbuf, kc, col0, ncols) in parts:
                dst = wbf[u, :, o:o + kc * ncols].rearrange("p (c n) -> p c n", c=kc)
                dma("poolc", dst, src_r[:, :, col0:col0 + ncols], [wbuf], [wbfT[u]])
                o += kc * ncols

    def mk_loader(u):
        def loader(slot):
            dma("sp", slot[:, 0:usize[u]], wbf[u, :, 0:usize[u]], [wbfT[u]], [slot])
        return loader

    for _ in range(n_slabs_total):
        for u in range(NU):
            units.append(mk_loader(u))

    dma("sp", pc[:], pcol_d[:, :], [pcol_d], [pc])
    dma("sp", csm[:], pbc_d[:, PB_SUB:PB_N], [pbc_d], [csm])
    dma("sp", identf[:], ident_d[:, :], [ident_d], [identf])
    dma("sp", bonesf[:], bones_d[:, :], [bones_d], [bonesf])
    dma("sp", jm[:], jmat_d[:, :], [jmat_d], [jm])
    dma("sp", cmask[:], cmask_d[:, :], [cmask_d], [cmask])
    dma("pool", masks[:], masks_d[:, :, :], [masks_d], [masks])
    dma("pool", w2a2b[:], w2a2_d[:, :], [w2a2_d], [w2a2b])
    dma("pool", g2b[:], g2_d[:, :], [g2_d], [g2b])
    emit_conversions()
    for _ in range(NRING):
        ws_load_next()
    cp("dve", identb[:], identf[:], [identf], [identb])
    P.op("dve", lambda e: e.memset(eps5[:], 1e-5), [], [eps5])
    P.op("dve", lambda e: e.memset(epsx[:], 64e-5), [], [epsx])
    P.op("dve", lambda e: e.memset(one1[:], 1.0), [], [one1])
    P.op("dve", lambda e: e.memset(mhalf[:], -0.5), [], [mhalf])
    P.op("dve", lambda e: e.memset(ln2x20[:], 20.0 * math.log(2.0)), [], [ln2x20])
    ts("dve", npc[:], pc[:], -1.0, None, ALU.mult, None, [pc], [npc])
    P.op("dve", lambda e: e.memset(zb[:], 0.0), [], [zb])
    P.op("dve", lambda e: e.memset(vaug[:].rearrange("p a b c -> p (a b c)"), 1.0), [], [vaug])
    ts("dve", csm[:, 0:128], csm[:, 0:128], 1.0 - LAMBDA_INIT, None, ALU.mult, None, [csm], [csm])
    RA = RegionAlloc(0)
    lt = RA([128, 2, 64], F32)
    ls = RA([128, 2], F32)
    le = RA([128, 2], F32)
    tt("dve", lt[:, 0, :], csm[:, 128:192], csm[:, 192:256], ALU.mult, [csm], [lt])
    tt("dve", lt[:, 1, :], csm[:, 256:320], csm[:, 320:384], ALU.mult, [csm], [lt])
    P.op("dve", lambda e: e.tensor_reduce(out=ls[:], in_=lt[:], axis=AX.X, op=ALU.add), [lt], [ls])
    act(le[:], ls[:], AF.Exp, [ls], [le])
    tt("dve", neglam[:], le[:, 1:2], le[:, 0:1], ALU.subtract, [le], [neglam])
    ts("dve", neglam[:], neglam[:], -LAMBDA_INIT, None, ALU.add, None, [neglam], [neglam])
    def layer_norm_stats(src_ap, srcT, k):
        st, mv, rs = stt_[k], mv_[k], rs_[k]
        for i in range(2):
            P.op("dve", lambda e, i=i: e.bn_stats(out=st[:, i, :], in_=src_ap[:, i * 512:(i + 1) * 512]), [srcT], [st])
        P.op("dve", lambda e: e.bn_aggr(out=mv[:], in_=st[:].rearrange("p a b -> p (a b)")), [st], [mv])
        act(rs[:], mv[:, 1:2], AF.Ln, [mv, eps5], [rs], bias=eps5[:], scale=1.0)
        act(rs[:], rs[:], AF.Exp, [rs], [rs], scale=-0.5)
        return mv, rs

    def transposes_to_featmajor(src, srcT, dst_ap, dstT, j, gcol, bcol, pbanks):
        for half in range(2):
            pb = pbanks[half]
            for q in range(4):
                kc = half * 4 + q
                trp(pb[:, q * 128:(q + 1) * 128], src[:, kc * 128:(kc + 1) * 128], identf[:], [srcT, identf], [pb])
            for q in range(4):
                kc = half * 4 + q
                act(dst_ap[:, kc, j * 128:(j + 1) * 128], pb[:, q * 128:(q + 1) * 128], AF.Identity,
                    [pb, pc], [dstT], bias=pc[:, bcol + kc:bcol + kc + 1], scale=pc[:, gcol + kc:gcol + kc + 1])

    psb = [ps[i][:].bitcast(BF16) for i in range(8)]
    yaT_ap = yab[:, 0:4, :]
    ybT_ap = yab[:, 4:8, :]
    CREG = 13056
    counts = {}

    for slab_i in range(n_slabs_total):
        b = slab_i // NSLAB
        g = slab_i % NSLAB
        t0 = g * TOK
        dbg_here = (dbg is not None and slab_i == dbg.get("slab", 0))

        RA_ = RegionAlloc(0)
        xt = [RA_([128, D], F32) for _ in range(2)]
        lnbuf = RA_([128, 2048], F32)
        dma("sp", lnbuf[:], pbc_d[:, PB_LNIN:PB_LNIN + 2048], [pbc_d], [lnbuf])
        for j in range(4):
            xb = xt[j % 2]
            dma("sp", xb[:], x_d[b, t0 + j * 128:t0 + (j + 1) * 128, :], [x_d], [xb])
            mv, rs = layer_norm_stats(xb, xb, j % 2)
            ts("dve", xb[:], xb[:], mv[:, 0:1], rs[:], ALU.subtract, ALU.mult, [xb, mv, rs], [xb])
            he = "dve" if slab_i == 0 else "pool"
            tt(he, hres_t[:, j, :], xb[:], lnbuf[:, 0:1024], ALU.mult, [xb, lnbuf], [hresT[j]])
            tt(he, hres_t[:, j, :], hres_t[:, j, :], lnbuf[:, 1024:2048], ALU.add, [hresT[j], lnbuf], [hresT[j]])
            transposes_to_featmajor(xb, xb, hT, hT, j, PC_LNIN_G, PC_LNIN_B, (ps[0], ps[1]))
        if dbg_here:
            dump("hT", hT, [128, 8, TOK], BF16)

        RC = RegionAlloc(0)
        qT = RC([128, 4, TOK], BF16)
        wq = ws_get()
        wq3 = wq[:, 0:4096].rearrange("p (c n) -> p c n", c=8)
        for h in range(4):
            pb = ps[2 + h % 2]
            for kc in range(8):
                mm(pb[:], wq3[:, kc, h * 128:(h + 1) * 128], hT[:, kc, :], kc == 0, kc == 7, [wq, hT], [pb])
            cp("act" if h % 2 else "dve", qT[:, h, :], pb[:], [pb], [qT])
        ws_release()
        wk = ws_get()
        wk3 = wk[:, 0:4096].rearrange("p (c n) -> p c n", c=8)
        for h in range(4):
            pb = ps[2 + h % 2]
            for kc in range(8):
                mm(pb[:], wk3[:, kc, h * 128:(h + 1) * 128], hT[:, kc, :], kc == 0, kc == 7, [wk, hT], [pb])
            cp("act" if h % 2 else "dve", kT[:, h, t0:t0 + TOK], pb[:], [pb], [kT])
        ws_release()
        wv = ws_get()
        wv3 = wv[:, 0:4096].rearrange("p (c n) -> p c n", c=8)
        for j in range(4):
            pb = ps[2 + j % 2]
            for kc in range(8):
                mm(pb[:], hT[:, kc, j * 128:(j + 1) * 128], wv3[:, kc, :], kc == 0, kc == 7, [wv, hT], [pb])
            cp("act" if j % 2 else "dve", vaug[:, 4 * g + j, :, 0:128], pb[:].rearrange("p (a b) -> p a b", a=4), [pb], [vaug])
        ws_release()

        if slab_i == 0:
            RS = RegionAlloc(40 * 1024)
            rb = RS([33, 4], F32)
            oh = RS([33, 383], F32)
            u4 = RS([4, 383], F32)
            hk = RS([128, 8, 128], F32)
            dma("sp", rb[:], rbaug_d[:, :], [rbaug_d], [rb])
            dma("sp", oh[:], ohu_d[:, :], [ohu_d], [oh])
            mm(ps[0][0:4, 0:383], rb[:], oh[:], True, True, [rb, oh], [ps[0]])
            cp("dve", u4[:], ps[0][0:4, 0:383], [ps[0]], [u4])
            dma("sp", scr_d[:, :], u4[:], [u4], [scr_d])
            for h in range(4):
                for blk in range(2):
                    src = bass.AP(tensor=scr_d.t.tensor, offset=383 * h + 128 * blk, ap=[[1, 128], [1, 128]])
                    dma("sp", hk[:, 2 * h + blk, :], src, [scr_d], [hk])
            for i in range(8):
                mm(ps[1 + i // 4][:, (i % 4) * 128:(i % 4 + 1) * 128], jm[:], hk[:, i, :], True, True, [jm, hk], [ps[1 + i // 4]])
            for i in range(2):
                cp("dve", mb[:, 4 * i:4 * i + 4, :], ps[1 + i][:].rearrange("p (a b) -> p a b", a=4), [ps[1 + i]], [mb])

        yaT_T = T(init_reads=P.snapshot(), t0=P.now())
        ybT_T = T(init_reads=P.snapshot(), t0=P.now())

        def gen_C():
            pt = [RC([128, TOK], BF16) for _ in range(3)]
            osb = RC([128, 2, 4, 129], F32)
            rz = RC([128, 2, 4, 1], F32)
            dd = [RC([128, 128], F32) for _ in range(2)]
            ssq = [RC([128, 1], F32) for _ in range(2)]
            sqj = RC([128, 128], F32)
            assert RC.o <= CREG
            nkt = 4 * g + 4
            it = 0
            epi = []
            for h in range(4):
                for c in range(2):
                    lo, hi = 64 * c, 64 * c + 64
                    for ob in (ps[2], ps[3]):
                        mm(ob[:, 0:258], zb[:, 0:128], zb[:, 0:258], True, False, [zb], [ob])
                    def emit_st(kt):
                        j0 = max(0, kt - 4 * g)
                        stb = ps[kt % 2]
                        need_diag = kt >= 4 * g
                        need_off = (kt >= 4 * g and j0 + 1 <= 3) or (kt == 4 * g - 1)
                        mm(stb[:, j0 * 128:512], kT[lo:hi, h, kt * 128:(kt + 1) * 128], qT[lo:hi, h, j0 * 128:512],
                           True, not (need_diag or need_off), [kT, qT], [stb])
                        if need_diag:
                            mm(stb[:, j0 * 128:(j0 + 1) * 128], identb[:], mb[:, 2 * h, :], False, not (j0 + 1 <= 3),
                               [identb, mb], [stb])
                            if j0 + 1 <= 3:
                                mm(stb[:, (j0 + 1) * 128:(j0 + 2) * 128], identb[:], mb[:, 2 * h + 1, :], False, True,
                                   [identb, mb], [stb])
                        elif kt == 4 * g - 1:
                            mm(stb[:, 0:128], identb[:], mb[:, 2 * h + 1, :], False, True, [identb, mb], [stb])

                    emit_st(0)
                    for kt in range(nkt):
                        j0 = max(0, kt - 4 * g)
                        stb = ps[kt % 2]
                        ptb = pt[it % 3]
                        it += 1
                        act(ptb[:, j0 * 128:512], stb[:, j0 * 128:512], AF.Exp, [stb, csm], [ptb],
                            bias=csm[:, 384 + h:385 + h], scale=0.125)
                        if kt + 1 < nkt:
                            emit_st(kt + 1)
                        for j in range(j0, 4):
                            ob = ps[2 + j // 2]
                            oc = (j % 2) * 129
                            last = (j % 2 == 1) and (kt == 4 * g + j)
                            mm(ob[:, oc:oc + 129], ptb[:, j * 128:(j + 1) * 128], vaug[:, kt, h, :], False, last,
                               [ptb, vaug], [ob])
                        if epi:
                            epi.pop(0)(ps[kt % 2])
                        yield
                    for j in range(4):
                        ob = ps[2 + j // 2]
                        oc = (j % 2) * 129
                        cp("dve", osb[:, c, j, :], ob[:, oc:oc + 129], [ob], [osb])
                    yield
                assert not epi

                def mk_chunk(h, j):
                    def chunk(pb):
                        if j == 0:
                            P.op("dve", lambda e: e.reciprocal(out=rz[:], in_=osb[:, :, :, 128:129]), [osb], [rz])
                            ts("dve", rz[:, 1, :, :], rz[:, 1, :, :], neglam[:], None, ALU.mult, None, [rz, neglam], [rz])
                        d_ = dd[j % 2]
                        sq_ = ssq[j % 2]
                        ts("dve", d_[:], osb[:, 0, j, 0:128], rz[:, 0, j, :], None, ALU.mult, None, [osb, rz], [d_])
                        stt(d_[:], osb[:, 1, j, 0:128], rz[:, 1, j, :], d_[:], ALU.mult, ALU.add, [osb, rz, d_], [d_])
                        P.op("act", lambda e: e.activation(out=sqj[:], in_=d_[:], func=AF.Square, accum_out=sq_[:]),
                             [d_], [sqj, sq_])
                        act(sq_[:], sq_[:], AF.Ln, [sq_, eps5], [sq_], bias=eps5[:], scale=1.0 / 128.0)
                        act(sq_[:], sq_[:], AF.Exp, [sq_], [sq_], scale=-0.5)
                        stt(d_[:], d_[:], sq_[:], csm[:, 0:128], ALU.mult, ALU.mult, [d_, sq_, csm], [d_])
                        trp(pb[:, 0:128], d_[:], identf[:], [d_, identf], [pb])
                        cp("act", yaT_ap[:, h, j * 128:(j + 1) * 128], pb[:, 0:128], [pb], [yaT_T])
                    return chunk
                epi.extend(mk_chunk(h, j) for j in range(4))
            while epi:
                epi.pop(0)(ps[len(epi) % 2])
                yield

        RD = RegionAlloc(CREG)
        twb = RD([128, TOK], BF16)
        sgl = RD([128, TOK], BF16)

        def mk_ctx(t):
            X = {}
            X["raw"] = RD([128, 3, 513], F32)
            X["f_"] = [RD([128, TOK], F32) for i in range(6)]
            X["Vb"] = RD([128, TOK], BF16)
            X["Mm"] = RD([128, NCH, 64], BF16)
            X["Nn"] = RD([128, NCH, 64], BF16)
            X["Pm"] = RD([128, NCH, 64], BF16)
            X["Xa1"] = RD([128, NCH, 64], BF16)
            X["XTa1"] = RD([128, NCH, 64], BF16)
            X["gst"] = RD([128, 4, NCH], F32)
            X["rhsb"] = [RD([128, 64], BF16) for _ in range(2)]
            X["ub"] = [RD([128, 64], BF16) for _ in range(2)]
            X["s0w"] = RD([128, 64], F32)
            X["set"] = dict(
                At=RD([128, TOK], BF16), Bt=RD([128, TOK], BF16), Kt=RD([128, TOK], BF16), Rt=RD([128, TOK], BF16),
                Btok=RD([128, NCH, 64], BF16), Ktok=RD([128, NCH, 64], BF16), Vtok=RD([128, NCH, 64], BF16),
                gT=RD([128, TOK], F32), bonT=RD([128, TOK], F32), WC=RD([128, NCH], F32),
                Pf=RD([128, NCH, 64], BF16), AKT=RD([128, NCH, 64], BF16), ARBT=RD([128, NCH, 64], BF16),
                ARKT=RD([128, NCH, 64], BF16), ytok=RD([128, NCH, 64], F32))
            X["DB"] = (ps[4 + 2 * t], ps[5 + 2 * t])
            X["DBb"] = (psb[4 + 2 * t], psb[5 + 2 * t])
            return X
        ctx = [mk_ctx(0), mk_ctx(1)]
        lraw = ctx[0]["raw"]

        def lerp_chunk(pb, dst3, idx, pcidx, dtmp):
            cp("act", dst3[:, idx, 1:513], pb[:], [pb], [dst3])
            if g == 0:
                P.op("dve", lambda e: e.memset(dst3[:, idx, 0:1], 0.0), [], [dst3])
            else:
                cp("dve", dst3[:, idx, 0:1], prevcol[:, pcidx:pcidx + 1], [prevcol], [dst3])
            cp("dve", prevcol[:, pcidx:pcidx + 1], dst3[:, idx, 512:513], [dst3], [prevcol])
            tt("dve", dtmp[:], dst3[:, idx, 0:512], dst3[:, idx, 1:513], ALU.subtract, [dst3], [dtmp])
            stt(dst3[:, idx, 1:513], dtmp[:], pc[:, PC_MU + pcidx:PC_MU + pcidx + 1], dst3[:, idx, 1:513],
                ALU.mult, ALU.add, [dtmp, pc, dst3], [dst3])

        def gen_D_lora():
            wl_ = ws_get()
            wl3 = wl_[:, 0:2048].rearrange("p (c n) -> p c n", c=8)
            for i in range(2):
                pb = ctx[0]["DB"][i]
                for kc in range(8):
                    mm(pb[:], wl3[:, kc, i * 128:(i + 1) * 128], hT[:, kc, :], kc == 0, kc == 7, [wl_, hT], [pb])
                lerp_chunk(pb, lraw, i, 12 + i, ctx[0]["f_"][4])
                yield
            ws_release()
            cp("dve", twb[64:128, :], lraw[64:128, 0, 1:513], [lraw], [twb])
            act(lraw[0:64, 0, 1:513], lraw[0:64, 0, 1:513], AF.Exp, [lraw], [lraw], scale=-2.0)
            act(lraw[0:64, 0, 1:513], lraw[0:64, 0, 1:513], AF.Ln, [lraw, one1], [lraw], bias=one1[0:64, :], scale=1.0)
            act(lraw[0:64, 0, 1:513], lraw[0:64, 0, 1:513], AF.Exp, [lraw], [lraw], scale=-1.0)
            ts("dve", twb[0:64, :], lraw[0:64, 0, 1:513], 2.0, -1.0, ALU.mult, ALU.add, [lraw], [twb])
            act(lraw[:, 1, 1:513], lraw[:, 1, 1:513], AF.Exp, [lraw], [lraw], scale=-1.0)
            act(lraw[:, 1, 1:513], lraw[:, 1, 1:513], AF.Ln, [lraw, one1], [lraw], bias=one1[:], scale=1.0)
            act(sgl[:], lraw[:, 1, 1:513], AF.Exp, [lraw], [sgl], scale=-1.0)
            yield

        fl = lambda bf_: bf_[:].rearrange("p n t -> p (n t)")

        def gen_P1(c, X):
            st_ = X["set"]
            raw, f_, Vb, Mm, Nn, Pm = X["raw"], X["f_"], X["Vb"], X["Mm"], X["Nn"], X["Pm"]
            Xa = [Mm, X["Xa1"]]
            XTa = [Nn, X["XTa1"]]
            DB, DBb = X["DB"], X["DBb"]
            dtmp = f_[4]
            u_exp = slab_i * NU + 4 + c
            assert ws["load"] > u_exp
            At, Bt, Kt, Rt = st_["At"], st_["Bt"], st_["Kt"], st_["Rt"]
            Btok, Ktok, Vtok = st_["Btok"], st_["Ktok"], st_["Vtok"]
            gT, bonT, WC = st_["gT"], st_["bonT"], st_["WC"]
            Pf, AKT, ARBT, ARKT = st_["Pf"], st_["AKT"], st_["ARBT"], st_["ARKT"]
            wr = ring[u_exp % NRING]
            wr4 = wr[:, 0:3072].rearrange("p (i c n) -> p i c n", i=3, c=8)
            for i in range(3):
                pb = DB[i % 2]
                for kc in range(8):
                    mm(pb[:], wr4[:, i, kc, :], hT[:, kc, :], kc == 0, kc == 7, [wr, hT], [pb])
                lerp_chunk(pb, raw, i, 4 * i + c, dtmp)
                yield
            ws_release_unit(u_exp)
            r_ = raw[:, 0, 1:513]
            k_ = raw[:, 1, 1:513]
            v_ = raw[:, 2, 1:513]
            mm(DB[0][:], w2a2b[0:64, c * 128:(c + 1) * 128], twb[0:64, :], True, True, [w2a2b, twb], [DB[0]])
            mm(DB[1][:], w2a2b[64:128, c * 128:(c + 1) * 128], twb[64:128, :], True, True, [w2a2b, twb], [DB[1]])
            sigw, cl, epos, eneg, eprv, a_ = f_
            act(sigw[:], DB[0][:], AF.Exp, [DB[0], npc], [sigw], bias=npc[:, PC_W0 + c:PC_W0 + c + 1], scale=-1.0)
            act(a_[:], DB[1][:], AF.Exp, [DB[1], npc], [a_], bias=npc[:, PC_A0 + c:PC_A0 + c + 1], scale=-1.0)
            act(sigw[:], sigw[:], AF.Ln, [sigw, one1], [sigw], bias=one1[:], scale=1.0)
            act(a_[:], a_[:], AF.Ln, [a_, one1], [a_], bias=one1[:], scale=1.0)
            act(sigw[:], sigw[:], AF.Exp, [sigw, mhalf], [sigw], bias=mhalf[:], scale=-1.0)
            act(a_[:], a_[:], AF.Exp, [a_], [a_], scale=-1.0)
            mm(DB[0][:], g2b[:, c * 128:(c + 1) * 128], sgl[:], True, True, [g2b, sgl], [DB[0]])
            cp("act", gT[:], DB[0][:], [DB[0]], [gT])
            yield
            P.op("dve", lambda e: e.tensor_tensor_scan(out=cl[:], data0=cmask[:], data1=sigw[:], initial=0.0,
                                                      op0=ALU.mult, op1=ALU.add), [cmask, sigw], [cl])
            act(epos[:], cl[:], AF.Exp, [cl], [epos], scale=-1.0)
            act(eneg[:], cl[:], AF.Exp, [cl], [eneg])
            tt("dve", eprv[:], cl[:], sigw[:], ALU.subtract, [cl, sigw], [eprv])
            act(eprv[:], eprv[:], AF.Exp, [eprv], [eprv], scale=-1.0)
            cp("dve", WC[:], epos[:].rearrange("p (n t) -> p n t", t=64)[:, :, 63], [epos], [WC])
            yield
            kkn, tmp = cl, sigw
            ts("dve", kkn[:], k_, pc[:, PC_KK + c:PC_KK + c + 1], None, ALU.mult, None, [raw, pc], [kkn])
            tt("dve", tmp[:], kkn[:], kkn[:], ALU.mult, [kkn], [tmp])
            mm(DB[0][:], bonesf[:], tmp[:], True, True, [bonesf, tmp], [DB[0]])
            ts("dve", tmp[:], DB[0][:], 1e-24, None, ALU.max, None, [DB[0]], [tmp])
            act(tmp[:], tmp[:], AF.Ln, [tmp], [tmp], scale=float(2.0 ** 40))
            act(tmp[:], tmp[:], AF.Exp, [tmp, ln2x20], [tmp], bias=ln2x20[:], scale=-0.5)
            tt("dve", kkn[:], kkn[:], tmp[:], ALU.mult, [kkn, tmp], [kkn])
            yield
            stt(At[:], kkn[:], -1.0, eprv[:], ALU.mult, ALU.mult, [kkn, eprv], [At])
            tt("dve", tmp[:], kkn[:], a_[:], ALU.mult, [kkn, a_], [tmp])
            tt("dve", Bt[:], tmp[:], eneg[:], ALU.mult, [tmp, eneg], [Bt])
            ts("dve", a_[:], a_[:], -1.0, pc[:, PC_KA + c:PC_KA + c + 1], ALU.add, ALU.mult, [a_, pc], [a_])
            stt(a_[:], a_[:], 1.0, k_, ALU.add, ALU.mult, [a_, raw], [a_])
            tt("dve", Kt[:], a_[:], eneg[:], ALU.mult, [a_, eneg], [Kt])
            tt("dve", Rt[:], r_, epos[:], ALU.mult, [raw, epos], [Rt])
            cp("act", Vb[:], v_, [raw], [Vb])
            yield
            stt(tmp[:], r_, pc[:, PC_RK + c:PC_RK + c + 1], a_[:], ALU.mult, ALU.mult, [raw, pc, a_], [tmp])
            mm(DB[1][:], bonesf[:], tmp[:], True, True, [bonesf, tmp], [DB[1]])
            tt("dve", bonT[:], DB[1][:], v_, ALU.mult, [DB[1], raw], [bonT])
            yield
            for ti, (srcb, dstb) in enumerate(((Bt, Btok), (Kt, Ktok), (Vb, Vtok))):
                pbk = DB[ti % 2]
                pv = DBb[ti % 2]
                for n in range(NCH):
                    for hh in range(2):
                        trp(pv[64 * hh:64 * hh + 64, n * 64:(n + 1) * 64], srcb[64 * hh:64 * hh + 64, n * 64:(n + 1) * 64],
                            identb[64 * hh:64 * hh + 64, 64 * hh:64 * hh + 64], [srcb, identb], [pbk])
                cp("act" if ti == 1 else "dve", dstb[:].rearrange("p n t -> p (n t)"), pv[:, 0:512], [pbk], [dstb])
                yield
            prods = ((Bt, At, Mm, 0), (At, Bt, Nn, 1), (Kt, At, AKT, 0), (Bt, Rt, ARBT, 2), (Kt, Rt, ARKT, 2))
            for pi, (la, rb_, dst, mk) in enumerate(prods):
                bank = DB[pi % 2]
                for n in range(NCH):
                    for hh in range(2):
                        sl = slice(64 * hh, 64 * hh + 64)
                        mm(bank[sl, n * 64:(n + 1) * 64], la[sl, n * 64:(n + 1) * 64], rb_[sl, n * 64:(n + 1) * 64],
                           True, True, [la, rb_], [bank])
                tt("dve", fl(dst), bank[:], masks[:, mk, :], ALU.mult, [bank, masks], [dst])
                yield
            tt("dve", fl(Pm), fl(Mm), masks[:, 3, :], ALU.add, [Mm, masks], [Pm])
            Xc, XTc, Pc = Mm, Nn, Pm
            for lev in range(1, 6):
                lastlev = (lev == 5)
                Xn, XTn = Xa[lev % 2], XTa[lev % 2]
                Pn = Pf if lastlev else Pm
                for n in range(NCH):
                    for hh in range(2):
                        sl = slice(64 * hh, 64 * hh + 64)
                        cs = slice(n * 64, (n + 1) * 64)
                        mm(DB[0][sl, cs], Xc[sl, n, :], XTc[sl, n, :], True, True, [Xc, XTc], [DB[0]])
                        if not lastlev:
                            mm(DB[1][sl, cs], XTc[sl, n, :], Xc[sl, n, :], True, True, [Xc, XTc], [DB[1]])
                cp("act", fl(XTn), DB[0][:], [DB[0]], [XTn])
                if not lastlev:
                    cp("dve", fl(Xn), DB[1][:], [DB[1]], [Xn])
                yield
                for n in range(NCH):
                    for hh in range(2):
                        sl = slice(64 * hh, 64 * hh + 64)
                        cs = slice(n * 64, (n + 1) * 64)
                        mm(DB[0][sl, cs], XTn[sl, n, :], Pc[sl, n, :], True, True, [XTn, Pc], [DB[0]])
                tt("dve", fl(Pn), DB[0][:], fl(Pc), ALU.add, [DB[0], Pc], [Pn])
                yield
                Xc, XTc, Pc = Xn, XTn, Pn

        def gen_P2(c, X):
            st_ = X["set"]
            gst, rhsb, ub, s0w = X["gst"], X["rhsb"], X["ub"], X["s0w"]
            ysq = Buf(X["f_"][1][:].rearrange("p (n t) -> p n t", t=64), X["f_"][1].T)
            DB = X["DB"]
            At, Rt = st_["At"], st_["Rt"]
            Btok, Ktok, Vtok = st_["Btok"], st_["Ktok"], st_["Vtok"]
            gT, bonT, WC = st_["gT"], st_["bonT"], st_["WC"]
            Pf, AKT, ARBT, ARKT, ytok = st_["Pf"], st_["AKT"], st_["ARBT"], st_["ARKT"], st_["ytok"]
            if g == 0:
                P.op("dve", lambda e: e.memset(S32[:, c, :], 0.0), [], [S32T[c]])
                P.op("dve", lambda e: e.memset(Sbf[:, c, :], 0.0), [], [SbfT[c]])
            for n in range(NCH):
                cs = slice(n * 64, (n + 1) * 64)
                tb = DB[n % 2]
                pr, pu, pst, py = tb[:, 0:64], tb[:, 64:128], tb[:, 128:192], tb[:, 192:256]
                rb2, ub2 = rhsb[n % 2], ub[n % 2]
                for hh in range(2):
                    sl = slice(64 * hh, 64 * hh + 64)
                    mm(pr[sl, :], At[sl, cs], Sbf[sl, c, :], True, False, [At, SbfT[c]], [tb])
                    mm(pr[sl, :], AKT[sl, n, :], Vtok[sl, n, :], False, True, [AKT, Vtok], [tb])
                cp("act", rb2[:], pr, [tb], [rb2])
                ts("dve", s0w[:], S32[:, c, :], WC[:, n:n + 1], None, ALU.mult, None, [S32T[c], WC], [s0w])
                yield
                for hh in range(2):
                    sl = slice(64 * hh, 64 * hh + 64)
                    mm(pu[sl, :], Pf[sl, n, :], rb2[sl, :], True, True, [Pf, rb2], [tb])
                cp("act", ub2[:], pu, [tb], [ub2])
                yield
                for hh in range(2):
                    sl = slice(64 * hh, 64 * hh + 64)
                    mm(pst[sl, :], Btok[sl, n, :], ub2[sl, :], True, False, [Btok, ub2], [tb])
                    mm(pst[sl, :], Ktok[sl, n, :], Vtok[sl, n, :], False, True, [Ktok, Vtok], [tb])
                for hh in range(2):
                    sl = slice(64 * hh, 64 * hh + 64)
                    mm(py[sl, :], Rt[sl, cs], Sbf[sl, c, :], True, False, [Rt, SbfT[c]], [tb])
                    mm(py[sl, :], ARBT[sl, n, :], ub2[sl, :], False, False, [ARBT, ub2], [tb])
                    mm(py[sl, :], ARKT[sl, n, :], Vtok[sl, n, :], False, True, [ARKT, Vtok], [tb])
                stt(Sbf[:, c, :], pst, WC[:, n:n + 1], s0w[:], ALU.mult, ALU.add, [tb, WC, s0w], [SbfT[c]])
                stt(S32[:, c, :], pst, WC[:, n:n + 1], s0w[:], ALU.mult, ALU.add, [tb, WC, s0w], [S32T[c]])
                cp("act", ytok[:, n, :], py, [tb], [ytok])
                yield
            P.op("dve", lambda e: e.tensor_reduce(out=gst[:, 0, :], in_=ytok[:], axis=AX.X, op=ALU.add), [ytok], [gst])
            tt("dve", ysq[:], ytok[:], ytok[:], ALU.mult, [ytok], [ysq])
            P.op("dve", lambda e: e.tensor_reduce(out=gst[:, 1, :], in_=ysq[:], axis=AX.X, op=ALU.add), [ysq], [gst])
            ts("dve", gst[:, 2, :], gst[:, 0, :], 1.0 / 64.0, None, ALU.mult, None, [gst], [gst])
            tt("dve", gst[:, 0, :], gst[:, 2, :], gst[:, 2, :], ALU.mult, [gst], [gst])
            stt(gst[:, 3, :], gst[:, 1, :], 1.0 / 64.0, gst[:, 0, :], ALU.mult, ALU.subtract, [gst], [gst])
            act(gst[:, 3, :], gst[:, 3, :], AF.Ln, [gst, epsx], [gst], bias=epsx[:], scale=1.0)
            act(gst[:, 3, :], gst[:, 3, :], AF.Exp, [gst], [gst], scale=-0.5)
            yield
            tt("dve", ysq[:], ytok[:], bc_last(gst[:, 2, :], 64), ALU.subtract, [ytok, gst], [ysq])
            tt("dve", ysq[:], ysq[:], bc_last(gst[:, 3, :], 64), ALU.mult, [ysq, gst], [ysq])
            for n in range(NCH):
                for hh in range(2):
                    sl = slice(64 * hh, 64 * hh + 64)
                    mm(DB[0][sl, n * 64:(n + 1) * 64], ysq[sl, n, :], identf[sl, 64 * hh:64 * hh + 64], True, True,
                       [ysq, identf], [DB[0]])
            etmp = ysq[:].rearrange("p n t -> p (n t)")
            act(etmp, DB[0][:], AF.Identity, [DB[0], pc], [ysq], bias=pc[:, PC_LXB + c:PC_LXB + c + 1],
                scale=pc[:, PC_LXG + c:PC_LXG + c + 1])
            tt("dve", etmp, etmp, bonT[:], ALU.add, [ysq, bonT], [ysq])
            tt("dve", ybT_ap[:, c, :], etmp, gT[:], ALU.mult, [ysq, gT], [ybT_T])
            yield

        def d_thread(t):
            for c in (t, t + 2):
                for _ in gen_P1(c, ctx[t]):
                    yield
                for _ in gen_P2(c, ctx[t]):
                    yield

        gC = gen_C()
        _drain(_sched(P, [gC, gen_D_lora()], stop_when=1))
        _drain(_sched(P, [gC, d_thread(0), d_thread(1)], bias=[(40.0 if g >= 2 else (15.0 if g == 1 else 0.0)), 0.0, 0.0]))
        if dbg_here:
            dump("yaT", Buf(yaT_ap, yaT_T), [128, 4, TOK], BF16)
            dump("ybT", Buf(ybT_ap, ybT_T), [128, 4, TOK], BF16)

        RE = RegionAlloc(0)
        mT = RE([128, 8, TOK], BF16)
        sga = [RE([128, TOK], F32) for _ in range(2)]
        sgb = [RE([128, TOK], F32) for _ in range(2)]
        n1 = [RE([128, D], F32) for _ in range(2)]
        lnbuf = RE([128, 2048], F32)
        dma("sp", lnbuf[:], pbc_d[:, PB_LN1:PB_LN1 + 2048], [pbc_d], [lnbuf])
        for f in range(8):
            wf = ws_get()
            ga3 = wf[:, 0:1024].rearrange("p (c n) -> p c n", c=8)
            gb3 = wf[:, 1024:2048].rearrange("p (c n) -> p c n", c=8)
            ua3 = wf[:, 2048:2560].rearrange("p (c n) -> p c n", c=4)
            ub3 = wf[:, 2560:3072].rearrange("p (c n) -> p c n", c=4)
            o4 = 4 * (f % 2)
            pga, pgb, pua, pub = ps[o4], ps[o4 + 1], ps[o4 + 2], ps[o4 + 3]
            wr_ = [pub]
            for kc in range(8):
                mm(pga[:], ga3[:, kc, :], hT[:, kc, :], kc == 0, kc == 7, [wf, hT], [pga])
            for kc in range(8):
                mm(pgb[:], gb3[:, kc, :], hT[:, kc, :], kc == 0, kc == 7, [wf, hT], [pgb])
            for kc in range(4):
                mm(pua[:], ua3[:, kc, :], yaT_ap[:, kc, :], kc == 0, kc == 3, [wf, yaT_T], [pua])
            for kc in range(4):
                mm(pub[:], ub3[:, kc, :], ybT_ap[:, kc, :], kc == 0, kc == 3, [wf, ybT_T], wr_)
            ws_release()
            sa, sb_ = sga[f % 2], sgb[f % 2]
            act(sa[:], pga[:], AF.Sigmoid, [pga], [sa])
            act(sb_[:], pgb[:], AF.Sigmoid, [pgb], [sb_])
            tt("dve", sa[:], sa[:], pua[:], ALU.mult, [sa, pua], [sa])
            tt("dve", sb_[:], sb_[:], pub[:], ALU.mult, [sb_] + wr_, [sb_])
            tt("dve", mT[:, f, :], sa[:], sb_[:], ALU.add, [sa, sb_], [mT])
        if dbg_here:
            dump("mT", mT, [128, 8, TOK], BF16)
        for half in range(2):
            wo = ws_get()
            wo3 = wo[:, 0:4096].rearrange("p (c n) -> p c n", c=8)
            for j in range(4):
                pb = ps[(2 * half + j) % 4]
                for f in range(8):
                    mm(pb[:], mT[:, f, j * 128:(j + 1) * 128], wo3[:, f, :], f == 0, f == 7, [wo, mT], [pb])
                stt(hres_t[:, j, half * 512:(half + 1) * 512], hres_t[:, j, half * 512:(half + 1) * 512], ALPHA, pb[:],
                    ALU.mult, ALU.add, [hresT[j], pb], [hresT[j]])
            ws_release()
        h1T_T = T(init_reads=P.snapshot(), t0=P.now())
        for j in range(4):
            nb = n1[j % 2]
            mv, rs = layer_norm_stats(hres_t[:, j, :], hresT[j], j % 2)
            ts("dve", nb[:], hres_t[:, j, :], mv[:, 0:1], rs[:], ALU.subtract, ALU.mult, [hresT[j], mv, rs], [nb])
            tt("pool", hres_t[:, j, :], nb[:], lnbuf[:, 0:1024], ALU.mult, [nb, lnbuf], [hresT[j]])
            tt("pool", hres_t[:, j, :], hres_t[:, j, :], lnbuf[:, 1024:2048], ALU.add, [hresT[j], lnbuf], [hresT[j]])
            transposes_to_featmajor(nb, nb, yab, h1T_T, j, PC_LN1_G, PC_LN1_B, (ps[4], ps[5]))

        RF = RegionAlloc(0)
        actT = RF([128, NHC, TOK], BF16)
        sil = [RF([128, TOK], F32) for _ in range(2)]
        n2 = [RF([128, D], F32) for _ in range(2)]
        lnbuf = RF([128, 2048], F32)
        dma("sp", lnbuf[:], pbc_d[:, PB_LN2:PB_LN2 + 2048], [pbc_d], [lnbuf])
        for u in range(11):
            wf = ws_get()
            w4 = wf[:, 0:4096].rearrange("p (a c n) -> p a c n", a=2, c=8)
            for q in range(2):
                hc = 2 * u + q
                pg, pu_ = ps[2 * (hc % 3)], ps[2 * (hc % 3) + 1]
                for kc in range(8):
                    mm(pg[:], w4[:, 0, kc, q * 128:(q + 1) * 128], yab[:, kc, :], kc == 0, kc == 7, [wf, h1T_T], [pg])
                for kc in range(8):
                    mm(pu_[:], w4[:, 1, kc, q * 128:(q + 1) * 128], yab[:, kc, :], kc == 0, kc == 7, [wf, h1T_T], [pu_])
                sl_ = sil[hc % 2]
                act(sl_[:], pg[:], AF.Silu, [pg], [sl_])
                tt("dve", actT[:, hc, :], sl_[:], pu_[:], ALU.mult, [sl_, pu_], [actT])
            ws_release()
        for u in range(8):
            wdn = ws_get()
            wd3 = wdn[:, 0:NHC * 128].rearrange("p (c n) -> p c n", c=NHC)
            pb = ps[u % 2]
            for j in range(4):
                for hc in range(NHC):
                    mm(pb[:, j * 128:(j + 1) * 128], actT[:, hc, j * 128:(j + 1) * 128], wd3[:, hc, :], hc == 0, hc == NHC - 1,
                       [wdn, actT], [pb])
            ws_release()
            for j in range(4):
                stt(hres_t[:, j, u * 128:(u + 1) * 128], hres_t[:, j, u * 128:(u + 1) * 128], ALPHA, pb[:, j * 128:(j + 1) * 128],
                    ALU.mult, ALU.add, [hresT[j], pb], [hresT[j]])
        for j in range(4):
            nb = n2[j % 2]
            mv, rs = layer_norm_stats(hres_t[:, j, :], hresT[j], j % 2)
            ts("dve", nb[:], hres_t[:, j, :], mv[:, 0:1], rs[:], ALU.subtract, ALU.mult, [hresT[j], mv, rs], [nb])
            tt("pool", nb[:], nb[:], lnbuf[:, 0:1024], ALU.mult, [nb, lnbuf], [nb])
            tt("pool", nb[:], nb[:], lnbuf[:, 1024:2048], ALU.add, [nb, lnbuf], [nb])
            dma("sp", y_d[b, t0 + j * 128:t0 + (j + 1) * 128, :], nb[:], [nb], [y_d])

    P.final_wait("sp", [y_d] + list(dbg_outs.values()))
    P.emit()
    return nc, dbg_outs


def _t5_bucket_np(dist):
    d = np.maximum(dist, 1).astype(np.float32)
    large = 16 + (np.log(d / np.float32(16)) / np.float32(math.log(128 / 16)) * np.float32(16)).astype(np.int32)
    large = np.minimum(large, 31)
    return np.where(dist < 16, dist, large)


def _host_constants():
    ohu = np.zeros((33, 383), np.float32)
    for i in range(383):
        if i < 127:
            ohu[32, i] = MASKVAL
        else:
            bkt = int(_t5_bucket_np(np.array([i - 127], np.int32))[0])
            ohu[bkt, i] += 8.0
            ohu[31, i] -= 8.0
    p = np.arange(128)[:, None] % 64
    t = np.arange(64)[None, :]
    m = np.stack([(p < t), (t < p), (p <= t), (p == t)], 0).astype(np.float32)
    masks = np.ascontiguousarray(np.broadcast_to(m[:, :, None, :], (4, 128, 8, 64)).transpose(1, 0, 2, 3).reshape(128, 4, 512))
    ident = np.eye(128, dtype=np.float32)
    jmat = np.ascontiguousarray(ident[::-1])
    bones = np.zeros((128, 128), np.float32)
    bones[:64, :64] = 1.0
    bones[64:, 64:] = 1.0
    cm = np.ones((128, 512), np.float32)
    cm[:, ::64] = 0.0
    return dict(ohu=ohu, masks=masks, ident=ident, jmat=jmat, bones=bones, cmask=cm)


def _prep_inputs(inp):
    f = lambda a: np.ascontiguousarray(np.asarray(a, dtype=np.float32))
    row = lambda a: np.asarray(a, np.float32).reshape(-1)
    pbc_row = np.concatenate([row(inp["ln_in_g"]), row(inp["ln_in_b"]), row(inp["ln1_g"]), row(inp["ln1_b"]),
                              row(inp["ln2_g"]), row(inp["ln2_b"]), row(inp["diff_subln_g"]),
                              row(inp["diff_lam_q1"]), row(inp["diff_lam_k1"]), row(inp["diff_lam_q2"]),
                              row(inp["diff_lam_k2"]), row(np.asarray(inp["rel_bias"])[31])])
    assert pbc_row.shape[0] == PB_N
    pbc = np.ascontiguousarray(np.broadcast_to(pbc_row[None, :], (128, PB_N)))
    col = lambda a: row(a).reshape(-1, 128).T
    pcol = np.ascontiguousarray(np.concatenate(
        [col(inp["ln_in_g"]), col(inp["ln_in_b"]), col(inp["ln1_g"]), col(inp["ln1_b"]), col(inp["rwkv_mu"]),
         col(inp["rwkv_w0"]), col(inp["rwkv_a0"]), col(inp["rwkv_k_k"]), col(inp["rwkv_k_a"]), col(inp["rwkv_r_k"]),
         col(inp["rwkv_lnx_g"]), col(inp["rwkv_lnx_b"])], axis=1))
    assert pcol.shape == (128, PC_N)
    w2a2 = np.ascontiguousarray(np.concatenate([np.asarray(inp["rwkv_w2"], np.float32)[0],
                                                np.asarray(inp["rwkv_a2"], np.float32)[0]], 0))
    rbaug = np.ascontiguousarray(np.concatenate([np.asarray(inp["rel_bias"], np.float32), np.ones((1, 4), np.float32)], 0))
    shared = dict(w_in=f(inp["w_in"][0]), w_up_a=f(inp["w_up_a"][0]), w_up_b=f(inp["w_up_b"][0]), w_out=f(inp["w_out"][0]),
                  wg=f(inp["ffn_w_gate"][0]), wu=f(inp["ffn_w_up"][0]), wd=f(inp["ffn_w_down"][0]),
                  pbc=pbc, pcol=pcol, w2a2=w2a2, g2=f(inp["rwkv_g2"][0]), rbaug=rbaug)
    shared.update(_host_constants())
    return shared


_CACHE = {}


def kernel(**inputs):
    x = np.asarray(inputs["x"], np.float32)
    shared = _prep_inputs(inputs)
    if "nc" not in _CACHE:
        _CACHE["nc"] = build_program()[0]
    nc = _CACHE["nc"]
    in_maps = []
    for c in range(NCORES):
        m = dict(shared)
        m["x"] = np.ascontiguousarray(x[BL * c:BL * (c + 1)])
        in_maps.append(m)
    res = run_bass_kernel_spmd(nc, in_maps, core_ids=list(range(NCORES)))
    out = np.concatenate([np.asarray(r["y"], np.float32) for r in res.results], axis=0)
    return out
```

```python
import math
import numpy as np
import concourse.bass as bass
import concourse.mybir as mybir
from concourse.bass_utils import run_bass_kernel_spmd

F32 = mybir.dt.float32
BF16 = mybir.dt.bfloat16
ALU = mybir.AluOpType
AF = mybir.ActivationFunctionType
AX = mybir.AxisListType

NCORES = 8
D = 1024
SEQ = 2048
BL = 2
TOK = 512
NSLAB = SEQ // TOK
ALPHA = 2.0 ** 0.25
LAMBDA_INIT = 0.2
FFN = 2816
NHC = FFN // 128
C = 64
NCH = TOK // C
MASKVAL = -1.0e9
EXPM05 = math.exp(-0.5)

PB_LNIN, PB_LN1, PB_LN2, PB_SUB, PB_LAM, PB_RB31, PB_N = 0, 2048, 4096, 6144, 6272, 6528, 6532
PC_LNIN_G, PC_LNIN_B, PC_LN1_G, PC_LN1_B, PC_MU, PC_W0, PC_A0, PC_KK, PC_KA, PC_RK, PC_LXG, PC_LXB, PC_N = \
    0, 8, 16, 24, 32, 46, 50, 54, 58, 62, 66, 70, 74


class T:
    __slots__ = ("w", "r", "excl", "tw", "tr")

    def __init__(self, init_reads=None, excl=False, t0=0.0):
        self.w = None
        self.r = dict(init_reads) if init_reads else {}
        self.excl = excl
        self.tw = 0.0
        self.tr = t0


class Buf:
    def __init__(self, t, tr=None):
        self.t = t
        self.T = tr if tr is not None else T()

    def __getitem__(self, idx):
        return self.t[idx]


def _T(x):
    return x.T if isinstance(x, Buf) else x


class Prog:
    ENG = ("pe", "dve", "act", "pool", "sp")

    def __init__(self, nc, n_dma_sems=10):
        self.nc = nc
        self.sems = {}
        self.cnt = {}
        for e in ("pe", "dve", "act", "pool"):
            self.sems[e] = nc.alloc_semaphore("s_" + e)
            self.cnt[e] = 0
        self.known = {e: {} for e in self.ENG}
        self.lists = {e: [] for e in self.ENG}
        self.dsem = {}
        for q in ("sp", "pool", "poolc"):
            lst = []
            for i in range(n_dma_sems):
                k = "d_%s_%d" % (q, i)
                self.sems[k] = nc.alloc_semaphore(k)
                self.cnt[k] = 0
                lst.append(k)
            self.dsem[q] = lst
        self.dnext = {"sp": 0, "pool": 0, "poolc": 0}
        self.efree = {e: 0.0 for e in self.ENG}
        self.step_fin = 0.0

    def now(self):
        return max(self.efree.values())

    def _time(self, e, reads, writes, cost, issue=None):
        ready = 0.0
        for t in reads:
            ready = max(ready, t.tw)
        for t in writes:
            ready = max(ready, t.tw, t.tr)
        start = max(self.efree[e], ready + 0.2)
        if issue is None:
            fin = start + cost
            self.efree[e] = fin
        else:
            self.efree[e] = start + issue
            fin = start + cost
        for t in reads:
            t.tr = max(t.tr, fin)
        for t in writes:
            t.tw = fin
            t.tr = 0.0
        self.step_fin = max(self.step_fin, fin)

    def snapshot(self):
        return {k: v for k, v in self.cnt.items() if v > 0 and not k.startswith("d_poolc")}

    def _deps(self, e, reads, writes, skip_self=False):
        deps = {}

        def add(k, v):
            if deps.get(k, 0) < v:
                deps[k] = v
        for t in reads:
            if t.w is not None:
                add(*t.w)
        for t in writes:
            if t.w is not None:
                add(*t.w)
            for k, v in t.r.items():
                add(k, v)
        waits = []
        kn = self.known[e]
        for k, v in deps.items():
            if skip_self and k == e:
                continue
            if kn.get(k, 0) < v:
                kn[k] = v
                waits.append((k, v))
        return waits

    def _mark(self, ev, reads, writes):
        k, v = ev
        for t in reads:
            if t.r.get(k, 0) < v:
                t.r[k] = v
        for t in writes:
            t.w = ev
            t.r = {}

    def op(self, e, fn, reads=(), writes=(), cost=0.3):
        reads = [_T(x) for x in reads]
        writes = [_T(x) for x in writes]
        writes = writes + [t for t in reads if t.excl]
        reads = [t for t in reads if not t.excl]
        self._time(e, reads, writes, cost)
        waits = self._deps(e, reads, writes, skip_self=(e == "pe"))
        self.cnt[e] += 1
        ev = (e, self.cnt[e])
        self.lists[e].append((waits, fn, (e, 1)))
        self._mark(ev, reads, writes)

    def dma(self, q, fn, reads=(), writes=(), cost=4.0):
        reads = [_T(x) for x in reads]
        writes = [_T(x) for x in writes]
        eng = "pool" if q == "poolc" else q
        self._time(eng, reads, writes, cost, issue=(0.1 if eng == "sp" else 1.0))
        waits = self._deps(eng, reads, writes)
        lst = self.dsem[q]
        k = lst[self.dnext[q] % len(lst)]
        self.dnext[q] += 1
        prev = self.cnt[k]
        if prev > 0 and self.known[eng].get(k, 0) < prev:
            self.known[eng][k] = prev
            waits.append((k, prev))
        self.cnt[k] += 16
        ev = (k, self.cnt[k])
        self.lists[eng].append((waits, fn, (k, 16)))
        self._mark(ev, reads, writes)

    def final_wait(self, e, tiles):
        waits = self._deps(e, [_T(x) for x in tiles], ())
        self.lists[e].append((waits, None, None))

    def emit(self):
        nc = self.nc
        with nc.Block() as block:
            def run(name):
                def body(engh):
                    for waits, fn, inc in self.lists[name]:
                        for k, v in waits:
                            engh.wait_ge(self.sems[k], v)
                        if fn is not None:
                            fn(engh).then_inc(self.sems[inc[0]], inc[1])
                return body
            block.sync(run("sp"))
            block.scalar(run("act"))
            block.vector(run("dve"))
            block.gpsimd(run("pool"))
            block.tensor(run("pe"))


def bc_last(ap, n):
    return bass.AP(tensor=ap.tensor, offset=ap.offset, ap=[list(x) for x in ap.ap] + [[0, n]])


def _merge(ga, gb, na, nb):
    ia = ib = 0
    da = db = False
    while not (da and db):
        if not da and (db or ia * nb <= ib * na):
            try:
                next(ga)
                ia += 1
            except StopIteration:
                da = True
        elif not db:
            try:
                next(gb)
                ib += 1
            except StopIteration:
                db = True
        yield
    return


def _sched(P, gens, stop_when=None, bias=None):
    clocks = [0.0] * len(gens)
    live = list(range(len(gens)))
    while live:
        i = min(live, key=lambda j: (clocks[j] - (bias[j] if bias else 0.0), j))
        P.step_fin = 0.0
        try:
            next(gens[i])
        except StopIteration:
            live.remove(i)
            if stop_when is not None and i == stop_when:
                return
            continue
        if P.step_fin > 0.0:
            clocks[i] = P.step_fin
        yield


def _drain(g):
    n = 0
    for _ in g:
        n += 1
    return n


def _chain(*gens):
    for g_ in gens:
        for _ in g_:
            yield


NRING = 3


def build_program(n_slabs_total=BL * NSLAB, dbg=None):
    nc = bass.Bass("TRN2", target_bir_lowering=False)
    P = Prog(nc)

    def din(name, shape):
        return Buf(nc.dram_tensor(name, list(shape), F32, kind="ExternalInput").ap())

    x_d = din("x", [BL, SEQ, D])
    w_in_d = din("w_in", [D, 5376])
    w_upa_d = din("w_up_a", [512, D])
    w_upb_d = din("w_up_b", [512, D])
    w_out_d = din("w_out", [D, D])
    wg_d = din("wg", [D, FFN])
    wu_d = din("wu", [D, FFN])
    wd_d = din("wd", [FFN, D])
    pbc_d = din("pbc", [128, PB_N])
    pcol_d = din("pcol", [128, PC_N])
    w2a2_d = din("w2a2", [128, 512])
    g2_d = din("g2", [128, 512])
    rbaug_d = din("rbaug", [33, 4])
    ohu_d = din("ohu", [33, 383])
    masks_d = din("masks", [128, 4, 512])
    ident_d = din("ident", [128, 128])
    jmat_d = din("jmat", [128, 128])
    bones_d = din("bones", [128, 128])
    cmask_d = din("cmask", [128, 512])
    y_d = Buf(nc.dram_tensor("y", [BL, SEQ, D], F32, kind="ExternalOutput").ap())
    scr_d = Buf(nc.dram_tensor("scr", [4, 383], F32, kind="Internal").ap())
    dbg_outs = {}

    off = [16384 + 256]
    SB_END = 16384 + 212863 - 256

    def nbytes(shape, dt):
        nb = int(np.prod(shape[1:])) * (2 if dt == BF16 else 4)
        return (nb + 63) // 64 * 64

    def S(name, shape, dt):
        nb = nbytes(shape, dt)
        o = off[0]
        off[0] += nb
        assert o + nb <= SB_END, (name, o, nb)
        return Buf(nc.alloc_sbuf_tensor_at(name, list(shape), dt, offset=o))

    ps = [Buf(nc.alloc_psum_tensor("ps%d" % i, [128, 512], F32), T(excl=True)) for i in range(8)]

    def _n(ap):
        return int(np.prod(ap.shape[1:]))

    def mm(out, lhsT, rhs, start, stop, reads, writes):
        c_ = 0.035 + 0.00036 * _n(rhs)
        if rhs.tensor.dtype == F32:
            c_ *= 4.0
        P.op("pe", lambda e: e.matmul(out=out, lhsT=lhsT, rhs=rhs, start=start, stop=stop), reads, writes, cost=c_)

    def trp(out, in_, ident, reads, writes):
        P.op("pe", lambda e: e.transpose(out=out, in_=in_, identity=ident), reads, writes, cost=0.07)

    def act(out, in_, func, reads, writes, bias=None, scale=None):
        kw = {}
        if bias is not None:
            kw["bias"] = bias
        if scale is not None:
            kw["scale"] = scale
        P.op("act", lambda e: e.activation(out=out, in_=in_, func=func, **kw), reads, writes, cost=0.15 + 0.00105 * _n(out))

    def _vc(eng, out):
        return (0.3 + 0.002 * _n(out)) if eng == "pool" else (0.08 + 0.0012 * _n(out))

    def tt(eng, out, in0, in1, op, reads, writes):
        P.op(eng, lambda e: e.tensor_tensor(out=out, in0=in0, in1=in1, op=op), reads, writes, cost=_vc(eng, out))

    def ts(eng, out, in0, s1, s2, op0, op1, reads, writes):
        if s2 is None:
            P.op(eng, lambda e: e.tensor_scalar(out=out, in0=in0, scalar1=s1, scalar2=None, op0=op0), reads, writes,
                 cost=_vc(eng, out))
        else:
            P.op(eng, lambda e: e.tensor_scalar(out=out, in0=in0, scalar1=s1, scalar2=s2, op0=op0, op1=op1), reads, writes,
                 cost=_vc(eng, out))

    def stt(out, in0, scalar, in1, op0, op1, reads, writes):
        P.op("dve", lambda e: e.scalar_tensor_tensor(out=out, in0=in0, scalar=scalar, in1=in1, op0=op0, op1=op1), reads, writes,
             cost=_vc("dve", out))

    def cp(eng, out, in_, reads, writes):
        if eng == "act":
            P.op("act", lambda e: e.activation(out=out, in_=in_, func=AF.Copy), reads, writes, cost=0.15 + 0.00105 * _n(out))
        else:
            P.op(eng, lambda e: e.tensor_copy(out=out, in_=in_), reads, writes, cost=_vc(eng, out))

    def dma(q, out, in_, reads, writes):
        P.dma(q, lambda e: e.dma_start(out=out, in_=in_), reads, writes)

    def dump(name, buf, shape, dt=F32, reads=None):
        if dbg is None or name not in dbg:
            return
        d = Buf(nc.dram_tensor("dbg_" + name, list(shape), dt, kind="ExternalOutput").ap())
        dbg_outs[name] = d
        dma("sp", d[:], buf[:], reads if reads is not None else [buf], [d])

    pc = S("pc", [128, PC_N], F32)
    csm = S("csm", [128, 388], F32)
    identf = S("identf", [128, 128], F32)
    identb = S("identb", [128, 128], BF16)
    bonesf = S("bonesf", [128, 128], F32)
    jm = S("jm", [128, 128], F32)
    masks = S("masks", [128, 4, 512], BF16)
    cmask = S("cmask", [128, 512], F32)
    mb = S("mb", [128, 8, 128], BF16)
    w2a2b = S("w2a2b", [128, 512], BF16)
    g2b = S("g2b", [128, 512], BF16)
    zb = S("zb", [128, 264], BF16)
    neglam = S("neglam", [128, 1], F32)
    eps5 = S("eps5", [128, 1], F32)
    epsx = S("epsx", [128, 1], F32)
    one1 = S("one1", [128, 1], F32)
    mhalf = S("mhalf", [128, 1], F32)
    ln2x20 = S("ln2x20", [128, 1], F32)
    npc = S("npc", [128, PC_N], F32)
    vaug = S("vaug", [128, 16, 4, 129], BF16)
    kT = S("kT", [128, 4, SEQ], BF16)
    S32 = S("S32", [128, 4, 64], F32)
    Sbf = S("Sbf", [128, 4, 64], BF16)
    S32T = [T() for _ in range(4)]
    SbfT = [T() for _ in range(4)]
    prevcol = S("prevcol", [128, 14], F32)
    ring = [S("ring%d" % i, [128, 4096], BF16) for i in range(NRING)]
    hres_t = S("hres", [128, 4, D], F32)
    hresT = [T() for _ in range(4)]
    hT = S("hT", [128, 8, TOK], BF16)
    yab = S("yab", [128, 8, TOK], BF16)
    stt_ = [S("st%d" % i, [128, 2, 6], F32) for i in range(2)]
    mv_ = [S("mv%d" % i, [128, 2], F32) for i in range(2)]
    rs_ = [S("rs%d" % i, [128, 1], F32) for i in range(2)]
    RBYTES = (SB_END - off[0]) // 256 * 256
    arena = S("arena", [128, RBYTES // 4], F32)

    def RV(o, shape, dt, fence=True):
        nb = nbytes(shape, dt)
        assert o % 64 == 0 and o + nb <= RBYTES, (o, nb, RBYTES)
        a = arena.t[0:shape[0], o // 4:(o + nb) // 4]
        if dt == BF16:
            a = a.bitcast(BF16)
        n = int(np.prod(shape[1:]))
        a = a[:, 0:n]
        if len(shape) == 3:
            a = a.rearrange("p (a b) -> p a b", a=shape[1])
        elif len(shape) == 4:
            a = a.rearrange("p (a b c) -> p a b c", a=shape[1], b=shape[2])
        return Buf(a, T(init_reads=P.snapshot(), t0=P.now()) if fence else T())

    class RegionAlloc:
        def __init__(self, base):
            self.o = base

        def __call__(self, shape, dt):
            b_ = RV(self.o, shape, dt)
            self.o += nbytes(shape, dt)
            return b_

    units = []
    ws = {"load": 0, "use": 0}

    def ws_load_next():
        u = ws["load"]
        if u < len(units):
            units[u](ring[u % NRING])
            ws["load"] += 1

    def ws_get():
        u = ws["use"]
        assert u < ws["load"], "unit not loaded"
        return ring[u % NRING]

    def ws_release():
        ws["use"] += 1
        ws_load_next()

    released = set()

    def ws_release_unit(u):
        released.add(u)
        while ws["use"] in released:
            released.discard(ws["use"])
            ws["use"] += 1
            ws_load_next()

    w_in_r = w_in_d.t.rearrange("(c p) n -> p c n", p=128)
    upa_r = w_upa_d.t.rearrange("(c p) n -> p c n", p=128)
    upb_r = w_upb_d.t.rearrange("(c p) n -> p c n", p=128)
    wout_r = w_out_d.t.rearrange("(c p) n -> p c n", p=128)
    wg_r = wg_d.t.rearrange("(c p) n -> p c n", p=128)
    wu_r = wu_d.t.rearrange("(c p) n -> p c n", p=128)
    wd_r = wd_d.t.rearrange("(c p) n -> p c n", p=128)

    unit_parts = []

    def u_multi(parts):
        unit_parts.append(parts)

    def u_cols(src_r, wbuf, kc, col0, ncols):
        u_multi([(src_r, wbuf, kc, col0, ncols)])

    u_cols(w_in_r, w_in_d, 8, 0, 512)
    u_cols(w_in_r, w_in_d, 8, 512, 512)
    u_cols(w_in_r, w_in_d, 8, 1024, 512)
    u_cols(w_in_r, w_in_d, 8, 3072, 256)
    for c in range(4):
        u_multi([(w_in_r, w_in_d, 8, 1536 + 512 * i + 128 * c, 128) for i in range(3)])
    for f in range(8):
        u_multi([(w_in_r, w_in_d, 8, 3328 + 128 * f, 128),
                 (w_in_r, w_in_d, 8, 4352 + 128 * f, 128),
                 (upa_r, w_upa_d, 4, 128 * f, 128),
                 (upb_r, w_upb_d, 4, 128 * f, 128)])
    for half in range(2):
        u_cols(wout_r, w_out_d, 8, 512 * half, 512)
    for u in range(11):
        u_multi([(wg_r, wg_d, 8, 256 * u, 256), (wu_r, wu_d, 8, 256 * u, 256)])
    for u in range(8):
        u_cols(wd_r, wd_d, 22, 128 * u, 128)
    NU = len(unit_parts)
    wbf = nc.dram_tensor("wbf", [NU, 128, 4096], BF16, kind="Internal").ap()
    wbfT = [T() for _ in range(NU)]
    usize = []
    for u, parts in enumerate(unit_parts):
        usize.append(sum(kc * ncols for (_, _, kc, _, ncols) in parts))

    def emit_conversions():
        for u, parts in enumerate(unit_parts):
            o = 0
            for (src_r, wbuf, kc, col0, ncols) in parts:
                dst = wbf[u, :, o:o + kc * ncols].rearrange("p (c n) -> p c n", c=kc)
                dma("poolc", dst, src_r[:, :, col0:col0 + ncols], [wbuf], [wbfT[u]])
                o += kc * ncols

    def mk_loader(u):
        def loader(slot):
            dma("sp", slot[:, 0:usize[u]], wbf[u, :, 0:usize[u]], [wbfT[u]], [slot])
        return loader

    for _ in range(n_slabs_total):
        for u in range(NU):
            units.append(mk_loader(u))

    dma("sp", pc[:], pcol_d[:, :], [pcol_d], [pc])
    dma("sp", csm[:], pbc_d[:, PB_SUB:PB_N], [pbc_d], [csm])
    dma("sp", identf[:], ident_d[:, :], [ident_d], [identf])
    dma("sp", bonesf[:], bones_d[:, :], [bones_d], [bonesf])
    dma("sp", jm[:], jmat_d[:, :], [jmat_d], [jm])
    dma("sp", cmask[:], cmask_d[:, :], [cmask_d], [cmask])
    dma("pool", masks[:], masks_d[:, :, :], [masks_d], [masks])
    dma("pool", w2a2b[:], w2a2_d[:, :], [w2a2_d], [w2a2b])
    dma("pool", g2b[:], g2_d[:, :], [g2_d], [g2b])
    emit_conversions()
    for _ in range(NRING):
        ws_load_next()
    cp("dve", identb[:], identf[:], [identf], [identb])
    P.op("dve", lambda e: e.memset(eps5[:], 1e-5), [], [eps5])
    P.op("dve", lambda e: e.memset(epsx[:], 64e-5), [], [epsx])
    P.op("dve", lambda e: e.memset(one1[:], 1.0), [], [one1])
    P.op("dve", lambda e: e.memset(mhalf[:], -0.5), [], [mhalf])
    P.op("dve", lambda e: e.memset(ln2x20[:], 20.0 * math.log(2.0)), [], [ln2x20])
    ts("dve", npc[:], pc[:], -1.0, None, ALU.mult, None, [pc], [npc])
    P.op("dve", lambda e: e.memset(zb[:], 0.0), [], [zb])
    P.op("dve", lambda e: e.memset(vaug[:].rearrange("p a b c -> p (a b c)"), 1.0), [], [vaug])
    ts("dve", csm[:, 0:128], csm[:, 0:128], 1.0 - LAMBDA_INIT, None, ALU.mult, None, [csm], [csm])
    RA = RegionAlloc(0)
    lt = RA([128, 2, 64], F32)
    ls = RA([128, 2], F32)
    le = RA([128, 2], F32)
    tt("dve", lt[:, 0, :], csm[:, 128:192], csm[:, 192:256], ALU.mult, [csm], [lt])
    tt("dve", lt[:, 1, :], csm[:, 256:320], csm[:, 320:384], ALU.mult, [csm], [lt])
    P.op("dve", lambda e: e.tensor_reduce(out=ls[:], in_=lt[:], axis=AX.X, op=ALU.add), [lt], [ls])
    act(le[:], ls[:], AF.Exp, [ls], [le])
    tt("dve", neglam[:], le[:, 1:2], le[:, 0:1], ALU.subtract, [le], [neglam])
    ts("dve", neglam[:], neglam[:], -LAMBDA_INIT, None, ALU.add, None, [neglam], [neglam])
    def layer_norm_stats(src_ap, srcT, k):
        st, mv, rs = stt_[k], mv_[k], rs_[k]
        for i in range(2):
            P.op("dve", lambda e, i=i: e.bn_stats(out=st[:, i, :], in_=src_ap[:, i * 512:(i + 1) * 512]), [srcT], [st])
        P.op("dve", lambda e: e.bn_aggr(out=mv[:], in_=st[:].rearrange("p a b -> p (a b)")), [st], [mv])
        act(rs[:], mv[:, 1:2], AF.Ln, [mv, eps5], [rs], bias=eps5[:], scale=1.0)
        act(rs[:], rs[:], AF.Exp, [rs], [rs], scale=-0.5)
        return mv, rs

    def transposes_to_featmajor(src, srcT, dst_ap, dstT, j, gcol, bcol, pbanks):
        for half in range(2):
            pb = pbanks[half]
            for q in range(4):
                kc = half * 4 + q
                trp(pb[:, q * 128:(q + 1) * 128], src[:, kc * 128:(kc + 1) * 128], identf[:], [srcT, identf], [pb])
            for q in range(4):
                kc = half * 4 + q
                act(dst_ap[:, kc, j * 128:(j + 1) * 128], pb[:, q * 128:(q + 1) * 128], AF.Identity,
                    [pb, pc], [dstT], bias=pc[:, bcol + kc:bcol + kc + 1], scale=pc[:, gcol + kc:gcol + kc + 1])

    psb = [ps[i][:].bitcast(BF16) for i in range(8)]
    yaT_ap = yab[:, 0:4, :]
    ybT_ap = yab[:, 4:8, :]
    CREG = 13056
    counts = {}

    for slab_i in range(n_slabs_total):
        b = slab_i // NSLAB
        g = slab_i % NSLAB
        t0 = g * TOK
        dbg_here = (dbg is not None and slab_i == dbg.get("slab", 0))

        RA_ = RegionAlloc(0)
        xt = [RA_([128, D], F32) for _ in range(2)]
        lnbuf = RA_([128, 2048], F32)
        dma("sp", lnbuf[:], pbc_d[:, PB_LNIN:PB_LNIN + 2048], [pbc_d], [lnbuf])
        for j in range(4):
            xb = xt[j % 2]
            dma("sp", xb[:], x_d[b, t0 + j * 128:t0 + (j + 1) * 128, :], [x_d], [xb])
            mv, rs = layer_norm_stats(xb, xb, j % 2)
            ts("dve", xb[:], xb[:], mv[:, 0:1], rs[:], ALU.subtract, ALU.mult, [xb, mv, rs], [xb])
            he = "dve" if slab_i == 0 else "pool"
            tt(he, hres_t[:, j, :], xb[:], lnbuf[:, 0:1024], ALU.mult, [xb, lnbuf], [hresT[j]])
            tt(he, hres_t[:, j, :], hres_t[:, j, :], lnbuf[:, 1024:2048], ALU.add, [hresT[j], lnbuf], [hresT[j]])
            transposes_to_featmajor(xb, xb, hT, hT, j, PC_LNIN_G, PC_LNIN_B, (ps[0], ps[1]))
        if dbg_here:
            dump("hT", hT, [128, 8, TOK], BF16)

        RC = RegionAlloc(0)
        qT = RC([128, 4, TOK], BF16)
        wq = ws_get()
        wq3 = wq[:, 0:4096].rearrange("p (c n) -> p c n", c=8)
        for h in range(4):
            pb = ps[2 + h % 2]
            for kc in range(8):
                mm(pb[:], wq3[:, kc, h * 128:(h + 1) * 128], hT[:, kc, :], kc == 0, kc == 7, [wq, hT], [pb])
            cp("act" if h % 2 else "dve", qT[:, h, :], pb[:], [pb], [qT])
        ws_release()
        wk = ws_get()
        wk3 = wk[:, 0:4096].rearrange("p (c n) -> p c n", c=8)
        for h in range(4):
            pb = ps[2 + h % 2]
            for kc in range(8):
                mm(pb[:], wk3[:, kc, h * 128:(h + 1) * 128], hT[:, kc, :], kc == 0, kc == 7, [wk, hT], [pb])
            cp("act" if h % 2 else "dve", kT[:, h, t0:t0 + TOK], pb[:], [pb], [kT])
        ws_release()
        wv = ws_get()
        wv3 = wv[:, 0:4096].rearrange("p (c n) -> p c n", c=8)
        for j in range(4):
            pb = ps[2 + j % 2]
            for kc in range(8):
                mm(pb[:], hT[:, kc, j * 128:(j + 1) * 128], wv3[:, kc, :], kc == 0, kc == 7, [wv, hT], [pb])
            cp("act" if j % 2 else "dve", vaug[:, 4 * g + j, :, 0:128], pb[:].rearrange("p (a b) -> p a b", a=4), [pb], [vaug])
        ws_release()

        if slab_i == 0:
            RS = RegionAlloc(40 * 1024)
            rb = RS([33, 4], F32)
            oh = RS([33, 383], F32)
            u4 = RS([4, 383], F32)
            hk = RS([128, 8, 128], F32)
            dma("sp", rb[:], rbaug_d[:, :], [rbaug_d], [rb])
            dma("sp", oh[:], ohu_d[:, :], [ohu_d], [oh])
            mm(ps[0][0:4, 0:383], rb[:], oh[:], True, True, [rb, oh], [ps[0]])
            cp("dve", u4[:], ps[0][0:4, 0:383], [ps[0]], [u4])
            dma("sp", scr_d[:, :], u4[:], [u4], [scr_d])
            for h in range(4):
                for blk in range(2):
                    src = bass.AP(tensor=scr_d.t.tensor, offset=383 * h + 128 * blk, ap=[[1, 128], [1, 128]])
                    dma("sp", hk[:, 2 * h + blk, :], src, [scr_d], [hk])
            for i in range(8):
                mm(ps[1 + i // 4][:, (i % 4) * 128:(i % 4 + 1) * 128], jm[:], hk[:, i, :], True, True, [jm, hk], [ps[1 + i // 4]])
            for i in range(2):
                cp("dve", mb[:, 4 * i:4 * i + 4, :], ps[1 + i][:].rearrange("p (a b) -> p a b", a=4), [ps[1 + i]], [mb])

        yaT_T = T(init_reads=P.snapshot(), t0=P.now())
        ybT_T = T(init_reads=P.snapshot(), t0=P.now())

        def gen_C():
            pt = [RC([128, TOK], BF16) for _ in range(3)]
            osb = RC([128, 2, 4, 129], F32)
            rz = RC([128, 2, 4, 1], F32)
            dd = [RC([128, 128], F32) for _ in range(2)]
            ssq = [RC([128, 1], F32) for _ in range(2)]
            sqj = RC([128, 128], F32)
            assert RC.o <= CREG
            nkt = 4 * g + 4
            it = 0
            epi = []
            for h in range(4):
                for c in range(2):
                    lo, hi = 64 * c, 64 * c + 64
                    for ob in (ps[2], ps[3]):
                        mm(ob[:, 0:258], zb[:, 0:128], zb[:, 0:258], True, False, [zb], [ob])
                    def emit_st(kt):
                        j0 = max(0, kt - 4 * g)
                        stb = ps[kt % 2]
                        need_diag = kt >= 4 * g
                        need_off = (kt >= 4 * g and j0 + 1 <= 3) or (kt == 4 * g - 1)
                        mm(stb[:, j0 * 128:512], kT[lo:hi, h, kt * 128:(kt + 1) * 128], qT[lo:hi, h, j0 * 128:512],
                           True, not (need_diag or need_off), [kT, qT], [stb])
                        if need_diag:
                            mm(stb[:, j0 * 128:(j0 + 1) * 128], identb[:], mb[:, 2 * h, :], False, not (j0 + 1 <= 3),
                               [identb, mb], [stb])
                            if j0 + 1 <= 3:
                                mm(stb[:, (j0 + 1) * 128:(j0 + 2) * 128], identb[:], mb[:, 2 * h + 1, :], False, True,
                                   [identb, mb], [stb])
                        elif kt == 4 * g - 1:
                            mm(stb[:, 0:128], identb[:], mb[:, 2 * h + 1, :], False, True, [identb, mb], [stb])

                    emit_st(0)
                    for kt in range(nkt):
                        j0 = max(0, kt - 4 * g)
                        stb = ps[kt % 2]
                        ptb = pt[it % 3]
                        it += 1
                        act(ptb[:, j0 * 128:512], stb[:, j0 * 128:512], AF.Exp, [stb, csm], [ptb],
                            bias=csm[:, 384 + h:385 + h], scale=0.125)
                        if kt + 1 < nkt:
                            emit_st(kt + 1)
                        for j in range(j0, 4):
                            ob = ps[2 + j // 2]
                            oc = (j % 2) * 129
                            last = (j % 2 == 1) and (kt == 4 * g + j)
                            mm(ob[:, oc:oc + 129], ptb[:, j * 128:(j + 1) * 128], vaug[:, kt, h, :], False, last,
                               [ptb, vaug], [ob])
                        if epi:
                            epi.pop(0)(ps[kt % 2])
                        yield
                    for j in range(4):
                        ob = ps[2 + j // 2]
                        oc = (j % 2) * 129
                        cp("dve", osb[:, c, j, :], ob[:, oc:oc + 129], [ob], [osb])
                    yield
                assert not epi

                def mk_chunk(h, j):
                    def chunk(pb):
                        if j == 0:
                            P.op("dve", lambda e: e.reciprocal(out=rz[:], in_=osb[:, :, :, 128:129]), [osb], [rz])
                            ts("dve", rz[:, 1, :, :], rz[:, 1, :, :], neglam[:], None, ALU.mult, None, [rz, neglam], [rz])
                        d_ = dd[j % 2]
                        sq_ = ssq[j % 2]
                        ts("dve", d_[:], osb[:, 0, j, 0:128], rz[:, 0, j, :], None, ALU.mult, None, [osb, rz], [d_])
                        stt(d_[:], osb[:, 1, j, 0:128], rz[:, 1, j, :], d_[:], ALU.mult, ALU.add, [osb, rz, d_], [d_])
                        P.op("act", lambda e: e.activation(out=sqj[:], in_=d_[:], func=AF.Square, accum_out=sq_[:]),
                             [d_], [sqj, sq_])
                        act(sq_[:], sq_[:], AF.Ln, [sq_, eps5], [sq_], bias=eps5[:], scale=1.0 / 128.0)
                        act(sq_[:], sq_[:], AF.Exp, [sq_], [sq_], scale=-0.5)
                        stt(d_[:], d_[:], sq_[:], csm[:, 0:128], ALU.mult, ALU.mult, [d_, sq_, csm], [d_])
                        trp(pb[:, 0:128], d_[:], identf[:], [d_, identf], [pb])
                        cp("act", yaT_ap[:, h, j * 128:(j + 1) * 128], pb[:, 0:128], [pb], [yaT_T])
                    return chunk
                epi.extend(mk_chunk(h, j) for j in range(4))
            while epi:
                epi.pop(0)(ps[len(epi) % 2])
                yield

        RD = RegionAlloc(CREG)
        twb = RD([128, TOK], BF16)
        sgl = RD([128, TOK], BF16)

        def mk_ctx(t):
            X = {}
            X["raw"] = RD([128, 3, 513], F32)
            X["f_"] = [RD([128, TOK], F32) for i in range(6)]
            X["Vb"] = RD([128, TOK], BF16)
            X["Mm"] = RD([128, NCH, 64], BF16)
            X["Nn"] = RD([128, NCH, 64], BF16)
            X["Pm"] = RD([128, NCH, 64], BF16)
            X["Xa1"] = RD([128, NCH, 64], BF16)
            X["XTa1"] = RD([128, NCH, 64], BF16)
            X["gst"] = RD([128, 4, NCH], F32)
            X["rhsb"] = [RD([128, 64], BF16) for _ in range(2)]
            X["ub"] = [RD([128, 64], BF16) for _ in range(2)]
            X["s0w"] = RD([128, 64], F32)
            X["set"] = dict(
                At=RD([128, TOK], BF16), Bt=RD([128, TOK], BF16), Kt=RD([128, TOK], BF16), Rt=RD([128, TOK], BF16),
                Btok=RD([128, NCH, 64], BF16), Ktok=RD([128, NCH, 64], BF16), Vtok=RD([128, NCH, 64], BF16),
                gT=RD([128, TOK], F32), bonT=RD([128, TOK], F32), WC=RD([128, NCH], F32),
                Pf=RD([128, NCH, 64], BF16), AKT=RD([128, NCH, 64], BF16), ARBT=RD([128, NCH, 64], BF16),
                ARKT=RD([128, NCH, 64], BF16), ytok=RD([128, NCH, 64], F32))
            X["DB"] = (ps[4 + 2 * t], ps[5 + 2 * t])
            X["DBb"] = (psb[4 + 2 * t], psb[5 + 2 * t])
            return X
        ctx = [mk_ctx(0), mk_ctx(1)]
        lraw = ctx[0]["raw"]

        def lerp_chunk(pb, dst3, idx, pcidx, dtmp):
            cp("act", dst3[:, idx, 1:513], pb[:], [pb], [dst3])
            if g == 0:
                P.op("dve", lambda e: e.memset(dst3[:, idx, 0:1], 0.0), [], [dst3])
            else:
                cp("dve", dst3[:, idx, 0:1], prevcol[:, pcidx:pcidx + 1], [prevcol], [dst3])
            cp("dve", prevcol[:, pcidx:pcidx + 1], dst3[:, idx, 512:513], [dst3], [prevcol])
            tt("dve", dtmp[:], dst3[:, idx, 0:512], dst3[:, idx, 1:513], ALU.subtract, [dst3], [dtmp])
            stt(dst3[:, idx, 1:513], dtmp[:], pc[:, PC_MU + pcidx:PC_MU + pcidx + 1], dst3[:, idx, 1:513],
                ALU.mult, ALU.add, [dtmp, pc, dst3], [dst3])

        def gen_D_lora():
            wl_ = ws_get()
            wl3 = wl_[:, 0:2048].rearrange("p (c n) -> p c n", c=8)
            for i in range(2):
                pb = ctx[0]["DB"][i]
                for kc in range(8):
                    mm(pb[:], wl3[:, kc, i * 128:(i + 1) * 128], hT[:, kc, :], kc == 0, kc == 7, [wl_, hT], [pb])
                lerp_chunk(pb, lraw, i, 12 + i, ctx[0]["f_"][4])
                yield
            ws_release()
            cp("dve", twb[64:128, :], lraw[64:128, 0, 1:513], [lraw], [twb])
            act(lraw[0:64, 0, 1:513], lraw[0:64, 0, 1:513], AF.Exp, [lraw], [lraw], scale=-2.0)
            act(lraw[0:64, 0, 1:513], lraw[0:64, 0, 1:513], AF.Ln, [lraw, one1], [lraw], bias=one1[0:64, :], scale=1.0)
            act(lraw[0:64, 0, 1:513], lraw[0:64, 0, 1:513], AF.Exp, [lraw], [lraw], scale=-1.0)
            ts("dve", twb[0:64, :], lraw[0:64, 0, 1:513], 2.0, -1.0, ALU.mult, ALU.add, [lraw], [twb])
            act(lraw[:, 1, 1:513], lraw[:, 1, 1:513], AF.Exp, [lraw], [lraw], scale=-1.0)
            act(lraw[:, 1, 1:513], lraw[:, 1, 1:513], AF.Ln, [lraw, one1], [lraw], bias=one1[:], scale=1.0)
            act(sgl[:], lraw[:, 1, 1:513], AF.Exp, [lraw], [sgl], scale=-1.0)
            yield

        fl = lambda bf_: bf_[:].rearrange("p n t -> p (n t)")

        def gen_P1(c, X):
            st_ = X["set"]
            raw, f_, Vb, Mm, Nn, Pm = X["raw"], X["f_"], X["Vb"], X["Mm"], X["Nn"], X["Pm"]
            Xa = [Mm, X["Xa1"]]
            XTa = [Nn, X["XTa1"]]
            DB, DBb = X["DB"], X["DBb"]
            dtmp = f_[4]
            u_exp = slab_i * NU + 4 + c
            assert ws["load"] > u_exp
            At, Bt, Kt, Rt = st_["At"], st_["Bt"], st_["Kt"], st_["Rt"]
            Btok, Ktok, Vtok = st_["Btok"], st_["Ktok"], st_["Vtok"]
            gT, bonT, WC = st_["gT"], st_["bonT"], st_["WC"]
            Pf, AKT, ARBT, ARKT = st_["Pf"], st_["AKT"], st_["ARBT"], st_["ARKT"]
            wr = ring[u_exp % NRING]
            wr4 = wr[:, 0:3072].rearrange("p (i c n) -> p i c n", i=3, c=8)
            for i in range(3):
                pb = DB[i % 2]
                for kc in range(8):
                    mm(pb[:], wr4[:, i, kc, :], hT[:, kc, :], kc == 0, kc == 7, [wr, hT], [pb])
                lerp_chunk(pb, raw, i, 4 * i + c, dtmp)
                yield
            ws_release_unit(u_exp)
            r_ = raw[:, 0, 1:513]
            k_ = raw[:, 1, 1:513]
            v_ = raw[:, 2, 1:513]
            mm(DB[0][:], w2a2b[0:64, c * 128:(c + 1) * 128], twb[0:64, :], True, True, [w2a2b, twb], [DB[0]])
            mm(DB[1][:], w2a2b[64:128, c * 128:(c + 1) * 128], twb[64:128, :], True, True, [w2a2b, twb], [DB[1]])
            sigw, cl, epos, eneg, eprv, a_ = f_
            act(sigw[:], DB[0][:], AF.Exp, [DB[0], npc], [sigw], bias=npc[:, PC_W0 + c:PC_W0 + c + 1], scale=-1.0)
            act(a_[:], DB[1][:], AF.Exp, [DB[1], npc], [a_], bias=npc[:, PC_A0 + c:PC_A0 + c + 1], scale=-1.0)
            act(sigw[:], sigw[:], AF.Ln, [sigw, one1], [sigw], bias=one1[:], scale=1.0)
            act(a_[:], a_[:], AF.Ln, [a_, one1], [a_], bias=one1[:], scale=1.0)
            act(sigw[:], sigw[:], AF.Exp, [sigw, mhalf], [sigw], bias=mhalf[:], scale=-1.0)
            act(a_[:], a_[:], AF.Exp, [a_], [a_], scale=-1.0)
            mm(DB[0][:], g2b[:, c * 128:(c + 1) * 128], sgl[:], True, True, [g2b, sgl], [DB[0]])
            cp("act", gT[:], DB[0][:], [DB[0]], [gT])
            yield
            P.op("dve", lambda e: e.tensor_tensor_scan(out=cl[:], data0=cmask[:], data1=sigw[:], initial=0.0,
                                                      op0=ALU.mult, op1=ALU.add), [cmask, sigw], [cl])
            act(epos[:], cl[:], AF.Exp, [cl], [epos], scale=-1.0)
            act(eneg[:], cl[:], AF.Exp, [cl], [eneg])
            tt("dve", eprv[:], cl[:], sigw[:], ALU.subtract, [cl, sigw], [eprv])
            act(eprv[:], eprv[:], AF.Exp, [eprv], [eprv], scale=-1.0)
            cp("dve", WC[:], epos[:].rearrange("p (n t) -> p n t", t=64)[:, :, 63], [epos], [WC])
            yield
            kkn, tmp = cl, sigw
            ts("dve", kkn[:], k_, pc[:, PC_KK + c:PC_KK + c + 1], None, ALU.mult, None, [raw, pc], [kkn])
            tt("dve", tmp[:], kkn[:], kkn[:], ALU.mult, [kkn], [tmp])
            mm(DB[0][:], bonesf[:], tmp[:], True, True, [bonesf, tmp], [DB[0]])
            ts("dve", tmp[:], DB[0][:], 1e-24, None, ALU.max, None, [DB[0]], [tmp])
            act(tmp[:], tmp[:], AF.Ln, [tmp], [tmp], scale=float(2.0 ** 40))
            act(tmp[:], tmp[:], AF.Exp, [tmp, ln2x20], [tmp], bias=ln2x20[:], scale=-0.5)
            tt("dve", kkn[:], kkn[:], tmp[:], ALU.mult, [kkn, tmp], [kkn])
            yield
            stt(At[:], kkn[:], -1.0, eprv[:], ALU.mult, ALU.mult, [kkn, eprv], [At])
            tt("dve", tmp[:], kkn[:], a_[:], ALU.mult, [kkn, a_], [tmp])
            tt("dve", Bt[:], tmp[:], eneg[:], ALU.mult, [tmp, eneg], [Bt])
            ts("dve", a_[:], a_[:], -1.0, pc[:, PC_KA + c:PC_KA + c + 1], ALU.add, ALU.mult, [a_, pc], [a_])
            stt(a_[:], a_[:], 1.0, k_, ALU.add, ALU.mult, [a_, raw], [a_])
            tt("dve", Kt[:], a_[:], eneg[:], ALU.mult, [a_, eneg], [Kt])
            tt("dve", Rt[:], r_, epos[:], ALU.mult, [raw, epos], [Rt])
            cp("act", Vb[:], v_, [raw], [Vb])
            yield
            stt(tmp[:], r_, pc[:, PC_RK + c:PC_RK + c + 1], a_[:], ALU.mult, ALU.mult, [raw, pc, a_], [tmp])
            mm(DB[1][:], bonesf[:], tmp[:], True, True, [bonesf, tmp], [DB[1]])
            tt("dve", bonT[:], DB[1][:], v_, ALU.mult, [DB[1], raw], [bonT])
            yield
            for ti, (srcb, dstb) in enumerate(((Bt, Btok), (Kt, Ktok), (Vb, Vtok))):
                pbk = DB[ti % 2]
                pv = DBb[ti % 2]
                for n in range(NCH):
                    for hh in range(2):
                        trp(pv[64 * hh:64 * hh + 64, n * 64:(n + 1) * 64], srcb[64 * hh:64 * hh + 64, n * 64:(n + 1) * 64],
                            identb[64 * hh:64 * hh + 64, 64 * hh:64 * hh + 64], [srcb, identb], [pbk])
                cp("act" if ti == 1 else "dve", dstb[:].rearrange("p n t -> p (n t)"), pv[:, 0:512], [pbk], [dstb])
                yield
            prods = ((Bt, At, Mm, 0), (At, Bt, Nn, 1), (Kt, At, AKT, 0), (Bt, Rt, ARBT, 2), (Kt, Rt, ARKT, 2))
            for pi, (la, rb_, dst, mk) in enumerate(prods):
                bank = DB[pi % 2]
                for n in range(NCH):
                    for hh in range(2):
                        sl = slice(64 * hh, 64 * hh + 64)
                        mm(bank[sl, n * 64:(n + 1) * 64], la[sl, n * 64:(n + 1) * 64], rb_[sl, n * 64:(n + 1) * 64],
                           True, True, [la, rb_], [bank])
                tt("dve", fl(dst), bank[:], masks[:, mk, :], ALU.mult, [bank, masks], [dst])
                yield
            tt("dve", fl(Pm), fl(Mm), masks[:, 3, :], ALU.add, [Mm, masks], [Pm])
            Xc, XTc, Pc = Mm, Nn, Pm
            for lev in range(1, 6):
                lastlev = (lev == 5)
                Xn, XTn = Xa[lev % 2], XTa[lev % 2]
                Pn = Pf if lastlev else Pm
                for n in range(NCH):
                    for hh in range(2):
                        sl = slice(64 * hh, 64 * hh + 64)
                        cs = slice(n * 64, (n + 1) * 64)
                        mm(DB[0][sl, cs], Xc[sl, n, :], XTc[sl, n, :], True, True, [Xc, XTc], [DB[0]])
                        if not lastlev:
                            mm(DB[1][sl, cs], XTc[sl, n, :], Xc[sl, n, :], True, True, [Xc, XTc], [DB[1]])
                cp("act", fl(XTn), DB[0][:], [DB[0]], [XTn])
                if not lastlev:
                    cp("dve", fl(Xn), DB[1][:], [DB[1]], [Xn])
                yield
                for n in range(NCH):
                    for hh in range(2):
                        sl = slice(64 * hh, 64 * hh + 64)
                        cs = slice(n * 64, (n + 1) * 64)
                        mm(DB[0][sl, cs], XTn[sl, n, :], Pc[sl, n, :], True, True, [XTn, Pc], [DB[0]])
                tt("dve", fl(Pn), DB[0][:], fl(Pc), ALU.add, [DB[0], Pc], [Pn])
                yield
                Xc, XTc, Pc = Xn, XTn, Pn

        def gen_P2(c, X):
            st_ = X["set"]
            gst, rhsb, ub, s0w = X["gst"], X["rhsb"], X["ub"], X["s0w"]
            ysq = Buf(X["f_"][1][:].rearrange("p (n t) -> p n t", t=64), X["f_"][1].T)
            DB = X["DB"]
            At, Rt = st_["At"], st_["Rt"]
            Btok, Ktok, Vtok = st_["Btok"], st_["Ktok"], st_["Vtok"]
            gT, bonT, WC = st_["gT"], st_["bonT"], st_["WC"]
            Pf, AKT, ARBT, ARKT, ytok = st_["Pf"], st_["AKT"], st_["ARBT"], st_["ARKT"], st_["ytok"]
            if g == 0:
                P.op("dve", lambda e: e.memset(S32[:, c, :], 0.0), [], [S32T[c]])
                P.op("dve", lambda e: e.memset(Sbf[:, c, :], 0.0), [], [SbfT[c]])
            for n in range(NCH):
                cs = slice(n * 64, (n + 1) * 64)
                tb = DB[n % 2]
                pr, pu, pst, py = tb[:, 0:64], tb[:, 64:128], tb[:, 128:192], tb[:, 192:256]
                rb2, ub2 = rhsb[n % 2], ub[n % 2]
                for hh in range(2):
                    sl = slice(64 * hh, 64 * hh + 64)
                    mm(pr[sl, :], At[sl, cs], Sbf[sl, c, :], True, False, [At, SbfT[c]], [tb])
                    mm(pr[sl, :], AKT[sl, n, :], Vtok[sl, n, :], False, True, [AKT, Vtok], [tb])
                cp("act", rb2[:], pr, [tb], [rb2])
                ts("dve", s0w[:], S32[:, c, :], WC[:, n:n + 1], None, ALU.mult, None, [S32T[c], WC], [s0w])
                yield
                for hh in range(2):
                    sl = slice(64 * hh, 64 * hh + 64)
                    mm(pu[sl, :], Pf[sl, n, :], rb2[sl, :], True, True, [Pf, rb2], [tb])
                cp("act", ub2[:], pu, [tb], [ub2])
                yield
                for hh in range(2):
                    sl = slice(64 * hh, 64 * hh + 64)
                    mm(pst[sl, :], Btok[sl, n, :], ub2[sl, :], True, False, [Btok, ub2], [tb])
                    mm(pst[sl, :], Ktok[sl, n, :], Vtok[sl, n, :], False, True, [Ktok, Vtok], [tb])
                for hh in range(2):
                    sl = slice(64 * hh, 64 * hh + 64)
                    mm(py[sl, :], Rt[sl, cs], Sbf[sl, c, :], True, False, [Rt, SbfT[c]], [tb])
                    mm(py[sl, :], ARBT[sl, n, :], ub2[sl, :], False, False, [ARBT, ub2], [tb])
                    mm(py[sl, :], ARKT[sl, n, :], Vtok[sl, n, :], False, True, [ARKT, Vtok], [tb])
                stt(Sbf[:, c, :], pst, WC[:, n:n + 1], s0w[:], ALU.mult, ALU.add, [tb, WC, s0w], [SbfT[c]])
                stt(S32[:, c, :], pst, WC[:, n:n + 1], s0w[:], ALU.mult, ALU.add, [tb, WC, s0w], [S32T[c]])
                cp("act", ytok[:, n, :], py, [tb], [ytok])
                yield
            P.op("dve", lambda e: e.tensor_reduce(out=gst[:, 0, :], in_=ytok[:], axis=AX.X, op=ALU.add), [ytok], [gst])
            tt("dve", ysq[:], ytok[:], ytok[:], ALU.mult, [ytok], [ysq])
            P.op("dve", lambda e: e.tensor_reduce(out=gst[:, 1, :], in_=ysq[:], axis=AX.X, op=ALU.add), [ysq], [gst])
            ts("dve", gst[:, 2, :], gst[:, 0, :], 1.0 / 64.0, None, ALU.mult, None, [gst], [gst])
            tt("dve", gst[:, 0, :], gst[:, 2, :], gst[:, 2, :], ALU.mult, [gst], [gst])
            stt(gst[:, 3, :], gst[:, 1, :], 1.0 / 64.0, gst[:, 0, :], ALU.mult, ALU.subtract, [gst], [gst])
            act(gst[:, 3, :], gst[:, 3, :], AF.Ln, [gst, epsx], [gst], bias=epsx[:], scale=1.0)
            act(gst[:, 3, :], gst[:, 3, :], AF.Exp, [gst], [gst], scale=-0.5)
            yield
            tt("dve", ysq[:], ytok[:], bc_last(gst[:, 2, :], 64), ALU.subtract, [ytok, gst], [ysq])
            tt("dve", ysq[:], ysq[:], bc_last(gst[:, 3, :], 64), ALU.mult, [ysq, gst], [ysq])
            for n in range(NCH):
                for hh in range(2):
                    sl = slice(64 * hh, 64 * hh + 64)
                    mm(DB[0][sl, n * 64:(n + 1) * 64], ysq[sl, n, :], identf[sl, 64 * hh:64 * hh + 64], True, True,
                       [ysq, identf], [DB[0]])
            etmp = ysq[:].rearrange("p n t -> p (n t)")
            act(etmp, DB[0][:], AF.Identity, [DB[0], pc], [ysq], bias=pc[:, PC_LXB + c:PC_LXB + c + 1],
                scale=pc[:, PC_LXG + c:PC_LXG + c + 1])
            tt("dve", etmp, etmp, bonT[:], ALU.add, [ysq, bonT], [ysq])
            tt("dve", ybT_ap[:, c, :], etmp, gT[:], ALU.mult, [ysq, gT], [ybT_T])
            yield

        def d_thread(t):
            for c in (t, t + 2):
                for _ in gen_P1(c, ctx[t]):
                    yield
                for _ in gen_P2(c, ctx[t]):
                    yield

        gC = gen_C()
        _drain(_sched(P, [gC, gen_D_lora()], stop_when=1))
        _drain(_sched(P, [gC, d_thread(0), d_thread(1)], bias=[(25.0 if g >= 2 else (8.0 if g == 1 else -15.0)), 0.0, 0.0]))
        if dbg_here:
            dump("yaT", Buf(yaT_ap, yaT_T), [128, 4, TOK], BF16)
            dump("ybT", Buf(ybT_ap, ybT_T), [128, 4, TOK], BF16)

        RE = RegionAlloc(0)
        mT = RE([128, 8, TOK], BF16)
        sga = [RE([128, TOK], F32) for _ in range(2)]
        sgb = [RE([128, TOK], F32) for _ in range(2)]
        n1 = [RE([128, D], F32) for _ in range(2)]
        lnbuf = RE([128, 2048], F32)
        dma("sp", lnbuf[:], pbc_d[:, PB_LN1:PB_LN1 + 2048], [pbc_d], [lnbuf])
        for f in range(8):
            wf = ws_get()
            ga3 = wf[:, 0:1024].rearrange("p (c n) -> p c n", c=8)
            gb3 = wf[:, 1024:2048].rearrange("p (c n) -> p c n", c=8)
            ua3 = wf[:, 2048:2560].rearrange("p (c n) -> p c n", c=4)
            ub3 = wf[:, 2560:3072].rearrange("p (c n) -> p c n", c=4)
            o4 = 4 * (f % 2)
            pga, pgb, pua, pub = ps[o4], ps[o4 + 1], ps[o4 + 2], ps[o4 + 3]
            wr_ = [pub]
            for kc in range(8):
                mm(pga[:], ga3[:, kc, :], hT[:, kc, :], kc == 0, kc == 7, [wf, hT], [pga])
            for kc in range(8):
                mm(pgb[:], gb3[:, kc, :], hT[:, kc, :], kc == 0, kc == 7, [wf, hT], [pgb])
            for kc in range(4):
                mm(pua[:], ua3[:, kc, :], yaT_ap[:, kc, :], kc == 0, kc == 3, [wf, yaT_T], [pua])
            for kc in range(4):
                mm(pub[:], ub3[:, kc, :], ybT_ap[:, kc, :], kc == 0, kc == 3, [wf, ybT_T], wr_)
            ws_release()
            sa, sb_ = sga[f % 2], sgb[f % 2]
            act(sa[:], pga[:], AF.Sigmoid, [pga], [sa])
            act(sb_[:], pgb[:], AF.Sigmoid, [pgb], [sb_])
            tt("dve", sa[:], sa[:], pua[:], ALU.mult, [sa, pua], [sa])
            tt("dve", sb_[:], sb_[:], pub[:], ALU.mult, [sb_] + wr_, [sb_])
            tt("dve", mT[:, f, :], sa[:], sb_[:], ALU.add, [sa, sb_], [mT])
        if dbg_here:
            dump("mT", mT, [128, 8, TOK], BF16)
        for half in range(2):
            wo = ws_get()
            wo3 = wo[:, 0:4096].rearrange("p (c n) -> p c n", c=8)
            for j in range(4):
                pb = ps[(2 * half + j) % 4]
                for f in range(8):
                    mm(pb[:], mT[:, f, j * 128:(j + 1) * 128], wo3[:, f, :], f == 0, f == 7, [wo, mT], [pb])
                stt(hres_t[:, j, half * 512:(half + 1) * 512], hres_t[:, j, half * 512:(half + 1) * 512], ALPHA, pb[:],
                    ALU.mult, ALU.add, [hresT[j], pb], [hresT[j]])
            ws_release()
        h1T_T = T(init_reads=P.snapshot(), t0=P.now())
        for j in range(4):
            nb = n1[j % 2]
            mv, rs = layer_norm_stats(hres_t[:, j, :], hresT[j], j % 2)
            ts("dve", nb[:], hres_t[:, j, :], mv[:, 0:1], rs[:], ALU.subtract, ALU.mult, [hresT[j], mv, rs], [nb])
            tt("pool", hres_t[:, j, :], nb[:], lnbuf[:, 0:1024], ALU.mult, [nb, lnbuf], [hresT[j]])
            tt("pool", hres_t[:, j, :], hres_t[:, j, :], lnbuf[:, 1024:2048], ALU.add, [hresT[j], lnbuf], [hresT[j]])
            transposes_to_featmajor(nb, nb, yab, h1T_T, j, PC_LN1_G, PC_LN1_B, (ps[4], ps[5]))

        RF = RegionAlloc(0)
        actT = RF([128, NHC, TOK], BF16)
        sil = [RF([128, TOK], F32) for _ in range(2)]
        n2 = [RF([128, D], F32) for _ in range(2)]
        lnbuf = RF([128, 2048], F32)
        dma("sp", lnbuf[:], pbc_d[:, PB_LN2:PB_LN2 + 2048], [pbc_d], [lnbuf])
        for u in range(11):
            wf = ws_get()
            w4 = wf[:, 0:4096].rearrange("p (a c n) -> p a c n", a=2, c=8)
            for q in range(2):
                hc = 2 * u + q
                pg, pu_ = ps[2 * (hc % 3)], ps[2 * (hc % 3) + 1]
                for kc in range(8):
                    mm(pg[:], w4[:, 0, kc, q * 128:(q + 1) * 128], yab[:, kc, :], kc == 0, kc == 7, [wf, h1T_T], [pg])
                for kc in range(8):
                    mm(pu_[:], w4[:, 1, kc, q * 128:(q + 1) * 128], yab[:, kc, :], kc == 0, kc == 7, [wf, h1T_T], [pu_])
                sl_ = sil[hc % 2]
                act(sl_[:], pg[:], AF.Silu, [pg], [sl_])
                tt("dve", actT[:, hc, :], sl_[:], pu_[:], ALU.mult, [sl_, pu_], [actT])
            ws_release()
        for u in range(8):
            wdn = ws_get()
            wd3 = wdn[:, 0:NHC * 128].rearrange("p (c n) -> p c n", c=NHC)
            pb = ps[u % 2]
            for j in range(4):
                for hc in range(NHC):
                    mm(pb[:, j * 128:(j + 1) * 128], actT[:, hc, j * 128:(j + 1) * 128], wd3[:, hc, :], hc == 0, hc == NHC - 1,
                       [wdn, actT], [pb])
            ws_release()
            for j in range(4):
                stt(hres_t[:, j, u * 128:(u + 1) * 128], hres_t[:, j, u * 128:(u + 1) * 128], ALPHA, pb[:, j * 128:(j + 1) * 128],
                    ALU.mult, ALU.add, [hresT[j], pb], [hresT[j]])
        for j in range(4):
            nb = n2[j % 2]
            mv, rs = layer_norm_stats(hres_t[:, j, :], hresT[j], j % 2)
            ts("dve", nb[:], hres_t[:, j, :], mv[:, 0:1], rs[:], ALU.subtract, ALU.mult, [hresT[j], mv, rs], [nb])
            tt("pool", nb[:], nb[:], lnbuf[:, 0:1024], ALU.mult, [nb, lnbuf], [nb])
            tt("pool", nb[:], nb[:], lnbuf[:, 1024:2048], ALU.add, [nb, lnbuf], [nb])
            dma("sp", y_d[b, t0 + j * 128:t0 + (j + 1) * 128, :], nb[:], [nb], [y_d])

    P.final_wait("sp", [y_d] + list(dbg_outs.values()))
    P.emit()
    return nc, dbg_outs


def _t5_bucket_np(dist):
    d = np.maximum(dist, 1).astype(np.float32)
    large = 16 + (np.log(d / np.float32(16)) / np.float32(math.log(128 / 16)) * np.float32(16)).astype(np.int32)
    large = np.minimum(large, 31)
    return np.where(dist < 16, dist, large)


def _host_constants():
    ohu = np.zeros((33, 383), np.float32)
    for i in range(383):
        if i < 127:
            ohu[32, i] = MASKVAL
        else:
            bkt = int(_t5_bucket_np(np.array([i - 127], np.int32))[0])
            ohu[bkt, i] += 8.0
            ohu[31, i] -= 8.0
    p = np.arange(128)[:, None] % 64
    t = np.arange(64)[None, :]
    m = np.stack([(p < t), (t < p), (p <= t), (p == t)], 0).astype(np.float32)
    masks = np.ascontiguousarray(np.broadcast_to(m[:, :, None, :], (4, 128, 8, 64)).transpose(1, 0, 2, 3).reshape(128, 4, 512))
    ident = np.eye(128, dtype=np.float32)
    jmat = np.ascontiguousarray(ident[::-1])
    bones = np.zeros((128, 128), np.float32)
    bones[:64, :64] = 1.0
    bones[64:, 64:] = 1.0
    cm = np.ones((128, 512), np.float32)
    cm[:, ::64] = 0.0
    return dict(ohu=ohu, masks=masks, ident=ident, jmat=jmat, bones=bones, cmask=cm)


def _prep_inputs(inp):
    f = lambda a: np.ascontiguousarray(np.asarray(a, dtype=np.float32))
    row = lambda a: np.asarray(a, np.float32).reshape(-1)
    pbc_row = np.concatenate([row(inp["ln_in_g"]), row(inp["ln_in_b"]), row(inp["ln1_g"]), row(inp["ln1_b"]),
                              row(inp["ln2_g"]), row(inp["ln2_b"]), row(inp["diff_subln_g"]),
                              row(inp["diff_lam_q1"]), row(inp["diff_lam_k1"]), row(inp["diff_lam_q2"]),
                              row(inp["diff_lam_k2"]), row(np.asarray(inp["rel_bias"])[31])])
    assert pbc_row.shape[0] == PB_N
    pbc = np.ascontiguousarray(np.broadcast_to(pbc_row[None, :], (128, PB_N)))
    col = lambda a: row(a).reshape(-1, 128).T
    pcol = np.ascontiguousarray(np.concatenate(
        [col(inp["ln_in_g"]), col(inp["ln_in_b"]), col(inp["ln1_g"]), col(inp["ln1_b"]), col(inp["rwkv_mu"]),
         col(inp["rwkv_w0"]), col(inp["rwkv_a0"]), col(inp["rwkv_k_k"]), col(inp["rwkv_k_a"]), col(inp["rwkv_r_k"]),
         col(inp["rwkv_lnx_g"]), col(inp["rwkv_lnx_b"])], axis=1))
    assert pcol.shape == (128, PC_N)
    w2a2 = np.ascontiguousarray(np.concatenate([np.asarray(inp["rwkv_w2"], np.float32)[0],
                                                np.asarray(inp["rwkv_a2"], np.float32)[0]], 0))
    rbaug = np.ascontiguousarray(np.concatenate([np.asarray(inp["rel_bias"], np.float32), np.ones((1, 4), np.float32)], 0))
    shared = dict(w_in=f(inp["w_in"][0]), w_up_a=f(inp["w_up_a"][0]), w_up_b=f(inp["w_up_b"][0]), w_out=f(inp["w_out"][0]),
                  wg=f(inp["ffn_w_gate"][0]), wu=f(inp["ffn_w_up"][0]), wd=f(inp["ffn_w_down"][0]),
                  pbc=pbc, pcol=pcol, w2a2=w2a2, g2=f(inp["rwkv_g2"][0]), rbaug=rbaug)
    shared.update(_host_constants())
    return shared


_CACHE = {}


def kernel(**inputs):
    x = np.asarray(inputs["x"], np.float32)
    shared = _prep_inputs(inputs)
    if "nc" not in _CACHE:
        _CACHE["nc"] = build_program()[0]
    nc = _CACHE["nc"]
    in_maps = []
    for c in range(NCORES):
        m = dict(shared)
        m["x"] = np.ascontiguousarray(x[BL * c:BL * (c + 1)])
        in_maps.append(m)
    res = run_bass_kernel_spmd(nc, in_maps, core_ids=list(range(NCORES)))
    out = np.concatenate([np.asarray(r["y"], np.float32) for r in res.results], axis=0)
    return out
```

```python
import math
import numpy as np
import concourse.bass as bass
import concourse.mybir as mybir
from concourse.bass_utils import run_bass_kernel_spmd

F32 = mybir.dt.float32
BF16 = mybir.dt.bfloat16
ALU = mybir.AluOpType
AF = mybir.ActivationFunctionType
AX = mybir.AxisListType

NCORES = 8
D = 1024
SEQ = 2048
BL = 2
TOK = 512
NSLAB = SEQ // TOK
ALPHA = 2.0 ** 0.25
LAMBDA_INIT = 0.2
FFN = 2816
NHC = FFN // 128
C = 64
NCH = TOK // C
MASKVAL = -1.0e9
EXPM05 = math.exp(-0.5)

PB_LNIN, PB_LN1, PB_LN2, PB_SUB, PB_LAM, PB_RB31, PB_N = 0, 2048, 4096, 6144, 6272, 6528, 6532
PC_LNIN_G, PC_LNIN_B, PC_LN1_G, PC_LN1_B, PC_MU, PC_W0, PC_A0, PC_KK, PC_KA, PC_RK, PC_LXG, PC_LXB, PC_N = \
    0, 8, 16, 24, 32, 46, 50, 54, 58, 62, 66, 70, 74


class T:
    __slots__ = ("w", "r", "excl", "tw", "tr")

    def __init__(self, init_reads=None, excl=False, t0=0.0):
        self.w = None
        self.r = dict(init_reads) if init_reads else {}
        self.excl = excl
        self.tw = 0.0
        self.tr = t0


class Buf:
    def __init__(self, t, tr=None):
        self.t = t
        self.T = tr if tr is not None else T()

    def __getitem__(self, idx):
        return self.t[idx]


def _T(x):
    return x.T if isinstance(x, Buf) else x


class Prog:
    ENG = ("pe", "dve", "act", "pool", "sp")

    def __init__(self, nc, n_dma_sems=10):
        self.nc = nc
        self.sems = {}
        self.cnt = {}
        for e in ("pe", "dve", "act", "pool"):
            self.sems[e] = nc.alloc_semaphore("s_" + e)
            self.cnt[e] = 0
        self.known = {e: {} for e in self.ENG}
        self.lists = {e: [] for e in self.ENG}
        self.dsem = {}
        for q in ("sp", "pool", "poolc"):
            lst = []
            for i in range(n_dma_sems):
                k = "d_%s_%d" % (q, i)
                self.sems[k] = nc.alloc_semaphore(k)
                self.cnt[k] = 0
                lst.append(k)
            self.dsem[q] = lst
        self.dnext = {"sp": 0, "pool": 0, "poolc": 0}
        self.efree = {e: 0.0 for e in self.ENG}
        self.step_fin = 0.0

    def now(self):
        return max(self.efree.values())

    def _time(self, e, reads, writes, cost, issue=None):
        ready = 0.0
        for t in reads:
            ready = max(ready, t.tw)
        for t in writes:
            ready = max(ready, t.tw, t.tr)
        start = max(self.efree[e], ready + 0.2)
        if issue is None:
            fin = start + cost
            self.efree[e] = fin
        else:
            self.efree[e] = start + issue
            fin = start + cost
        for t in reads:
            t.tr = max(t.tr, fin)
        for t in writes:
            t.tw = fin
            t.tr = 0.0
        self.step_fin = max(self.step_fin, fin)

    def snapshot(self):
        return {k: v for k, v in self.cnt.items() if v > 0 and not k.startswith("d_poolc")}

    def _deps(self, e, reads, writes, skip_self=False):
        deps = {}

        def add(k, v):
            if deps.get(k, 0) < v:
                deps[k] = v
        for t in reads:
            if t.w is not None:
                add(*t.w)
        for t in writes:
            if t.w is not None:
                add(*t.w)
            for k, v in t.r.items():
                add(k, v)
        waits = []
        kn = self.known[e]
        for k, v in deps.items():
            if skip_self and k == e:
                continue
            if kn.get(k, 0) < v:
                kn[k] = v
                waits.append((k, v))
        return waits

    def _mark(self, ev, reads, writes):
        k, v = ev
        for t in reads:
            if t.r.get(k, 0) < v:
                t.r[k] = v
        for t in writes:
            t.w = ev
            t.r = {}

    def op(self, e, fn, reads=(), writes=(), cost=0.3):
        reads = [_T(x) for x in reads]
        writes = [_T(x) for x in writes]
        writes = writes + [t for t in reads if t.excl]
        reads = [t for t in reads if not t.excl]
        self._time(e, reads, writes, cost)
        waits = self._deps(e, reads, writes, skip_self=(e == "pe"))
        self.cnt[e] += 1
        ev = (e, self.cnt[e])
        self.lists[e].append((waits, fn, (e, 1)))
        self._mark(ev, reads, writes)

    def dma(self, q, fn, reads=(), writes=(), cost=4.0):
        reads = [_T(x) for x in reads]
        writes = [_T(x) for x in writes]
        eng = "pool" if q == "poolc" else q
        self._time(eng, reads, writes, cost, issue=(0.1 if eng == "sp" else 1.0))
        waits = self._deps(eng, reads, writes)
        lst = self.dsem[q]
        k = lst[self.dnext[q] % len(lst)]
        self.dnext[q] += 1
        prev = self.cnt[k]
        if prev > 0 and self.known[eng].get(k, 0) < prev:
            self.known[eng][k] = prev
            waits.append((k, prev))
        self.cnt[k] += 16
        ev = (k, self.cnt[k])
        self.lists[eng].append((waits, fn, (k, 16)))
        self._mark(ev, reads, writes)

    def final_wait(self, e, tiles):
        waits = self._deps(e, [_T(x) for x in tiles], ())
        self.lists[e].append((waits, None, None))

    def emit(self):
        nc = self.nc
        with nc.Block() as block:
            def run(name):
                def body(engh):
                    for waits, fn, inc in self.lists[name]:
                        for k, v in waits:
                            engh.wait_ge(self.sems[k], v)
                        if fn is not None:
                            fn(engh).then_inc(self.sems[inc[0]], inc[1])
                return body
            block.sync(run("sp"))
            block.scalar(run("act"))
            block.vector(run("dve"))
            block.gpsimd(run("pool"))
            block.tensor(run("pe"))


def bc_last(ap, n):
    return bass.AP(tensor=ap.tensor, offset=ap.offset, ap=[list(x) for x in ap.ap] + [[0, n]])


def _merge(ga, gb, na, nb):
    ia = ib = 0
    da = db = False
    while not (da and db):
        if not da and (db or ia * nb <= ib * na):
            try:
                next(ga)
                ia += 1
            except StopIteration:
                da = True
        elif not db:
            try:
                next(gb)
                ib += 1
            except StopIteration:
                db = True
        yield
    return


def _sched(P, gens, stop_when=None, bias=None):
    clocks = [0.0] * len(gens)
    live = list(range(len(gens)))
    while live:
        i = min(live, key=lambda j: (clocks[j] - (bias[j] if bias else 0.0), j))
        P.step_fin = 0.0
        try:
            next(gens[i])
        except StopIteration:
            live.remove(i)
            if stop_when is not None and i == stop_when:
                return
            continue
        if P.step_fin > 0.0:
            clocks[i] = P.step_fin
        yield


def _drain(g):
    n = 0
    for _ in g:
        n += 1
    return n


def _chain(*gens):
    for g_ in gens:
        for _ in g_:
            yield


NRING = 3


def build_program(n_slabs_total=BL * NSLAB, dbg=None):
    nc = bass.Bass("TRN2", target_bir_lowering=False)
    P = Prog(nc)

    def din(name, shape):
        return Buf(nc.dram_tensor(name, list(shape), F32, kind="ExternalInput").ap())

    x_d = din("x", [BL, SEQ, D])
    w_in_d = din("w_in", [D, 5376])
    w_upa_d = din("w_up_a", [512, D])
    w_upb_d = din("w_up_b", [512, D])
    w_out_d = din("w_out", [D, D])
    wg_d = din("wg", [D, FFN])
    wu_d = din("wu", [D, FFN])
    wd_d = din("wd", [FFN, D])
    pbc_d = din("pbc", [128, PB_N])
    pcol_d = din("pcol", [128, PC_N])
    w2a2_d = din("w2a2", [128, 512])
    g2_d = din("g2", [128, 512])
    rbaug_d = din("rbaug", [33, 4])
    ohu_d = din("ohu", [33, 383])
    masks_d = din("masks", [128, 4, 512])
    ident_d = din("ident", [128, 128])
    jmat_d = din("jmat", [128, 128])
    bones_d = din("bones", [128, 128])
    cmask_d = din("cmask", [128, 512])
    y_d = Buf(nc.dram_tensor("y", [BL, SEQ, D], F32, kind="ExternalOutput").ap())
    scr_d = Buf(nc.dram_tensor("scr", [4, 383], F32, kind="Internal").ap())
    dbg_outs = {}

    off = [16384 + 256]
    SB_END = 16384 + 212863 - 256

    def nbytes(shape, dt):
        nb = int(np.prod(shape[1:])) * (2 if dt == BF16 else 4)
        return (nb + 63) // 64 * 64

    def S(name, shape, dt):
        nb = nbytes(shape, dt)
        o = off[0]
        off[0] += nb
        assert o + nb <= SB_END, (name, o, nb)
        return Buf(nc.alloc_sbuf_tensor_at(name, list(shape), dt, offset=o))

    ps = [Buf(nc.alloc_psum_tensor("ps%d" % i, [128, 512], F32), T(excl=True)) for i in range(8)]

    def _n(ap):
        return int(np.prod(ap.shape[1:]))

    def mm(out, lhsT, rhs, start, stop, reads, writes):
        c_ = 0.035 + 0.00036 * _n(rhs)
        if rhs.tensor.dtype == F32:
            c_ *= 4.0
        P.op("pe", lambda e: e.matmul(out=out, lhsT=lhsT, rhs=rhs, start=start, stop=stop), reads, writes, cost=c_)

    def trp(out, in_, ident, reads, writes):
        P.op("pe", lambda e: e.transpose(out=out, in_=in_, identity=ident), reads, writes, cost=0.07)

    def act(out, in_, func, reads, writes, bias=None, scale=None):
        kw = {}
        if bias is not None:
            kw["bias"] = bias
        if scale is not None:
            kw["scale"] = scale
        P.op("act", lambda e: e.activation(out=out, in_=in_, func=func, **kw), reads, writes, cost=0.15 + 0.00105 * _n(out))

    def _vc(eng, out):
        return (0.3 + 0.002 * _n(out)) if eng == "pool" else (0.08 + 0.0012 * _n(out))

    def tt(eng, out, in0, in1, op, reads, writes):
        P.op(eng, lambda e: e.tensor_tensor(out=out, in0=in0, in1=in1, op=op), reads, writes, cost=_vc(eng, out))

    def ts(eng, out, in0, s1, s2, op0, op1, reads, writes):
        if s2 is None:
            P.op(eng, lambda e: e.tensor_scalar(out=out, in0=in0, scalar1=s1, scalar2=None, op0=op0), reads, writes,
                 cost=_vc(eng, out))
        else:
            P.op(eng, lambda e: e.tensor_scalar(out=out, in0=in0, scalar1=s1, scalar2=s2, op0=op0, op1=op1), reads, writes,
                 cost=_vc(eng, out))

    def stt(out, in0, scalar, in1, op0, op1, reads, writes):
        P.op("dve", lambda e: e.scalar_tensor_tensor(out=out, in0=in0, scalar=scalar, in1=in1, op0=op0, op1=op1), reads, writes,
             cost=_vc("dve", out))

    def cp(eng, out, in_, reads, writes):
        if eng == "act":
            P.op("act", lambda e: e.activation(out=out, in_=in_, func=AF.Copy), reads, writes, cost=0.15 + 0.00105 * _n(out))
        else:
            P.op(eng, lambda e: e.tensor_copy(out=out, in_=in_), reads, writes, cost=_vc(eng, out))

    def dma(q, out, in_, reads, writes):
        P.dma(q, lambda e: e.dma_start(out=out, in_=in_), reads, writes)

    def dump(name, buf, shape, dt=F32, reads=None):
        if dbg is None or name not in dbg:
            return
        d = Buf(nc.dram_tensor("dbg_" + name, list(shape), dt, kind="ExternalOutput").ap())
        dbg_outs[name] = d
        dma("sp", d[:], buf[:], reads if reads is not None else [buf], [d])

    pc = S("pc", [128, PC_N], F32)
    csm = S("csm", [128, 388], F32)
    identf = S("identf", [128, 128], F32)
    identb = S("identb", [128, 128], BF16)
    bonesf = S("bonesf", [128, 128], F32)
    jm = S("jm", [128, 128], F32)
    masks = S("masks", [128, 4, 512], BF16)
    cmask = S("cmask", [128, 512], F32)
    mb = S("mb", [128, 8, 128], BF16)
    w2a2b = S("w2a2b", [128, 512], BF16)
    g2b = S("g2b", [128, 512], BF16)
    zb = S("zb", [128, 264], BF16)
    neglam = S("neglam", [128, 1], F32)
    eps5 = S("eps5", [128, 1], F32)
    epsx = S("epsx", [128, 1], F32)
    one1 = S("one1", [128, 1], F32)
    mhalf = S("mhalf", [128, 1], F32)
    ln2x20 = S("ln2x20", [128, 1], F32)
    npc = S("npc", [128, PC_N], F32)
    vaug = S("vaug", [128, 16, 4, 129], BF16)
    kT = S("kT", [128, 4, SEQ], BF16)
    S32 = S("S32", [128, 4, 64], F32)
    Sbf = S("Sbf", [128, 4, 64], BF16)
    S32T = [T() for _ in range(4)]
    SbfT = [T() for _ in range(4)]
    prevcol = S("prevcol", [128, 14], F32)
    ring = [S("ring%d" % i, [128, 4096], BF16) for i in range(NRING)]
    hres_t = S("hres", [128, 4, D], F32)
    hresT = [T() for _ in range(4)]
    hT = S("hT", [128, 8, TOK], BF16)
    yab = S("yab", [128, 8, TOK], BF16)
    stt_ = [S("st%d" % i, [128, 2, 6], F32) for i in range(2)]
    mv_ = [S("mv%d" % i, [128, 2], F32) for i in range(2)]
    rs_ = [S("rs%d" % i, [128, 1], F32) for i in range(2)]
    RBYTES = (SB_END - off[0]) // 256 * 256
    arena = S("arena", [128, RBYTES // 4], F32)

    def RV(o, shape, dt, fence=True):
        nb = nbytes(shape, dt)
        assert o % 64 == 0 and o + nb <= RBYTES, (o, nb, RBYTES)
        a = arena.t[0:shape[0], o // 4:(o + nb) // 4]
        if dt == BF16:
            a = a.bitcast(BF16)
        n = int(np.prod(shape[1:]))
        a = a[:, 0:n]
        if len(shape) == 3:
            a = a.rearrange("p (a b) -> p a b", a=shape[1])
        elif len(shape) == 4:
            a = a.rearrange("p (a b c) -> p a b c", a=shape[1], b=shape[2])
        return Buf(a, T(init_reads=P.snapshot(), t0=P.now()) if fence else T())

    class RegionAlloc:
        def __init__(self, base):
            self.o = base

        def __call__(self, shape, dt):
            b_ = RV(self.o, shape, dt)
            self.o += nbytes(shape, dt)
            return b_

    units = []
    ws = {"load": 0, "use": 0}

    def ws_load_next():
        u = ws["load"]
        if u < len(units):
            units[u](ring[u % NRING])
            ws["load"] += 1

    def ws_get():
        u = ws["use"]
        assert u < ws["load"], "unit not loaded"
        return ring[u % NRING]

    def ws_release():
        ws["use"] += 1
        ws_load_next()

    released = set()

    def ws_release_unit(u):
        released.add(u)
        while ws["use"] in released:
            released.discard(ws["use"])
            ws["use"] += 1
            ws_load_next()

    w_in_r = w_in_d.t.rearrange("(c p) n -> p c n", p=128)
    upa_r = w_upa_d.t.rearrange("(c p) n -> p c n", p=128)
    upb_r = w_upb_d.t.rearrange("(c p) n -> p c n", p=128)
    wout_r = w_out_d.t.rearrange("(c p) n -> p c n", p=128)
    wg_r = wg_d.t.rearrange("(c p) n -> p c n", p=128)
    wu_r = wu_d.t.rearrange("(c p) n -> p c n", p=128)
    wd_r = wd_d.t.rearrange("(c p) n -> p c n", p=128)

    unit_parts = []

    def u_multi(parts):
        unit_parts.append(parts)

    def u_cols(src_r, wbuf, kc, col0, ncols):
        u_multi([(src_r, wbuf, kc, col0, ncols)])

    u_cols(w_in_r, w_in_d, 8, 0, 512)
    u_cols(w_in_r, w_in_d, 8, 512, 512)
    u_cols(w_in_r, w_in_d, 8, 1024, 512)
    u_cols(w_in_r, w_in_d, 8, 3072, 256)
    for c in range(4):
        u_multi([(w_in_r, w_in_d, 8, 1536 + 512 * i + 128 * c, 128) for i in range(3)])
    for f in range(8):
        u_multi([(w_in_r, w_in_d, 8, 3328 + 128 * f, 128),
                 (w_in_r, w_in_d, 8, 4352 + 128 * f, 128),
                 (upa_r, w_upa_d, 4, 128 * f, 128),
                 (upb_r, w_upb_d, 4, 128 * f, 128)])
    for half in range(2):
        u_cols(wout_r, w_out_d, 8, 512 * half, 512)
    for u in range(11):
        u_multi([(wg_r, wg_d, 8, 256 * u, 256), (wu_r, wu_d, 8, 256 * u, 256)])
    for u in range(8):
        u_cols(wd_r, wd_d, 22, 128 * u, 128)
    NU = len(unit_parts)
    wbf = nc.dram_tensor("wbf", [NU, 128, 4096], BF16, kind="Internal").ap()
    wbfT = [T() for _ in range(NU)]
    usize = []
    for u, parts in enumerate(unit_parts):
        usize.append(sum(kc * ncols for (_, _, kc, _, ncols) in parts))

    def emit_conversions():
        for u, parts in enumerate(unit_parts):
            o = 0
            for (src_r, wbuf, kc, col0, ncols) in parts:
                dst = wbf[u, :, o:o + kc * ncols].rearrange("p (c n) -> p c n", c=kc)
                dma("poolc", dst, src_r[:, :, col0:col0 + ncols], [wbuf], [wbfT[u]])
                o += kc * ncols

    def mk_loader(u):
        def loader(slot):
            dma("sp", slot[:, 0:usize[u]], wbf[u, :, 0:usize[u]], [wbfT[u]], [slot])
        return loader

    for _ in range(n_slabs_total):
        for u in range(NU):
            units.append(mk_loader(u))

    dma("sp", pc[:], pcol_d[:, :], [pcol_d], [pc])
    dma("sp", csm[:], pbc_d[:, PB_SUB:PB_N], [pbc_d], [csm])
    dma("sp", identf[:], ident_d[:, :], [ident_d], [identf])
    dma("sp", bonesf[:], bones_d[:, :], [bones_d], [bonesf])
    dma("sp", jm[:], jmat_d[:, :], [jmat_d], [jm])
    dma("sp", cmask[:], cmask_d[:, :], [cmask_d], [cmask])
    dma("pool", masks[:], masks_d[:, :, :], [masks_d], [masks])
    dma("pool", w2a2b[:], w2a2_d[:, :], [w2a2_d], [w2a2b])
    dma("pool", g2b[:], g2_d[:, :], [g2_d], [g2b])
    emit_conversions()
    for _ in range(NRING):
        ws_load_next()
    cp("dve", identb[:], identf[:], [identf], [identb])
    P.op("dve", lambda e: e.memset(eps5[:], 1e-5), [], [eps5])
    P.op("dve", lambda e: e.memset(epsx[:], 64e-5), [], [epsx])
    P.op("dve", lambda e: e.memset(one1[:], 1.0), [], [one1])
    P.op("dve", lambda e: e.memset(mhalf[:], -0.5), [], [mhalf])
    P.op("dve", lambda e: e.memset(ln2x20[:], 20.0 * math.log(2.0)), [], [ln2x20])
    ts("dve", npc[:], pc[:], -1.0, None, ALU.mult, None, [pc], [npc])
    P.op("dve", lambda e: e.memset(zb[:], 0.0), [], [zb])
    P.op("dve", lambda e: e.memset(vaug[:].rearrange("p a b c -> p (a b c)"), 1.0), [], [vaug])
    ts("dve", csm[:, 0:128], csm[:, 0:128], 1.0 - LAMBDA_INIT, None, ALU.mult, None, [csm], [csm])
    RA = RegionAlloc(0)
    lt = RA([128, 2, 64], F32)
    ls = RA([128, 2], F32)
    le = RA([128, 2], F32)
    tt("dve", lt[:, 0, :], csm[:, 128:192], csm[:, 192:256], ALU.mult, [csm], [lt])
    tt("dve", lt[:, 1, :], csm[:, 256:320], csm[:, 320:384], ALU.mult, [csm], [lt])
    P.op("dve", lambda e: e.tensor_reduce(out=ls[:], in_=lt[:], axis=AX.X, op=ALU.add), [lt], [ls])
    act(le[:], ls[:], AF.Exp, [ls], [le])
    tt("dve", neglam[:], le[:, 1:2], le[:, 0:1], ALU.subtract, [le], [neglam])
    ts("dve", neglam[:], neglam[:], -LAMBDA_INIT, None, ALU.add, None, [neglam], [neglam])
    def layer_norm_stats(src_ap, srcT, k):
        st, mv, rs = stt_[k], mv_[k], rs_[k]
        for i in range(2):
            P.op("dve", lambda e, i=i: e.bn_stats(out=st[:, i, :], in_=src_ap[:, i * 512:(i + 1) * 512]), [srcT], [st])
        P.op("dve", lambda e: e.bn_aggr(out=mv[:], in_=st[:].rearrange("p a b -> p (a b)")), [st], [mv])
        act(rs[:], mv[:, 1:2], AF.Ln, [mv, eps5], [rs], bias=eps5[:], scale=1.0)
        act(rs[:], rs[:], AF.Exp, [rs], [rs], scale=-0.5)
        return mv, rs

    def transposes_to_featmajor(src, srcT, dst_ap, dstT, j, gcol, bcol, pbanks):
        for half in range(2):
            pb = pbanks[half]
            for q in range(4):
                kc = half * 4 + q
                trp(pb[:, q * 128:(q + 1) * 128], src[:, kc * 128:(kc + 1) * 128], identf[:], [srcT, identf], [pb])
            for q in range(4):
                kc = half * 4 + q
                act(dst_ap[:, kc, j * 128:(j + 1) * 128], pb[:, q * 128:(q + 1) * 128], AF.Identity,
                    [pb, pc], [dstT], bias=pc[:, bcol + kc:bcol + kc + 1], scale=pc[:, gcol + kc:gcol + kc + 1])

    psb = [ps[i][:].bitcast(BF16) for i in range(8)]
    yaT_ap = yab[:, 0:4, :]
    ybT_ap = yab[:, 4:8, :]
    CREG = 13056
    counts = {}

    for slab_i in range(n_slabs_total):
        b = slab_i // NSLAB
        g = slab_i % NSLAB
        t0 = g * TOK
        dbg_here = (dbg is not None and slab_i == dbg.get("slab", 0))

        RA_ = RegionAlloc(0)
        xt = [RA_([128, D], F32) for _ in range(2)]
        lnbuf = RA_([128, 2048], F32)
        dma("sp", lnbuf[:], pbc_d[:, PB_LNIN:PB_LNIN + 2048], [pbc_d], [lnbuf])
        for j in range(4):
            xb = xt[j % 2]
            dma("sp", xb[:], x_d[b, t0 + j * 128:t0 + (j + 1) * 128, :], [x_d], [xb])
            mv, rs = layer_norm_stats(xb, xb, j % 2)
            ts("dve", xb[:], xb[:], mv[:, 0:1], rs[:], ALU.subtract, ALU.mult, [xb, mv, rs], [xb])
            he = "dve" if slab_i == 0 else "pool"
            tt(he, hres_t[:, j, :], xb[:], lnbuf[:, 0:1024], ALU.mult, [xb, lnbuf], [hresT[j]])
            tt(he, hres_t[:, j, :], hres_t[:, j, :], lnbuf[:, 1024:2048], ALU.add, [hresT[j], lnbuf], [hresT[j]])
            transposes_to_featmajor(xb, xb, hT, hT, j, PC_LNIN_G, PC_LNIN_B, (ps[0], ps[1]))
        if dbg_here:
            dump("hT", hT, [128, 8, TOK], BF16)

        RC = RegionAlloc(0)
        qT = RC([128, 4, TOK], BF16)
        wq = ws_get()
        wq3 = wq[:, 0:4096].rearrange("p (c n) -> p c n", c=8)
        for h in range(4):
            pb = ps[2 + h % 2]
            for kc in range(8):
                mm(pb[:], wq3[:, kc, h * 128:(h + 1) * 128], hT[:, kc, :], kc == 0, kc == 7, [wq, hT], [pb])
            cp("act" if h % 2 else "dve", qT[:, h, :], pb[:], [pb], [qT])
        ws_release()
        wk = ws_get()
        wk3 = wk[:, 0:4096].rearrange("p (c n) -> p c n", c=8)
        for h in range(4):
            pb = ps[2 + h % 2]
            for kc in range(8):
                mm(pb[:], wk3[:, kc, h * 128:(h + 1) * 128], hT[:, kc, :], kc == 0, kc == 7, [wk, hT], [pb])
            cp("act" if h % 2 else "dve", kT[:, h, t0:t0 + TOK], pb[:], [pb], [kT])
        ws_release()
        wv = ws_get()
        wv3 = wv[:, 0:4096].rearrange("p (c n) -> p c n", c=8)
        for j in range(4):
            pb = ps[2 + j % 2]
            for kc in range(8):
                mm(pb[:], hT[:, kc, j * 128:(j + 1) * 128], wv3[:, kc, :], kc == 0, kc == 7, [wv, hT], [pb])
            cp("act" if j % 2 else "dve", vaug[:, 4 * g + j, :, 0:128], pb[:].rearrange("p (a b) -> p a b", a=4), [pb], [vaug])
        ws_release()

        if slab_i == 0:
            RS = RegionAlloc(40 * 1024)
            rb = RS([33, 4], F32)
            oh = RS([33, 383], F32)
            u4 = RS([4, 383], F32)
            hk = RS([128, 8, 128], F32)
            dma("sp", rb[:], rbaug_d[:, :], [rbaug_d], [rb])
            dma("sp", oh[:], ohu_d[:, :], [ohu_d], [oh])
            mm(ps[0][0:4, 0:383], rb[:], oh[:], True, True, [rb, oh], [ps[0]])
            cp("dve", u4[:], ps[0][0:4, 0:383], [ps[0]], [u4])
            dma("sp", scr_d[:, :], u4[:], [u4], [scr_d])
            for h in range(4):
                for blk in range(2):
                    src = bass.AP(tensor=scr_d.t.tensor, offset=383 * h + 128 * blk, ap=[[1, 128], [1, 128]])
                    dma("sp", hk[:, 2 * h + blk, :], src, [scr_d], [hk])
            for i in range(8):
                mm(ps[1 + i // 4][:, (i % 4) * 128:(i % 4 + 1) * 128], jm[:], hk[:, i, :], True, True, [jm, hk], [ps[1 + i // 4]])
            for i in range(2):
                cp("dve", mb[:, 4 * i:4 * i + 4, :], ps[1 + i][:].rearrange("p (a b) -> p a b", a=4), [ps[1 + i]], [mb])

        yaT_T = T(init_reads=P.snapshot(), t0=P.now())
        ybT_T = T(init_reads=P.snapshot(), t0=P.now())

        def gen_C():
            pt = [RC([128, TOK], BF16) for _ in range(3)]
            osb = RC([128, 2, 4, 129], F32)
            rz = RC([128, 2, 4, 1], F32)
            dd = [RC([128, 128], F32) for _ in range(2)]
            ssq = [RC([128, 1], F32) for _ in range(2)]
            sqj = RC([128, 128], F32)
            assert RC.o <= CREG
            nkt = 4 * g + 4
            it = 0
            epi = []
            for h in range(4):
                for c in range(2):
                    lo, hi = 64 * c, 64 * c + 64
                    for ob in (ps[2], ps[3]):
                        mm(ob[:, 0:258], zb[:, 0:128], zb[:, 0:258], True, False, [zb], [ob])
                    def emit_st(kt):
                        j0 = max(0, kt - 4 * g)
                        stb = ps[kt % 2]
                        need_diag = kt >= 4 * g
                        need_off = (kt >= 4 * g and j0 + 1 <= 3) or (kt == 4 * g - 1)
                        mm(stb[:, j0 * 128:512], kT[lo:hi, h, kt * 128:(kt + 1) * 128], qT[lo:hi, h, j0 * 128:512],
                           True, not (need_diag or need_off), [kT, qT], [stb])
                        if need_diag:
                            mm(stb[:, j0 * 128:(j0 + 1) * 128], identb[:], mb[:, 2 * h, :], False, not (j0 + 1 <= 3),
                               [identb, mb], [stb])
                            if j0 + 1 <= 3:
                                mm(stb[:, (j0 + 1) * 128:(j0 + 2) * 128], identb[:], mb[:, 2 * h + 1, :], False, True,
                                   [identb, mb], [stb])
                        elif kt == 4 * g - 1:
                            mm(stb[:, 0:128], identb[:], mb[:, 2 * h + 1, :], False, True, [identb, mb], [stb])

                    emit_st(0)
                    for kt in range(nkt):
                        j0 = max(0, kt - 4 * g)
                        stb = ps[kt % 2]
                        ptb = pt[it % 3]
                        it += 1
                        act(ptb[:, j0 * 128:512], stb[:, j0 * 128:512], AF.Exp, [stb, csm], [ptb],
                            bias=csm[:, 384 + h:385 + h], scale=0.125)
                        if kt + 1 < nkt:
                            emit_st(kt + 1)
                        for j in range(j0, 4):
                            ob = ps[2 + j // 2]
                            oc = (j % 2) * 129
                            last = (j % 2 == 1) and (kt == 4 * g + j)
                            mm(ob[:, oc:oc + 129], ptb[:, j * 128:(j + 1) * 128], vaug[:, kt, h, :], False, last,
                               [ptb, vaug], [ob])
                        if epi:
                            epi.pop(0)(ps[kt % 2])
                        yield
                    for j in range(4):
                        ob = ps[2 + j // 2]
                        oc = (j % 2) * 129
                        cp("dve", osb[:, c, j, :], ob[:, oc:oc + 129], [ob], [osb])
                    yield
                assert not epi

                def mk_chunk(h, j):
                    def chunk(pb):
                        if j == 0:
                            P.op("dve", lambda e: e.reciprocal(out=rz[:], in_=osb[:, :, :, 128:129]), [osb], [rz])
                            ts("dve", rz[:, 1, :, :], rz[:, 1, :, :], neglam[:], None, ALU.mult, None, [rz, neglam], [rz])
                        d_ = dd[j % 2]
                        sq_ = ssq[j % 2]
                        ts("dve", d_[:], osb[:, 0, j, 0:128], rz[:, 0, j, :], None, ALU.mult, None, [osb, rz], [d_])
                        stt(d_[:], osb[:, 1, j, 0:128], rz[:, 1, j, :], d_[:], ALU.mult, ALU.add, [osb, rz, d_], [d_])
                        P.op("act", lambda e: e.activation(out=sqj[:], in_=d_[:], func=AF.Square, accum_out=sq_[:]),
                             [d_], [sqj, sq_])
                        act(sq_[:], sq_[:], AF.Ln, [sq_, eps5], [sq_], bias=eps5[:], scale=1.0 / 128.0)
                        act(sq_[:], sq_[:], AF.Exp, [sq_], [sq_], scale=-0.5)
                        stt(d_[:], d_[:], sq_[:], csm[:, 0:128], ALU.mult, ALU.mult, [d_, sq_, csm], [d_])
                        trp(pb[:, 0:128], d_[:], identf[:], [d_, identf], [pb])
                        cp("act", yaT_ap[:, h, j * 128:(j + 1) * 128], pb[:, 0:128], [pb], [yaT_T])
                    return chunk
                epi.extend(mk_chunk(h, j) for j in range(4))
            while epi:
                epi.pop(0)(ps[len(epi) % 2])
                yield

        RD = RegionAlloc(CREG)
        twb = RD([128, TOK], BF16)
        sgl = RD([128, TOK], BF16)

        def mk_ctx(t):
            X = {}
            X["raw"] = RD([128, 3, 513], F32)
            X["f_"] = [RD([128, TOK], F32) for i in range(6)]
            X["Vb"] = RD([128, TOK], BF16)
            X["Mm"] = RD([128, NCH, 64], BF16)
            X["Nn"] = RD([128, NCH, 64], BF16)
            X["Pm"] = RD([128, NCH, 64], BF16)
            X["Xa1"] = RD([128, NCH, 64], BF16)
            X["XTa1"] = RD([128, NCH, 64], BF16)
            X["gst"] = RD([128, 4, NCH], F32)
            X["rhsb"] = [RD([128, 64], BF16) for _ in range(2)]
            X["ub"] = [RD([128, 64], BF16) for _ in range(2)]
            X["s0w"] = RD([128, 64], F32)
            X["set"] = dict(
                At=RD([128, TOK], BF16), Bt=RD([128, TOK], BF16), Kt=RD([128, TOK], BF16), Rt=RD([128, TOK], BF16),
                Btok=RD([128, NCH, 64], BF16), Ktok=RD([128, NCH, 64], BF16), Vtok=RD([128, NCH, 64], BF16),
                gT=RD([128, TOK], F32), bonT=RD([128, TOK], F32), WC=RD([128, NCH], F32),
                Pf=RD([128, NCH, 64], BF16), AKT=RD([128, NCH, 64], BF16), ARBT=RD([128, NCH, 64], BF16),
                ARKT=RD([128, NCH, 64], BF16), ytok=RD([128, NCH, 64], F32))
            X["DB"] = (ps[4 + 2 * t], ps[5 + 2 * t])
            X["DBb"] = (psb[4 + 2 * t], psb[5 + 2 * t])
            return X
        ctx = [mk_ctx(0), mk_ctx(1)]
        lraw = ctx[0]["raw"]

        def lerp_chunk(pb, dst3, idx, pcidx, dtmp):
            cp("act", dst3[:, idx, 1:513], pb[:], [pb], [dst3])
            if g == 0:
                P.op("dve", lambda e: e.memset(dst3[:, idx, 0:1], 0.0), [], [dst3])
            else:
                cp("dve", dst3[:, idx, 0:1], prevcol[:, pcidx:pcidx + 1], [prevcol], [dst3])
            cp("dve", prevcol[:, pcidx:pcidx + 1], dst3[:, idx, 512:513], [dst3], [prevcol])
            tt("dve", dtmp[:], dst3[:, idx, 0:512], dst3[:, idx, 1:513], ALU.subtract, [dst3], [dtmp])
            stt(dst3[:, idx, 1:513], dtmp[:], pc[:, PC_MU + pcidx:PC_MU + pcidx + 1], dst3[:, idx, 1:513],
                ALU.mult, ALU.add, [dtmp, pc, dst3], [dst3])

        def gen_D_lora():
            wl_ = ws_get()
            wl3 = wl_[:, 0:2048].rearrange("p (c n) -> p c n", c=8)
            for i in range(2):
                pb = ctx[0]["DB"][i]
                for kc in range(8):
                    mm(pb[:], wl3[:, kc, i * 128:(i + 1) * 128], hT[:, kc, :], kc == 0, kc == 7, [wl_, hT], [pb])
                lerp_chunk(pb, lraw, i, 12 + i, ctx[0]["f_"][4])
                yield
            ws_release()
            cp("dve", twb[64:128, :], lraw[64:128, 0, 1:513], [lraw], [twb])
            act(lraw[0:64, 0, 1:513], lraw[0:64, 0, 1:513], AF.Exp, [lraw], [lraw], scale=-2.0)
            act(lraw[0:64, 0, 1:513], lraw[0:64, 0, 1:513], AF.Ln, [lraw, one1], [lraw], bias=one1[0:64, :], scale=1.0)
            act(lraw[0:64, 0, 1:513], lraw[0:64, 0, 1:513], AF.Exp, [lraw], [lraw], scale=-1.0)
            ts("dve", twb[0:64, :], lraw[0:64, 0, 1:513], 2.0, -1.0, ALU.mult, ALU.add, [lraw], [twb])
            act(lraw[:, 1, 1:513], lraw[:, 1, 1:513], AF.Exp, [lraw], [lraw], scale=-1.0)
            act(lraw[:, 1, 1:513], lraw[:, 1, 1:513], AF.Ln, [lraw, one1], [lraw], bias=one1[:], scale=1.0)
            act(sgl[:], lraw[:, 1, 1:513], AF.Exp, [lraw], [sgl], scale=-1.0)
            yield

        fl = lambda bf_: bf_[:].rearrange("p n t -> p (n t)")

        def gen_P1(c, X):
            st_ = X["set"]
            raw, f_, Vb, Mm, Nn, Pm = X["raw"], X["f_"], X["Vb"], X["Mm"], X["Nn"], X["Pm"]
            Xa = [Mm, X["Xa1"]]
            XTa = [Nn, X["XTa1"]]
            DB, DBb = X["DB"], X["DBb"]
            dtmp = f_[4]
            u_exp = slab_i * NU + 4 + c
            assert ws["load"] > u_exp
            At, Bt, Kt, Rt = st_["At"], st_["Bt"], st_["Kt"], st_["Rt"]
            Btok, Ktok, Vtok = st_["Btok"], st_["Ktok"], st_["Vtok"]
            gT, bonT, WC = st_["gT"], st_["bonT"], st_["WC"]
            Pf, AKT, ARBT, ARKT = st_["Pf"], st_["AKT"], st_["ARBT"], st_["ARKT"]
            wr = ring[u_exp % NRING]
            wr4 = wr[:, 0:3072].rearrange("p (i c n) -> p i c n", i=3, c=8)
            for i in range(3):
                pb = DB[i % 2]
                for kc in range(8):
                    mm(pb[:], wr4[:, i, kc, :], hT[:, kc, :], kc == 0, kc == 7, [wr, hT], [pb])
                lerp_chunk(pb, raw, i, 4 * i + c, dtmp)
                yield
            ws_release_unit(u_exp)
            r_ = raw[:, 0, 1:513]
            k_ = raw[:, 1, 1:513]
            v_ = raw[:, 2, 1:513]
            mm(DB[0][:], w2a2b[0:64, c * 128:(c + 1) * 128], twb[0:64, :], True, True, [w2a2b, twb], [DB[0]])
            mm(DB[1][:], w2a2b[64:128, c * 128:(c + 1) * 128], twb[64:128, :], True, True, [w2a2b, twb], [DB[1]])
            sigw, cl, epos, eneg, eprv, a_ = f_
            act(sigw[:], DB[0][:], AF.Exp, [DB[0], npc], [sigw], bias=npc[:, PC_W0 + c:PC_W0 + c + 1], scale=-1.0)
            act(a_[:], DB[1][:], AF.Exp, [DB[1], npc], [a_], bias=npc[:, PC_A0 + c:PC_A0 + c + 1], scale=-1.0)
            act(sigw[:], sigw[:], AF.Ln, [sigw, one1], [sigw], bias=one1[:], scale=1.0)
            act(a_[:], a_[:], AF.Ln, [a_, one1], [a_], bias=one1[:], scale=1.0)
            act(sigw[:], sigw[:], AF.Exp, [sigw, mhalf], [sigw], bias=mhalf[:], scale=-1.0)
            act(a_[:], a_[:], AF.Exp, [a_], [a_], scale=-1.0)
            mm(DB[0][:], g2b[:, c * 128:(c + 1) * 128], sgl[:], True, True, [g2b, sgl], [DB[0]])
            cp("act", gT[:], DB[0][:], [DB[0]], [gT])
            yield
            P.op("dve", lambda e: e.tensor_tensor_scan(out=cl[:], data0=cmask[:], data1=sigw[:], initial=0.0,
                                                      op0=ALU.mult, op1=ALU.add), [cmask, sigw], [cl])
            act(epos[:], cl[:], AF.Exp, [cl], [epos], scale=-1.0)
            act(eneg[:], cl[:], AF.Exp, [cl], [eneg])
            tt("dve", eprv[:], cl[:], sigw[:], ALU.subtract, [cl, sigw], [eprv])
            act(eprv[:], eprv[:], AF.Exp, [eprv], [eprv], scale=-1.0)
            cp("dve", WC[:], epos[:].rearrange("p (n t) -> p n t", t=64)[:, :, 63], [epos], [WC])
            yield
            kkn, tmp = cl, sigw
            ts("dve", kkn[:], k_, pc[:, PC_KK + c:PC_KK + c + 1], None, ALU.mult, None, [raw, pc], [kkn])
            tt("dve", tmp[:], kkn[:], kkn[:], ALU.mult, [kkn], [tmp])
            mm(DB[0][:], bonesf[:], tmp[:], True, True, [bonesf, tmp], [DB[0]])
            ts("dve", tmp[:], DB[0][:], 1e-24, None, ALU.max, None, [DB[0]], [tmp])
            act(tmp[:], tmp[:], AF.Ln, [tmp], [tmp], scale=float(2.0 ** 40))
            act(tmp[:], tmp[:], AF.Exp, [tmp, ln2x20], [tmp], bias=ln2x20[:], scale=-0.5)
            tt("dve", kkn[:], kkn[:], tmp[:], ALU.mult, [kkn, tmp], [kkn])
            yield
            stt(At[:], kkn[:], -1.0, eprv[:], ALU.mult, ALU.mult, [kkn, eprv], [At])
            tt("dve", tmp[:], kkn[:], a_[:], ALU.mult, [kkn, a_], [tmp])
            tt("dve", Bt[:], tmp[:], eneg[:], ALU.mult, [tmp, eneg], [Bt])
            ts("dve", a_[:], a_[:], -1.0, pc[:, PC_KA + c:PC_KA + c + 1], ALU.add, ALU.mult, [a_, pc], [a_])
            stt(a_[:], a_[:], 1.0, k_, ALU.add, ALU.mult, [a_, raw], [a_])
            tt("dve", Kt[:], a_[:], eneg[:], ALU.mult, [a_, eneg], [Kt])
            tt("dve", Rt[:], r_, epos[:], ALU.mult, [raw, epos], [Rt])
            cp("act", Vb[:], v_, [raw], [Vb])
            yield
            stt(tmp[:], r_, pc[:, PC_RK + c:PC_RK + c + 1], a_[:], ALU.mult, ALU.mult, [raw, pc, a_], [tmp])
            mm(DB[1][:], bonesf[:], tmp[:], True, True, [bonesf, tmp], [DB[1]])
            tt("dve", bonT[:], DB[1][:], v_, ALU.mult, [DB[1], raw], [bonT])
            yield
            for ti, (srcb, dstb) in enumerate(((Bt, Btok), (Kt, Ktok), (Vb, Vtok))):
                pbk = DB[ti % 2]
                pv = DBb[ti % 2]
                for n in range(NCH):
                    for hh in range(2):
                        trp(pv[64 * hh:64 * hh + 64, n * 64:(n + 1) * 64], srcb[64 * hh:64 * hh + 64, n * 64:(n + 1) * 64],
                            identb[64 * hh:64 * hh + 64, 64 * hh:64 * hh + 64], [srcb, identb], [pbk])
                cp("act" if ti == 1 else "dve", dstb[:].rearrange("p n t -> p (n t)"), pv[:, 0:512], [pbk], [dstb])
                yield
            prods = ((Bt, At, Mm, 0), (At, Bt, Nn, 1), (Kt, At, AKT, 0), (Bt, Rt, ARBT, 2), (Kt, Rt, ARKT, 2))
            for pi, (la, rb_, dst, mk) in enumerate(prods):
                bank = DB[pi % 2]
                for n in range(NCH):
                    for hh in range(2):
                        sl = slice(64 * hh, 64 * hh + 64)
                        mm(bank[sl, n * 64:(n + 1) * 64], la[sl, n * 64:(n + 1) * 64], rb_[sl, n * 64:(n + 1) * 64],
                           True, True, [la, rb_], [bank])
                tt("dve", fl(dst), bank[:], masks[:, mk, :], ALU.mult, [bank, masks], [dst])
                yield
            tt("dve", fl(Pm), fl(Mm), masks[:, 3, :], ALU.add, [Mm, masks], [Pm])
            Xc, XTc, Pc = Mm, Nn, Pm
            for lev in range(1, 6):
                lastlev = (lev == 5)
                Xn, XTn = Xa[lev % 2], XTa[lev % 2]
                Pn = Pf if lastlev else Pm
                for n in range(NCH):
                    for hh in range(2):
                        sl = slice(64 * hh, 64 * hh + 64)
                        cs = slice(n * 64, (n + 1) * 64)
                        mm(DB[0][sl, cs], Xc[sl, n, :], XTc[sl, n, :], True, True, [Xc, XTc], [DB[0]])
                        if not lastlev:
                            mm(DB[1][sl, cs], XTc[sl, n, :], Xc[sl, n, :], True, True, [Xc, XTc], [DB[1]])
                cp("act", fl(XTn), DB[0][:], [DB[0]], [XTn])
                if not lastlev:
                    cp("dve", fl(Xn), DB[1][:], [DB[1]], [Xn])
                yield
                for n in range(NCH):
                    for hh in range(2):
                        sl = slice(64 * hh, 64 * hh + 64)
                        cs = slice(n * 64, (n + 1) * 64)
                        mm(DB[0][sl, cs], XTn[sl, n, :], Pc[sl, n, :], True, True, [XTn, Pc], [DB[0]])
                tt("dve", fl(Pn), DB[0][:], fl(Pc), ALU.add, [DB[0], Pc], [Pn])
                yield
                Xc, XTc, Pc = Xn, XTn, Pn

        def gen_P2(c, X):
            st_ = X["set"]
            gst, rhsb, ub, s0w = X["gst"], X["rhsb"], X["ub"], X["s0w"]
            ysq = Buf(X["f_"][1][:].rearrange("p (n t) -> p n t", t=64), X["f_"][1].T)
            DB = X["DB"]
            At, Rt = st_["At"], st_["Rt"]
            Btok, Ktok, Vtok = st_["Btok"], st_["Ktok"], st_["Vtok"]
            gT, bonT, WC = st_["gT"], st_["bonT"], st_["WC"]
            Pf, AKT, ARBT, ARKT, ytok = st_["Pf"], st_["AKT"], st_["ARBT"], st_["ARKT"], st_["ytok"]
            if g == 0:
                P.op("dve", lambda e: e.memset(S32[:, c, :], 0.0), [], [S32T[c]])
                P.op("dve", lambda e: e.memset(Sbf[:, c, :], 0.0), [], [SbfT[c]])
            for n in range(NCH):
                cs = slice(n * 64, (n + 1) * 64)
                tb = DB[n % 2]
                pr, pu, pst, py = tb[:, 0:64], tb[:, 64:128], tb[:, 128:192], tb[:, 192:256]
                rb2, ub2 = rhsb[n % 2], ub[n % 2]
                for hh in range(2):
                    sl = slice(64 * hh, 64 * hh + 64)
                    mm(pr[sl, :], At[sl, cs], Sbf[sl, c, :], True, False, [At, SbfT[c]], [tb])
                    mm(pr[sl, :], AKT[sl, n, :], Vtok[sl, n, :], False, True, [AKT, Vtok], [tb])
                cp("act", rb2[:], pr, [tb], [rb2])
                ts("dve", s0w[:], S32[:, c, :], WC[:, n:n + 1], None, ALU.mult, None, [S32T[c], WC], [s0w])
                yield
                for hh in range(2):
                    sl = slice(64 * hh, 64 * hh + 64)
                    mm(pu[sl, :], Pf[sl, n, :], rb2[sl, :], True, True, [Pf, rb2], [tb])
                cp("act", ub2[:], pu, [tb], [ub2])
                yield
                for hh in range(2):
                    sl = slice(64 * hh, 64 * hh + 64)
                    mm(pst[sl, :], Btok[sl, n, :], ub2[sl, :], True, False, [Btok, ub2], [tb])
                    mm(pst[sl, :], Ktok[sl, n, :], Vtok[sl, n, :], False, True, [Ktok, Vtok], [tb])
                for hh in range(2):
                    sl = slice(64 * hh, 64 * hh + 64)
                    mm(py[sl, :], Rt[sl, cs], Sbf[sl, c, :], True, False, [Rt, SbfT[c]], [tb])
                    mm(py[sl, :], ARBT[sl, n, :], ub2[sl, :], False, False, [ARBT, ub2], [tb])
                    mm(py[sl, :], ARKT[sl, n, :], Vtok[sl, n, :], False, True, [ARKT, Vtok], [tb])
                stt(Sbf[:, c, :], pst, WC[:, n:n + 1], s0w[:], ALU.mult, ALU.add, [tb, WC, s0w], [SbfT[c]])
                stt(S32[:, c, :], pst, WC[:, n:n + 1], s0w[:], ALU.mult, ALU.add, [tb, WC, s0w], [S32T[c]])
                cp("act", ytok[:, n, :], py, [tb], [ytok])
                yield
            P.op("dve", lambda e: e.tensor_reduce(out=gst[:, 0, :], in_=ytok[:], axis=AX.X, op=ALU.add), [ytok], [gst])
            tt("dve", ysq[:], ytok[:], ytok[:], ALU.mult, [ytok], [ysq])
            P.op("dve", lambda e: e.tensor_reduce(out=gst[:, 1, :], in_=ysq[:], axis=AX.X, op=ALU.add), [ysq], [gst])
            ts("dve", gst[:, 2, :], gst[:, 0, :], 1.0 / 64.0, None, ALU.mult, None, [gst], [gst])
            tt("dve", gst[:, 0, :], gst[:, 2, :], gst[:, 2, :], ALU.mult, [gst], [gst])
            stt(gst[:, 3, :], gst[:, 1, :], 1.0 / 64.0, gst[:, 0, :], ALU.mult, ALU.subtract, [gst], [gst])
            act(gst[:, 3, :], gst[:, 3, :], AF.Ln, [gst, epsx], [gst], bias=epsx[:], scale=1.0)
            act(gst[:, 3, :], gst[:, 3, :], AF.Exp, [gst], [gst], scale=-0.5)
            yield
            tt("dve", ysq[:], ytok[:], bc_last(gst[:, 2, :], 64), ALU.subtract, [ytok, gst], [ysq])
            tt("dve", ysq[:], ysq[:], bc_last(gst[:, 3, :], 64), ALU.mult, [ysq, gst], [ysq])
            for n in range(NCH):
                for hh in range(2):
                    sl = slice(64 * hh, 64 * hh + 64)
                    mm(DB[0][sl, n * 64:(n + 1) * 64], ysq[sl, n, :], identf[sl, 64 * hh:64 * hh + 64], True, True,
                       [ysq, identf], [DB[0]])
            etmp = ysq[:].rearrange("p n t -> p (n t)")
            act(etmp, DB[0][:], AF.Identity, [DB[0], pc], [ysq], bias=pc[:, PC_LXB + c:PC_LXB + c + 1],
                scale=pc[:, PC_LXG + c:PC_LXG + c + 1])
            tt("dve", etmp, etmp, bonT[:], ALU.add, [ysq, bonT], [ysq])
            tt("dve", ybT_ap[:, c, :], etmp, gT[:], ALU.mult, [ysq, gT], [ybT_T])
            yield

        def d_thread(t):
            for c in (t, t + 2):
                for _ in gen_P1(c, ctx[t]):
                    yield
                for _ in gen_P2(c, ctx[t]):
                    yield

        gC = gen_C()
        _drain(_sched(P, [gC, gen_D_lora()], stop_when=1))
        _drain(_sched(P, [gC, d_thread(0), d_thread(1)], bias=[(32.0 if g == 3 else (20.0 if g == 2 else (8.0 if g == 1 else 0.0))), 0.0, 0.0]))
        if dbg_here:
            dump("yaT", Buf(yaT_ap, yaT_T), [128, 4, TOK], BF16)
            dump("ybT", Buf(ybT_ap, ybT_T), [128, 4, TOK], BF16)

        RE = RegionAlloc(0)
        mT = RE([128, 8, TOK], BF16)
        sga = [RE([128, TOK], F32) for _ in range(2)]
        sgb = [RE([128, TOK], F32) for _ in range(2)]
        n1 = [RE([128, D], F32) for _ in range(2)]
        lnbuf = RE([128, 2048], F32)
        dma("sp", lnbuf[:], pbc_d[:, PB_LN1:PB_LN1 + 2048], [pbc_d], [lnbuf])
        for f in range(8):
            wf = ws_get()
            ga3 = wf[:, 0:1024].rearrange("p (c n) -> p c n", c=8)
            gb3 = wf[:, 1024:2048].rearrange("p (c n) -> p c n", c=8)
            ua3 = wf[:, 2048:2560].rearrange("p (c n) -> p c n", c=4)
            ub3 = wf[:, 2560:3072].rearrange("p (c n) -> p c n", c=4)
            o4 = 4 * (f % 2)
            pga, pgb, pua, pub = ps[o4], ps[o4 + 1], ps[o4 + 2], ps[o4 + 3]
            wr_ = [pub]
            for kc in range(8):
                mm(pga[:], ga3[:, kc, :], hT[:, kc, :], kc == 0, kc == 7, [wf, hT], [pga])
            for kc in range(8):
                mm(pgb[:], gb3[:, kc, :], hT[:, kc, :], kc == 0, kc == 7, [wf, hT], [pgb])
            for kc in range(4):
                mm(pua[:], ua3[:, kc, :], yaT_ap[:, kc, :], kc == 0, kc == 3, [wf, yaT_T], [pua])
            for kc in range(4):
                mm(pub[:], ub3[:, kc, :], ybT_ap[:, kc, :], kc == 0, kc == 3, [wf, ybT_T], wr_)
            ws_release()
            sa, sb_ = sga[f % 2], sgb[f % 2]
            act(sa[:], pga[:], AF.Sigmoid, [pga], [sa])
            act(sb_[:], pgb[:], AF.Sigmoid, [pgb], [sb_])
            tt("dve", sa[:], sa[:], pua[:], ALU.mult, [sa, pua], [sa])
            tt("dve", sb_[:], sb_[:], pub[:], ALU.mult, [sb_] + wr_, [sb_])
            tt("dve", mT[:, f, :], sa[:], sb_[:], ALU.add, [sa, sb_], [mT])
        if dbg_here:
            dump("mT", mT, [128, 8, TOK], BF16)
        for half in range(2):
            wo = ws_get()
            wo3 = wo[:, 0:4096].rearrange("p (c n) -> p c n", c=8)
            for j in range(4):
                pb = ps[(2 * half + j) % 4]
                for f in range(8):
                    mm(pb[:], mT[:, f, j * 128:(j + 1) * 128], wo3[:, f, :], f == 0, f == 7, [wo, mT], [pb])
                stt(hres_t[:, j, half * 512:(half + 1) * 512], hres_t[:, j, half * 512:(half + 1) * 512], ALPHA, pb[:],
                    ALU.mult, ALU.add, [hresT[j], pb], [hresT[j]])
            ws_release()
        h1T_T = T(init_reads=P.snapshot(), t0=P.now())
        for j in range(4):
            nb = n1[j % 2]
            mv, rs = layer_norm_stats(hres_t[:, j, :], hresT[j], j % 2)
            ts("dve", nb[:], hres_t[:, j, :], mv[:, 0:1], rs[:], ALU.subtract, ALU.mult, [hresT[j], mv, rs], [nb])
            tt("pool", hres_t[:, j, :], nb[:], lnbuf[:, 0:1024], ALU.mult, [nb, lnbuf], [hresT[j]])
            tt("pool", hres_t[:, j, :], hres_t[:, j, :], lnbuf[:, 1024:2048], ALU.add, [hresT[j], lnbuf], [hresT[j]])
            transposes_to_featmajor(nb, nb, yab, h1T_T, j, PC_LN1_G, PC_LN1_B, (ps[4], ps[5]))

        RF = RegionAlloc(0)
        actT = RF([128, NHC, TOK], BF16)
        sil = [RF([128, TOK], F32) for _ in range(2)]
        n2 = [RF([128, D], F32) for _ in range(2)]
        lnbuf = RF([128, 2048], F32)
        dma("sp", lnbuf[:], pbc_d[:, PB_LN2:PB_LN2 + 2048], [pbc_d], [lnbuf])
        for u in range(11):
            wf = ws_get()
            w4 = wf[:, 0:4096].rearrange("p (a c n) -> p a c n", a=2, c=8)
            for q in range(2):
                hc = 2 * u + q
                pg, pu_ = ps[2 * (hc % 3)], ps[2 * (hc % 3) + 1]
                for kc in range(8):
                    mm(pg[:], w4[:, 0, kc, q * 128:(q + 1) * 128], yab[:, kc, :], kc == 0, kc == 7, [wf, h1T_T], [pg])
                for kc in range(8):
                    mm(pu_[:], w4[:, 1, kc, q * 128:(q + 1) * 128], yab[:, kc, :], kc == 0, kc == 7, [wf, h1T_T], [pu_])
                sl_ = sil[hc % 2]
                act(sl_[:], pg[:], AF.Silu, [pg], [sl_])
                tt("dve", actT[:, hc, :], sl_[:], pu_[:], ALU.mult, [sl_, pu_], [actT])
            ws_release()
        for u in range(8):
            wdn = ws_get()
            wd3 = wdn[:, 0:NHC * 128].rearrange("p (c n) -> p c n", c=NHC)
            pb = ps[u % 2]
            for j in range(4):
                for hc in range(NHC):
                    mm(pb[:, j * 128:(j + 1) * 128], actT[:, hc, j * 128:(j + 1) * 128], wd3[:, hc, :], hc == 0, hc == NHC - 1,
                       [wdn, actT], [pb])
            ws_release()
            for j in range(4):
                stt(hres_t[:, j, u * 128:(u + 1) * 128], hres_t[:, j, u * 128:(u + 1) * 128], ALPHA, pb[:, j * 128:(j + 1) * 128],
                    ALU.mult, ALU.add, [hresT[j], pb], [hresT[j]])
        for j in range(4):
            nb = n2[j % 2]
            mv, rs = layer_norm_stats(hres_t[:, j, :], hresT[j], j % 2)
            ts("dve", nb[:], hres_t[:, j, :], mv[:, 0:1], rs[:], ALU.subtract, ALU.mult, [hresT[j], mv, rs], [nb])
            tt("pool", nb[:], nb[:], lnbuf[:, 0:1024], ALU.mult, [nb, lnbuf], [nb])
            tt("pool", nb[:], nb[:], lnbuf[:, 1024:2048], ALU.add, [nb, lnbuf], [nb])
            dma("sp", y_d[b, t0 + j * 128:t0 + (j + 1) * 128, :], nb[:], [nb], [y_d])

    P.final_wait("sp", [y_d] + list(dbg_outs.values()))
    P.emit()
    return nc, dbg_outs


def _t5_bucket_np(dist):
    d = np.maximum(dist, 1).astype(np.float32)
    large = 16 + (np.log(d / np.float32(16)) / np.float32(math.log(128 / 16)) * np.float32(16)).astype(np.int32)
    large = np.minimum(large, 31)
    return np.where(dist < 16, dist, large)


def _host_constants():
    ohu = np.zeros((33, 383), np.float32)
    for i in range(383):
        if i < 127:
            ohu[32, i] = MASKVAL
        else:
            bkt = int(_t5_bucket_np(np.array([i - 127], np.int32))[0])
            ohu[bkt, i] += 8.0
            ohu[31, i] -= 8.0
    p = np.arange(128)[:, None] % 64
    t = np.arange(64)[None, :]
    m = np.stack([(p < t), (t < p), (p <= t), (p == t)], 0).astype(np.float32)
    masks = np.ascontiguousarray(np.broadcast_to(m[:, :, None, :], (4, 128, 8, 64)).transpose(1, 0, 2, 3).reshape(128, 4, 512))
    ident = np.eye(128, dtype=np.float32)
    jmat = np.ascontiguousarray(ident[::-1])
    bones = np.zeros((128, 128), np.float32)
    bones[:64, :64] = 1.0
    bones[64:, 64:] = 1.0
    cm = np.ones((128, 512), np.float32)
    cm[:, ::64] = 0.0
    return dict(ohu=ohu, masks=masks, ident=ident, jmat=jmat, bones=bones, cmask=cm)


def _prep_inputs(inp):
    f = lambda a: np.ascontiguousarray(np.asarray(a, dtype=np.float32))
    row = lambda a: np.asarray(a, np.float32).reshape(-1)
    pbc_row = np.concatenate([row(inp["ln_in_g"]), row(inp["ln_in_b"]), row(inp["ln1_g"]), row(inp["ln1_b"]),
                              row(inp["ln2_g"]), row(inp["ln2_b"]), row(inp["diff_subln_g"]),
                              row(inp["diff_lam_q1"]), row(inp["diff_lam_k1"]), row(inp["diff_lam_q2"]),
                              row(inp["diff_lam_k2"]), row(np.asarray(inp["rel_bias"])[31])])
    assert pbc_row.shape[0] == PB_N
    pbc = np.ascontiguousarray(np.broadcast_to(pbc_row[None, :], (128, PB_N)))
    col = lambda a: row(a).reshape(-1, 128).T
    pcol = np.ascontiguousarray(np.concatenate(
        [col(inp["ln_in_g"]), col(inp["ln_in_b"]), col(inp["ln1_g"]), col(inp["ln1_b"]), col(inp["rwkv_mu"]),
         col(inp["rwkv_w0"]), col(inp["rwkv_a0"]), col(inp["rwkv_k_k"]), col(inp["rwkv_k_a"]), col(inp["rwkv_r_k"]),
         col(inp["rwkv_lnx_g"]), col(inp["rwkv_lnx_b"])], axis=1))
    assert pcol.shape == (128, PC_N)
    w2a2 = np.ascontiguousarray(np.concatenate([np.asarray(inp["rwkv_w2"], np.float32)[0],
                                                np.asarray(inp["rwkv_a2"], np.float32)[0]], 0))
    rbaug = np.ascontiguousarray(np.concatenate([np.asarray(inp["rel_bias"], np.float32), np.ones((1, 4), np.float32)], 0))
    shared = dict(w_in=f(inp["w_in"][0]), w_up_a=f(inp["w_up_a"][0]), w_up_b=f(inp["w_up_b"][0]), w_out=f(inp["w_out"][0]),
                  wg=f(inp["ffn_w_gate"][0]), wu=f(inp["ffn_w_up"][0]), wd=f(inp["ffn_w_down"][0]),
                  pbc=pbc, pcol=pcol, w2a2=w2a2, g2=f(inp["rwkv_g2"][0]), rbaug=rbaug)
    shared.update(_host_constants())
    return shared


_CACHE = {}


def kernel(**inputs):
    x = np.asarray(inputs["x"], np.float32)
    shared = _prep_inputs(inputs)
    if "nc" not in _CACHE:
        _CACHE["nc"] = build_program()[0]
    nc = _CACHE["nc"]
    in_maps = []
    for c in range(NCORES):
        m = dict(shared)
        m["x"] = np.ascontiguousarray(x[BL * c:BL * (c + 1)])
        in_maps.append(m)
    res = run_bass_kernel_spmd(nc, in_maps, core_ids=list(range(NCORES)))
    out = np.concatenate([np.asarray(r["y"], np.float32) for r in res.results], axis=0)
    return out
```

```python
import math
import numpy as np
import concourse.bass as bass
import concourse.mybir as mybir
from concourse.bass_utils import run_bass_kernel_spmd

F32 = mybir.dt.float32
BF16 = mybir.dt.bfloat16
ALU = mybir.AluOpType
AF = mybir.ActivationFunctionType
AX = mybir.AxisListType

NCORES = 8
D = 1024
SEQ = 2048
BL = 2
TOK = 512
NSLAB = SEQ // TOK
ALPHA = 2.0 ** 0.25
LAMBDA_INIT = 0.2
FFN = 2816
NHC = FFN // 128
C = 64
NCH = TOK // C
MASKVAL = -1.0e9
EXPM05 = math.exp(-0.5)

PB_LNIN, PB_LN1, PB_LN2, PB_SUB, PB_LAM, PB_RB31, PB_N = 0, 2048, 4096, 6144, 6272, 6528, 6532
PC_LNIN_G, PC_LNIN_B, PC_LN1_G, PC_LN1_B, PC_MU, PC_W0, PC_A0, PC_KK, PC_KA, PC_RK, PC_LXG, PC_LXB, PC_N = \
    0, 8, 16, 24, 32, 46, 50, 54, 58, 62, 66, 70, 74


class T:
    __slots__ = ("w", "r", "excl", "tw", "tr")

    def __init__(self, init_reads=None, excl=False, t0=0.0):
        self.w = None
        self.r = dict(init_reads) if init_reads else {}
        self.excl = excl
        self.tw = 0.0
        self.tr = t0


class Buf:
    def __init__(self, t, tr=None):
        self.t = t
        self.T = tr if tr is not None else T()

    def __getitem__(self, idx):
        return self.t[idx]


def _T(x):
    return x.T if isinstance(x, Buf) else x


class Prog:
    ENG = ("pe", "dve", "act", "pool", "sp")

    def __init__(self, nc, n_dma_sems=10):
        self.nc = nc
        self.sems = {}
        self.cnt = {}
        for e in ("pe", "dve", "act", "pool"):
            self.sems[e] = nc.alloc_semaphore("s_" + e)
            self.cnt[e] = 0
        self.known = {e: {} for e in self.ENG}
        self.lists = {e: [] for e in self.ENG}
        self.dsem = {}
        for q in ("sp", "pool", "poolc"):
            lst = []
            for i in range(n_dma_sems):
                k = "d_%s_%d" % (q, i)
                self.sems[k] = nc.alloc_semaphore(k)
                self.cnt[k] = 0
                lst.append(k)
            self.dsem[q] = lst
        self.dnext = {"sp": 0, "pool": 0, "poolc": 0}
        self.efree = {e: 0.0 for e in self.ENG}
        self.step_fin = 0.0

    def now(self):
        return max(self.efree.values())

    def _time(self, e, reads, writes, cost, issue=None):
        ready = 0.0
        for t in reads:
            ready = max(ready, t.tw)
        for t in writes:
            ready = max(ready, t.tw, t.tr)
        start = max(self.efree[e], ready + 0.2)
        if issue is None:
            fin = start + cost
            self.efree[e] = fin
        else:
            self.efree[e] = start + issue
            fin = start + cost
        for t in reads:
            t.tr = max(t.tr, fin)
        for t in writes:
            t.tw = fin
            t.tr = 0.0
        self.step_fin = max(self.step_fin, fin)

    def snapshot(self):
        return {k: v for k, v in self.cnt.items() if v > 0 and not k.startswith("d_poolc")}

    def _deps(self, e, reads, writes, skip_self=False):
        deps = {}

        def add(k, v):
            if deps.get(k, 0) < v:
                deps[k] = v
        for t in reads:
            if t.w is not None:
                add(*t.w)
        for t in writes:
            if t.w is not None:
                add(*t.w)
            for k, v in t.r.items():
                add(k, v)
        waits = []
        kn = self.known[e]
        for k, v in deps.items():
            if skip_self and k == e:
                continue
            if kn.get(k, 0) < v:
                kn[k] = v
                waits.append((k, v))
        return waits

    def _mark(self, ev, reads, writes):
        k, v = ev
        for t in reads:
            if t.r.get(k, 0) < v:
                t.r[k] = v
        for t in writes:
            t.w = ev
            t.r = {}

    def op(self, e, fn, reads=(), writes=(), cost=0.3):
        reads = [_T(x) for x in reads]
        writes = [_T(x) for x in writes]
        writes = writes + [t for t in reads if t.excl]
        reads = [t for t in reads if not t.excl]
        self._time(e, reads, writes, cost)
        waits = self._deps(e, reads, writes, skip_self=(e == "pe"))
        self.cnt[e] += 1
        ev = (e, self.cnt[e])
        self.lists[e].append((waits, fn, (e, 1)))
        self._mark(ev, reads, writes)

    def dma(self, q, fn, reads=(), writes=(), cost=4.0):
        reads = [_T(x) for x in reads]
        writes = [_T(x) for x in writes]
        eng = "pool" if q == "poolc" else q
        self._time(eng, reads, writes, cost, issue=(0.1 if eng == "sp" else 1.0))
        waits = self._deps(eng, reads, writes)
        lst = self.dsem[q]
        k = lst[self.dnext[q] % len(lst)]
        self.dnext[q] += 1
        prev = self.cnt[k]
        if prev > 0 and self.known[eng].get(k, 0) < prev:
            self.known[eng][k] = prev
            waits.append((k, prev))
        self.cnt[k] += 16
        ev = (k, self.cnt[k])
        self.lists[eng].append((waits, fn, (k, 16)))
        self._mark(ev, reads, writes)

    def final_wait(self, e, tiles):
        waits = self._deps(e, [_T(x) for x in tiles], ())
        self.lists[e].append((waits, None, None))

    def emit(self):
        nc = self.nc
        with nc.Block() as block:
            def run(name):
                def body(engh):
                    for waits, fn, inc in self.lists[name]:
                        for k, v in waits:
                            engh.wait_ge(self.sems[k], v)
                        if fn is not None:
                            fn(engh).then_inc(self.sems[inc[0]], inc[1])
                return body
            block.sync(run("sp"))
            block.scalar(run("act"))
            block.vector(run("dve"))
            block.gpsimd(run("pool"))
            block.tensor(run("pe"))


def bc_last(ap, n):
    return bass.AP(tensor=ap.tensor, offset=ap.offset, ap=[list(x) for x in ap.ap] + [[0, n]])


def _merge(ga, gb, na, nb):
    ia = ib = 0
    da = db = False
    while not (da and db):
        if not da and (db or ia * nb <= ib * na):
            try:
                next(ga)
                ia += 1
            except StopIteration:
                da = True
        elif not db:
            try:
                next(gb)
                ib += 1
            except StopIteration:
                db = True
        yield
    return


def _sched(P, gens, stop_when=None, bias=None):
    clocks = [0.0] * len(gens)
    live = list(range(len(gens)))
    while live:
        i = min(live, key=lambda j: (clocks[j] - (bias[j] if bias else 0.0), j))
        P.step_fin = 0.0
        try:
            next(gens[i])
        except StopIteration:
            live.remove(i)
            if stop_when is not None and i == stop_when:
                return
            continue
        if P.step_fin > 0.0:
            clocks[i] = P.step_fin
        yield


def _drain(g):
    n = 0
    for _ in g:
        n += 1
    return n


def _chain(*gens):
    for g_ in gens:
        for _ in g_:
            yield


NRING = 3


def build_program(n_slabs_total=BL * NSLAB, dbg=None):
    nc = bass.Bass("TRN2", target_bir_lowering=False)
    P = Prog(nc)

    def din(name, shape):
        return Buf(nc.dram_tensor(name, list(shape), F32, kind="ExternalInput").ap())

    x_d = din("x", [BL, SEQ, D])
    w_in_d = din("w_in", [D, 5376])
    w_upa_d = din("w_up_a", [512, D])
    w_upb_d = din("w_up_b", [512, D])
    w_out_d = din("w_out", [D, D])
    wg_d = din("wg", [D, FFN])
    wu_d = din("wu", [D, FFN])
    wd_d = din("wd", [FFN, D])
    pbc_d = din("pbc", [128, PB_N])
    pcol_d = din("pcol", [128, PC_N])
    w2a2_d = din("w2a2", [128, 512])
    g2_d = din("g2", [128, 512])
    rbaug_d = din("rbaug", [33, 4])
    ohu_d = din("ohu", [33, 383])
    masks_d = din("masks", [128, 4, 512])
    ident_d = din("ident", [128, 128])
    jmat_d = din("jmat", [128, 128])
    bones_d = din("bones", [128, 128])
    cmask_d = din("cmask", [128, 512])
    y_d = Buf(nc.dram_tensor("y", [BL, SEQ, D], F32, kind="ExternalOutput").ap())
    scr_d = Buf(nc.dram_tensor("scr", [4, 383], F32, kind="Internal").ap())
    dbg_outs = {}

    off = [16384 + 256]
    SB_END = 16384 + 212863 - 256

    def nbytes(shape, dt):
        nb = int(np.prod(shape[1:])) * (2 if dt == BF16 else 4)
        return (nb + 63) // 64 * 64

    def S(name, shape, dt):
        nb = nbytes(shape, dt)
        o = off[0]
        off[0] += nb
        assert o + nb <= SB_END, (name, o, nb)
        return Buf(nc.alloc_sbuf_tensor_at(name, list(shape), dt, offset=o))

    ps = [Buf(nc.alloc_psum_tensor("ps%d" % i, [128, 512], F32), T(excl=True)) for i in range(8)]

    def _n(ap):
        return int(np.prod(ap.shape[1:]))

    def mm(out, lhsT, rhs, start, stop, reads, writes):
        c_ = 0.035 + 0.00036 * _n(rhs)
        if rhs.tensor.dtype == F32:
            c_ *= 4.0
        P.op("pe", lambda e: e.matmul(out=out, lhsT=lhsT, rhs=rhs, start=start, stop=stop), reads, writes, cost=c_)

    def trp(out, in_, ident, reads, writes):
        P.op("pe", lambda e: e.transpose(out=out, in_=in_, identity=ident), reads, writes, cost=0.07)

    def act(out, in_, func, reads, writes, bias=None, scale=None):
        kw = {}
        if bias is not None:
            kw["bias"] = bias
        if scale is not None:
            kw["scale"] = scale
        P.op("act", lambda e: e.activation(out=out, in_=in_, func=func, **kw), reads, writes, cost=0.15 + 0.00105 * _n(out))

    def _vc(eng, out):
        return (0.3 + 0.002 * _n(out)) if eng == "pool" else (0.08 + 0.0012 * _n(out))

    def tt(eng, out, in0, in1, op, reads, writes):
        P.op(eng, lambda e: e.tensor_tensor(out=out, in0=in0, in1=in1, op=op), reads, writes, cost=_vc(eng, out))

    def ts(eng, out, in0, s1, s2, op0, op1, reads, writes):
        if s2 is None:
            P.op(eng, lambda e: e.tensor_scalar(out=out, in0=in0, scalar1=s1, scalar2=None, op0=op0), reads, writes,
                 cost=_vc(eng, out))
        else:
            P.op(eng, lambda e: e.tensor_scalar(out=out, in0=in0, scalar1=s1, scalar2=s2, op0=op0, op1=op1), reads, writes,
                 cost=_vc(eng, out))

    def stt(out, in0, scalar, in1, op0, op1, reads, writes):
        P.op("dve", lambda e: e.scalar_tensor_tensor(out=out, in0=in0, scalar=scalar, in1=in1, op0=op0, op1=op1), reads, writes,
             cost=_vc("dve", out))

    def cp(eng, out, in_, reads, writes):
        if eng == "act":
            P.op("act", lambda e: e.activation(out=out, in_=in_, func=AF.Copy), reads, writes, cost=0.15 + 0.00105 * _n(out))
        else:
            P.op(eng, lambda e: e.tensor_copy(out=out, in_=in_), reads, writes, cost=_vc(eng, out))

    def dma(q, out, in_, reads, writes):
        P.dma(q, lambda e: e.dma_start(out=out, in_=in_), reads, writes)

    def dump(name, buf, shape, dt=F32, reads=None):
        if dbg is None or name not in dbg:
            return
        d = Buf(nc.dram_tensor("dbg_" + name, list(shape), dt, kind="ExternalOutput").ap())
        dbg_outs[name] = d
        dma("sp", d[:], buf[:], reads if reads is not None else [buf], [d])

    pc = S("pc", [128, PC_N], F32)
    csm = S("csm", [128, 388], F32)
    identf = S("identf", [128, 128], F32)
    identb = S("identb", [128, 128], BF16)
    bonesf = S("bonesf", [128, 128], F32)
    jm = S("jm", [128, 128], F32)
    masks = S("masks", [128, 4, 512], BF16)
    cmask = S("cmask", [128, 512], F32)
    mb = S("mb", [128, 8, 128], BF16)
    w2a2b = S("w2a2b", [128, 512], BF16)
    g2b = S("g2b", [128, 512], BF16)
    zb = S("zb", [128, 264], BF16)
    neglam = S("neglam", [128, 1], F32)
    eps5 = S("eps5", [128, 1], F32)
    epsx = S("epsx", [128, 1], F32)
    one1 = S("one1", [128, 1], F32)
    mhalf = S("mhalf", [128, 1], F32)
    ln2x20 = S("ln2x20", [128, 1], F32)
    npc = S("npc", [128, PC_N], F32)
    vaug = S("vaug", [128, 16, 4, 129], BF16)
    kT = S("kT", [128, 4, SEQ], BF16)
    S32 = S("S32", [128, 4, 64], F32)
    Sbf = S("Sbf", [128, 4, 64], BF16)
    S32T = [T() for _ in range(4)]
    SbfT = [T() for _ in range(4)]
    prevcol = S("prevcol", [128, 14], F32)
    ring = [S("ring%d" % i, [128, 4096], BF16) for i in range(NRING)]
    hres_t = S("hres", [128, 4, D], F32)
    hresT = [T() for _ in range(4)]
    hT = S("hT", [128, 8, TOK], BF16)
    yab = S("yab", [128, 8, TOK], BF16)
    stt_ = [S("st%d" % i, [128, 2, 6], F32) for i in range(2)]
    mv_ = [S("mv%d" % i, [128, 2], F32) for i in range(2)]
    rs_ = [S("rs%d" % i, [128, 1], F32) for i in range(2)]
    RBYTES = (SB_END - off[0]) // 256 * 256
    arena = S("arena", [128, RBYTES // 4], F32)

    def RV(o, shape, dt, fence=True):
        nb = nbytes(shape, dt)
        assert o % 64 == 0 and o + nb <= RBYTES, (o, nb, RBYTES)
        a = arena.t[0:shape[0], o // 4:(o + nb) // 4]
        if dt == BF16:
            a = a.bitcast(BF16)
        n = int(np.prod(shape[1:]))
        a = a[:, 0:n]
        if len(shape) == 3:
            a = a.rearrange("p (a b) -> p a b", a=shape[1])
        elif len(shape) == 4:
            a = a.rearrange("p (a b c) -> p a b c", a=shape[1], b=shape[2])
        return Buf(a, T(init_reads=P.snapshot(), t0=P.now()) if fence else T())

    class RegionAlloc:
        def __init__(self, base):
            self.o = base

        def __call__(self, shape, dt):
            b_ = RV(self.o, shape, dt)
            self.o += nbytes(shape, dt)
            return b_

    units = []
    ws = {"load": 0, "use": 0}

    def ws_load_next():
        u = ws["load"]
        if u < len(units):
            units[u](ring[u % NRING])
            ws["load"] += 1

    def ws_get():
        u = ws["use"]
        assert u < ws["load"], "unit not loaded"
        return ring[u % NRING]

    def ws_release():
        ws["use"] += 1
        ws_load_next()

    released = set()

    def ws_release_unit(u):
        released.add(u)
        while ws["use"] in released:
            released.discard(ws["use"])
            ws["use"] += 1
            ws_load_next()

    w_in_r = w_in_d.t.rearrange("(c p) n -> p c n", p=128)
    upa_r = w_upa_d.t.rearrange("(c p) n -> p c n", p=128)
    upb_r = w_upb_d.t.rearrange("(c p) n -> p c n", p=128)
    wout_r = w_out_d.t.rearrange("(c p) n -> p c n", p=128)
    wg_r = wg_d.t.rearrange("(c p) n -> p c n", p=128)
    wu_r = wu_d.t.rearrange("(c p) n -> p c n", p=128)
    wd_r = wd_d.t.rearrange("(c p) n -> p c n", p=128)

    unit_parts = []

    def u_multi(parts):
        unit_parts.append(parts)

    def u_cols(src_r, wbuf, kc, col0, ncols):
        u_multi([(src_r, wbuf, kc, col0, ncols)])

    u_cols(w_in_r, w_in_d, 8, 0, 512)
    u_cols(w_in_r, w_in_d, 8, 512, 512)
    u_cols(w_in_r, w_in_d, 8, 1024, 512)
    u_cols(w_in_r, w_in_d, 8, 3072, 256)
    for c in range(4):
        u_multi([(w_in_r, w_in_d, 8, 1536 + 512 * i + 128 * c, 128) for i in range(3)])
    for f in range(8):
        u_multi([(w_in_r, w_in_d, 8, 3328 + 128 * f, 128),
                 (w_in_r, w_in_d, 8, 4352 + 128 * f, 128),
                 (upa_r, w_upa_d, 4, 128 * f, 128),
                 (upb_r, w_upb_d, 4, 128 * f, 128)])
    for half in range(2):
        u_cols(wout_r, w_out_d, 8, 512 * half, 512)
    for u in range(11):
        u_multi([(wg_r, wg_d, 8, 256 * u, 256), (wu_r, wu_d, 8, 256 * u, 256)])
    for u in range(8):
        u_cols(wd_r, wd_d, 22, 128 * u, 128)
    NU = len(unit_parts)
    wbf = nc.dram_tensor("wbf", [NU, 128, 4096], BF16, kind="Internal").ap()
    wbfT = [T() for _ in range(NU)]
    usize = []
    for u, parts in enumerate(unit_parts):
        usize.append(sum(kc * ncols for (_, _, kc, _, ncols) in parts))

    def emit_conversions():
        for u, parts in enumerate(unit_parts):
            o = 0
            for (src_r, wbuf, kc, col0, ncols) in parts:
                dst = wbf[u, :, o:o + kc * ncols].rearrange("p (c n) -> p c n", c=kc)
                dma("poolc", dst, src_r[:, :, col0:col0 + ncols], [wbuf], [wbfT[u]])
                o += kc * ncols

    def mk_loader(u):
        def loader(slot):
            dma("sp", slot[:, 0:usize[u]], wbf[u, :, 0:usize[u]], [wbfT[u]], [slot])
        return loader

    for _ in range(n_slabs_total):
        for u in range(NU):
            units.append(mk_loader(u))

    dma("sp", pc[:], pcol_d[:, :], [pcol_d], [pc])
    dma("sp", csm[:], pbc_d[:, PB_SUB:PB_N], [pbc_d], [csm])
    dma("sp", identf[:], ident_d[:, :], [ident_d], [identf])
    dma("sp", bonesf[:], bones_d[:, :], [bones_d], [bonesf])
    dma("sp", jm[:], jmat_d[:, :], [jmat_d], [jm])
    dma("sp", cmask[:], cmask_d[:, :], [cmask_d], [cmask])
    dma("pool", masks[:], masks_d[:, :, :], [masks_d], [masks])
    dma("pool", w2a2b[:], w2a2_d[:, :], [w2a2_d], [w2a2b])
    dma("pool", g2b[:], g2_d[:, :], [g2_d], [g2b])
    emit_conversions()
    for _ in range(NRING):
        ws_load_next()
    cp("dve", identb[:], identf[:], [identf], [identb])
    P.op("dve", lambda e: e.memset(eps5[:], 1e-5), [], [eps5])
    P.op("dve", lambda e: e.memset(epsx[:], 64e-5), [], [epsx])
    P.op("dve", lambda e: e.memset(one1[:], 1.0), [], [one1])
    P.op("dve", lambda e: e.memset(mhalf[:], -0.5), [], [mhalf])
    P.op("dve", lambda e: e.memset(ln2x20[:], 20.0 * math.log(2.0)), [], [ln2x20])
    ts("dve", npc[:], pc[:], -1.0, None, ALU.mult, None, [pc], [npc])
    P.op("dve", lambda e: e.memset(zb[:], 0.0), [], [zb])
    P.op("dve", lambda e: e.memset(vaug[:].rearrange("p a b c -> p (a b c)"), 1.0), [], [vaug])
    ts("dve", csm[:, 0:128], csm[:, 0:128], 1.0 - LAMBDA_INIT, None, ALU.mult, None, [csm], [csm])
    RA = RegionAlloc(0)
    lt = RA([128, 2, 64], F32)
    ls = RA([128, 2], F32)
    le = RA([128, 2], F32)
    tt("dve", lt[:, 0, :], csm[:, 128:192], csm[:, 192:256], ALU.mult, [csm], [lt])
    tt("dve", lt[:, 1, :], csm[:, 256:320], csm[:, 320:384], ALU.mult, [csm], [lt])
    P.op("dve", lambda e: e.tensor_reduce(out=ls[:], in_=lt[:], axis=AX.X, op=ALU.add), [lt], [ls])
    act(le[:], ls[:], AF.Exp, [ls], [le])
    tt("dve", neglam[:], le[:, 1:2], le[:, 0:1], ALU.subtract, [le], [neglam])
    ts("dve", neglam[:], neglam[:], -LAMBDA_INIT, None, ALU.add, None, [neglam], [neglam])
    def layer_norm_stats(src_ap, srcT, k):
        st, mv, rs = stt_[k], mv_[k], rs_[k]
        for i in range(2):
            P.op("dve", lambda e, i=i: e.bn_stats(out=st[:, i, :], in_=src_ap[:, i * 512:(i + 1) * 512]), [srcT], [st])
        P.op("dve", lambda e: e.bn_aggr(out=mv[:], in_=st[:].rearrange("p a b -> p (a b)")), [st], [mv])
        act(rs[:], mv[:, 1:2], AF.Ln, [mv, eps5], [rs], bias=eps5[:], scale=1.0)
        act(rs[:], rs[:], AF.Exp, [rs], [rs], scale=-0.5)
        return mv, rs

    def transposes_to_featmajor(src, srcT, dst_ap, dstT, j, gcol, bcol, pbanks):
        for half in range(2):
            pb = pbanks[half]
            for q in range(4):
                kc = half * 4 + q
                trp(pb[:, q * 128:(q + 1) * 128], src[:, kc * 128:(kc + 1) * 128], identf[:], [srcT, identf], [pb])
            for q in range(4):
                kc = half * 4 + q
                act(dst_ap[:, kc, j * 128:(j + 1) * 128], pb[:, q * 128:(q + 1) * 128], AF.Identity,
                    [pb, pc], [dstT], bias=pc[:, bcol + kc:bcol + kc + 1], scale=pc[:, gcol + kc:gcol + kc + 1])

    psb = [ps[i][:].bitcast(BF16) for i in range(8)]
    yaT_ap = yab[:, 0:4, :]
    ybT_ap = yab[:, 4:8, :]
    CREG = 13056
    counts = {}

    for slab_i in range(n_slabs_total):
        b = slab_i // NSLAB
        g = slab_i % NSLAB
        t0 = g * TOK
        dbg_here = (dbg is not None and slab_i == dbg.get("slab", 0))

        RA_ = RegionAlloc(0)
        xt = [RA_([128, D], F32) for _ in range(2)]
        lnbuf = RA_([128, 2048], F32)
        dma("sp", lnbuf[:], pbc_d[:, PB_LNIN:PB_LNIN + 2048], [pbc_d], [lnbuf])
        for j in range(4):
            xb = xt[j % 2]
            dma("sp", xb[:], x_d[b, t0 + j * 128:t0 + (j + 1) * 128, :], [x_d], [xb])
            mv, rs = layer_norm_stats(xb, xb, j % 2)
            ts("dve", xb[:], xb[:], mv[:, 0:1], rs[:], ALU.subtract, ALU.mult, [xb, mv, rs], [xb])
            he = "dve" if slab_i == 0 else "pool"
            tt(he, hres_t[:, j, :], xb[:], lnbuf[:, 0:1024], ALU.mult, [xb, lnbuf], [hresT[j]])
            tt(he, hres_t[:, j, :], hres_t[:, j, :], lnbuf[:, 1024:2048], ALU.add, [hresT[j], lnbuf], [hresT[j]])
            transposes_to_featmajor(xb, xb, hT, hT, j, PC_LNIN_G, PC_LNIN_B, (ps[0], ps[1]))
        if dbg_here:
            dump("hT", hT, [128, 8, TOK], BF16)

        RC = RegionAlloc(0)
        qT = RC([128, 4, TOK], BF16)
        wq = ws_get()
        wq3 = wq[:, 0:4096].rearrange("p (c n) -> p c n", c=8)
        for h in range(4):
            pb = ps[2 + h % 2]
            for kc in range(8):
                mm(pb[:], wq3[:, kc, h * 128:(h + 1) * 128], hT[:, kc, :], kc == 0, kc == 7, [wq, hT], [pb])
            cp("act" if h % 2 else "dve", qT[:, h, :], pb[:], [pb], [qT])
        ws_release()
        wk = ws_get()
        wk3 = wk[:, 0:4096].rearrange("p (c n) -> p c n", c=8)
        for h in range(4):
            pb = ps[2 + h % 2]
            for kc in range(8):
                mm(pb[:], wk3[:, kc, h * 128:(h + 1) * 128], hT[:, kc, :], kc == 0, kc == 7, [wk, hT], [pb])
            cp("act" if h % 2 else "dve", kT[:, h, t0:t0 + TOK], pb[:], [pb], [kT])
        ws_release()
        wv = ws_get()
        wv3 = wv[:, 0:4096].rearrange("p (c n) -> p c n", c=8)
        for j in range(4):
            pb = ps[2 + j % 2]
            for kc in range(8):
                mm(pb[:], hT[:, kc, j * 128:(j + 1) * 128], wv3[:, kc, :], kc == 0, kc == 7, [wv, hT], [pb])
            cp("act" if j % 2 else "dve", vaug[:, 4 * g + j, :, 0:128], pb[:].rearrange("p (a b) -> p a b", a=4), [pb], [vaug])
        ws_release()

        if slab_i == 0:
            RS = RegionAlloc(40 * 1024)
            rb = RS([33, 4], F32)
            oh = RS([33, 383], F32)
            u4 = RS([4, 383], F32)
            hk = RS([128, 8, 128], F32)
            dma("sp", rb[:], rbaug_d[:, :], [rbaug_d], [rb])
            dma("sp", oh[:], ohu_d[:, :], [ohu_d], [oh])
            mm(ps[0][0:4, 0:383], rb[:], oh[:], True, True, [rb, oh], [ps[0]])
            cp("dve", u4[:], ps[0][0:4, 0:383], [ps[0]], [u4])
            dma("sp", scr_d[:, :], u4[:], [u4], [scr_d])
            for h in range(4):
                for blk in range(2):
                    src = bass.AP(tensor=scr_d.t.tensor, offset=383 * h + 128 * blk, ap=[[1, 128], [1, 128]])
                    dma("sp", hk[:, 2 * h + blk, :], src, [scr_d], [hk])
            for i in range(8):
                mm(ps[1 + i // 4][:, (i % 4) * 128:(i % 4 + 1) * 128], jm[:], hk[:, i, :], True, True, [jm, hk], [ps[1 + i // 4]])
            for i in range(2):
                cp("dve", mb[:, 4 * i:4 * i + 4, :], ps[1 + i][:].rearrange("p (a b) -> p a b", a=4), [ps[1 + i]], [mb])

        yaT_T = T(init_reads=P.snapshot(), t0=P.now())
        ybT_T = T(init_reads=P.snapshot(), t0=P.now())

        def gen_C():
            pt = [RC([128, TOK], BF16) for _ in range(3)]
            osb = RC([128, 2, 4, 129], F32)
            rz = RC([128, 2, 4, 1], F32)
            dd = [RC([128, 128], F32) for _ in range(2)]
            ssq = [RC([128, 1], F32) for _ in range(2)]
            sqj = RC([128, 128], F32)
            assert RC.o <= CREG
            nkt = 4 * g + 4
            it = 0
            epi = []
            for h in range(4):
                for c in range(2):
                    lo, hi = 64 * c, 64 * c + 64
                    for ob in (ps[2], ps[3]):
                        mm(ob[:, 0:258], zb[:, 0:128], zb[:, 0:258], True, False, [zb], [ob])
                    def emit_st(kt):
                        j0 = max(0, kt - 4 * g)
                        stb = ps[kt % 2]
                        need_diag = kt >= 4 * g
                        need_off = (kt >= 4 * g and j0 + 1 <= 3) or (kt == 4 * g - 1)
                        mm(stb[:, j0 * 128:512], kT[lo:hi, h, kt * 128:(kt + 1) * 128], qT[lo:hi, h, j0 * 128:512],
                           True, not (need_diag or need_off), [kT, qT], [stb])
                        if need_diag:
                            mm(stb[:, j0 * 128:(j0 + 1) * 128], identb[:], mb[:, 2 * h, :], False, not (j0 + 1 <= 3),
                               [identb, mb], [stb])
                            if j0 + 1 <= 3:
                                mm(stb[:, (j0 + 1) * 128:(j0 + 2) * 128], identb[:], mb[:, 2 * h + 1, :], False, True,
                                   [identb, mb], [stb])
                        elif kt == 4 * g - 1:
                            mm(stb[:, 0:128], identb[:], mb[:, 2 * h + 1, :], False, True, [identb, mb], [stb])

                    emit_st(0)
                    for kt in range(nkt):
                        j0 = max(0, kt - 4 * g)
                        stb = ps[kt % 2]
                        ptb = pt[it % 3]
                        it += 1
                        act(ptb[:, j0 * 128:512], stb[:, j0 * 128:512], AF.Exp, [stb, csm], [ptb],
                            bias=csm[:, 384 + h:385 + h], scale=0.125)
                        if kt + 1 < nkt:
                            emit_st(kt + 1)
                        for j in range(j0, 4):
                            ob = ps[2 + j // 2]
                            oc = (j % 2) * 129
                            last = (j % 2 == 1) and (kt == 4 * g + j)
                            mm(ob[:, oc:oc + 129], ptb[:, j * 128:(j + 1) * 128], vaug[:, kt, h, :], False, last,
                               [ptb, vaug], [ob])
                        if epi:
                            epi.pop(0)(ps[kt % 2])
                        yield
                    for j in range(4):
                        ob = ps[2 + j // 2]
                        oc = (j % 2) * 129
                        cp("dve", osb[:, c, j, :], ob[:, oc:oc + 129], [ob], [osb])
                    yield
                assert not epi

                def mk_chunk(h, j):
                    def chunk(pb):
                        if j == 0:
                            P.op("dve", lambda e: e.reciprocal(out=rz[:], in_=osb[:, :, :, 128:129]), [osb], [rz])
                            ts("dve", rz[:, 1, :, :], rz[:, 1, :, :], neglam[:], None, ALU.mult, None, [rz, neglam], [rz])
                        d_ = dd[j % 2]
                        sq_ = ssq[j % 2]
                        ts("dve", d_[:], osb[:, 0, j, 0:128], rz[:, 0, j, :], None, ALU.mult, None, [osb, rz], [d_])
                        stt(d_[:], osb[:, 1, j, 0:128], rz[:, 1, j, :], d_[:], ALU.mult, ALU.add, [osb, rz, d_], [d_])
                        P.op("act", lambda e: e.activation(out=sqj[:], in_=d_[:], func=AF.Square, accum_out=sq_[:]),
                             [d_], [sqj, sq_])
                        act(sq_[:], sq_[:], AF.Ln, [sq_, eps5], [sq_], bias=eps5[:], scale=1.0 / 128.0)
                        act(sq_[:], sq_[:], AF.Exp, [sq_], [sq_], scale=-0.5)
                        stt(d_[:], d_[:], sq_[:], csm[:, 0:128], ALU.mult, ALU.mult, [d_, sq_, csm], [d_])
                        trp(pb[:, 0:128], d_[:], identf[:], [d_, identf], [pb])
                        cp("act", yaT_ap[:, h, j * 128:(j + 1) * 128], pb[:, 0:128], [pb], [yaT_T])
                    return chunk
                epi.extend(mk_chunk(h, j) for j in range(4))
            while epi:
                epi.pop(0)(ps[len(epi) % 2])
                yield

        RD = RegionAlloc(CREG)
        twb = RD([128, TOK], BF16)
        sgl = RD([128, TOK], BF16)

        def mk_ctx(t):
            X = {}
            X["raw"] = RD([128, 3, 513], F32)
            X["f_"] = [RD([128, TOK], F32) for i in range(6)]
            X["Vb"] = RD([128, TOK], BF16)
            X["Mm"] = RD([128, NCH, 64], BF16)
            X["Nn"] = RD([128, NCH, 64], BF16)
            X["Pm"] = RD([128, NCH, 64], BF16)
            X["Xa1"] = RD([128, NCH, 64], BF16)
            X["XTa1"] = RD([128, NCH, 64], BF16)
            X["gst"] = RD([128, 4, NCH], F32)
            X["rhsb"] = [RD([128, 64], BF16) for _ in range(2)]
            X["ub"] = [RD([128, 64], BF16) for _ in range(2)]
            X["s0w"] = RD([128, 64], F32)
            X["set"] = dict(
                At=RD([128, TOK], BF16), Bt=RD([128, TOK], BF16), Kt=RD([128, TOK], BF16), Rt=RD([128, TOK], BF16),
                Btok=RD([128, NCH, 64], BF16), Ktok=RD([128, NCH, 64], BF16), Vtok=RD([128, NCH, 64], BF16),
                gT=RD([128, TOK], F32), bonT=RD([128, TOK], F32), WC=RD([128, NCH], F32),
                Pf=RD([128, NCH, 64], BF16), AKT=RD([128, NCH, 64], BF16), ARBT=RD([128, NCH, 64], BF16),
                ARKT=RD([128, NCH, 64], BF16), ytok=RD([128, NCH, 64], F32))
            X["DB"] = (ps[4 + 2 * t], ps[5 + 2 * t])
            X["DBb"] = (psb[4 + 2 * t], psb[5 + 2 * t])
            return X
        ctx = [mk_ctx(0), mk_ctx(1)]
        lraw = ctx[0]["raw"]

        def lerp_chunk(pb, dst3, idx, pcidx, dtmp):
            cp("act", dst3[:, idx, 1:513], pb[:], [pb], [dst3])
            if g == 0:
                P.op("dve", lambda e: e.memset(dst3[:, idx, 0:1], 0.0), [], [dst3])
            else:
                cp("dve", dst3[:, idx, 0:1], prevcol[:, pcidx:pcidx + 1], [prevcol], [dst3])
            cp("dve", prevcol[:, pcidx:pcidx + 1], dst3[:, idx, 512:513], [dst3], [prevcol])
            tt("dve", dtmp[:], dst3[:, idx, 0:512], dst3[:, idx, 1:513], ALU.subtract, [dst3], [dtmp])
            stt(dst3[:, idx, 1:513], dtmp[:], pc[:, PC_MU + pcidx:PC_MU + pcidx + 1], dst3[:, idx, 1:513],
                ALU.mult, ALU.add, [dtmp, pc, dst3], [dst3])

        def gen_D_lora():
            wl_ = ws_get()
            wl3 = wl_[:, 0:2048].rearrange("p (c n) -> p c n", c=8)
            for i in range(2):
                pb = ctx[0]["DB"][i]
                for kc in range(8):
                    mm(pb[:], wl3[:, kc, i * 128:(i + 1) * 128], hT[:, kc, :], kc == 0, kc == 7, [wl_, hT], [pb])
                lerp_chunk(pb, lraw, i, 12 + i, ctx[0]["f_"][4])
                yield
            ws_release()
            cp("dve", twb[64:128, :], lraw[64:128, 0, 1:513], [lraw], [twb])
            act(lraw[0:64, 0, 1:513], lraw[0:64, 0, 1:513], AF.Exp, [lraw], [lraw], scale=-2.0)
            act(lraw[0:64, 0, 1:513], lraw[0:64, 0, 1:513], AF.Ln, [lraw, one1], [lraw], bias=one1[0:64, :], scale=1.0)
            act(lraw[0:64, 0, 1:513], lraw[0:64, 0, 1:513], AF.Exp, [lraw], [lraw], scale=-1.0)
            ts("dve", twb[0:64, :], lraw[0:64, 0, 1:513], 2.0, -1.0, ALU.mult, ALU.add, [lraw], [twb])
            act(lraw[:, 1, 1:513], lraw[:, 1, 1:513], AF.Exp, [lraw], [lraw], scale=-1.0)
            act(lraw[:, 1, 1:513], lraw[:, 1, 1:513], AF.Ln, [lraw, one1], [lraw], bias=one1[:], scale=1.0)
            act(sgl[:], lraw[:, 1, 1:513], AF.Exp, [lraw], [sgl], scale=-1.0)
            yield

        fl = lambda bf_: bf_[:].rearrange("p n t -> p (n t)")

        def gen_P1(c, X):
            st_ = X["set"]
            raw, f_, Vb, Mm, Nn, Pm = X["raw"], X["f_"], X["Vb"], X["Mm"], X["Nn"], X["Pm"]
            Xa = [Mm, X["Xa1"]]
            XTa = [Nn, X["XTa1"]]
            DB, DBb = X["DB"], X["DBb"]
            dtmp = f_[4]
            u_exp = slab_i * NU + 4 + c
            assert ws["load"] > u_exp
            At, Bt, Kt, Rt = st_["At"], st_["Bt"], st_["Kt"], st_["Rt"]
            Btok, Ktok, Vtok = st_["Btok"], st_["Ktok"], st_["Vtok"]
            gT, bonT, WC = st_["gT"], st_["bonT"], st_["WC"]
            Pf, AKT, ARBT, ARKT = st_["Pf"], st_["AKT"], st_["ARBT"], st_["ARKT"]
            wr = ring[u_exp % NRING]
            wr4 = wr[:, 0:3072].rearrange("p (i c n) -> p i c n", i=3, c=8)
            for i in range(3):
                pb = DB[i % 2]
                for kc in range(8):
                    mm(pb[:], wr4[:, i, kc, :], hT[:, kc, :], kc == 0, kc == 7, [wr, hT], [pb])
                lerp_chunk(pb, raw, i, 4 * i + c, dtmp)
                yield
            ws_release_unit(u_exp)
            r_ = raw[:, 0, 1:513]
            k_ = raw[:, 1, 1:513]
            v_ = raw[:, 2, 1:513]
            mm(DB[0][:], w2a2b[0:64, c * 128:(c + 1) * 128], twb[0:64, :], True, True, [w2a2b, twb], [DB[0]])
            mm(DB[1][:], w2a2b[64:128, c * 128:(c + 1) * 128], twb[64:128, :], True, True, [w2a2b, twb], [DB[1]])
            sigw, cl, epos, eneg, eprv, a_ = f_
            act(sigw[:], DB[0][:], AF.Exp, [DB[0], npc], [sigw], bias=npc[:, PC_W0 + c:PC_W0 + c + 1], scale=-1.0)
            act(a_[:], DB[1][:], AF.Exp, [DB[1], npc], [a_], bias=npc[:, PC_A0 + c:PC_A0 + c + 1], scale=-1.0)
            act(sigw[:], sigw[:], AF.Ln, [sigw, one1], [sigw], bias=one1[:], scale=1.0)
            act(a_[:], a_[:], AF.Ln, [a_, one1], [a_], bias=one1[:], scale=1.0)
            act(sigw[:], sigw[:], AF.Exp, [sigw, mhalf], [sigw], bias=mhalf[:], scale=-1.0)
            act(a_[:], a_[:], AF.Exp, [a_], [a_], scale=-1.0)
            mm(DB[0][:], g2b[:, c * 128:(c + 1) * 128], sgl[:], True, True, [g2b, sgl], [DB[0]])
            cp("act", gT[:], DB[0][:], [DB[0]], [gT])
            yield
            P.op("dve", lambda e: e.tensor_tensor_scan(out=cl[:], data0=cmask[:], data1=sigw[:], initial=0.0,
                                                      op0=ALU.mult, op1=ALU.add), [cmask, sigw], [cl])
            act(epos[:], cl[:], AF.Exp, [cl], [epos], scale=-1.0)
            act(eneg[:], cl[:], AF.Exp, [cl], [eneg])
            tt("dve", eprv[:], cl[:], sigw[:], ALU.subtract, [cl, sigw], [eprv])
            act(eprv[:], eprv[:], AF.Exp, [eprv], [eprv], scale=-1.0)
            cp("dve", WC[:], epos[:].rearrange("p (n t) -> p n t", t=64)[:, :, 63], [epos], [WC])
            yield
            kkn, tmp = cl, sigw
            ts("dve", kkn[:], k_, pc[:, PC_KK + c:PC_KK + c + 1], None, ALU.mult, None, [raw, pc], [kkn])
            tt("dve", tmp[:], kkn[:], kkn[:], ALU.mult, [kkn], [tmp])
            mm(DB[0][:], bonesf[:], tmp[:], True, True, [bonesf, tmp], [DB[0]])
            ts("dve", tmp[:], DB[0][:], 1e-24, None, ALU.max, None, [DB[0]], [tmp])
            act(tmp[:], tmp[:], AF.Ln, [tmp], [tmp], scale=float(2.0 ** 40))
            act(tmp[:], tmp[:], AF.Exp, [tmp, ln2x20], [tmp], bias=ln2x20[:], scale=-0.5)
            tt("dve", kkn[:], kkn[:], tmp[:], ALU.mult, [kkn, tmp], [kkn])
            yield
            stt(At[:], kkn[:], -1.0, eprv[:], ALU.mult, ALU.mult, [kkn, eprv], [At])
            tt("dve", tmp[:], kkn[:], a_[:], ALU.mult, [kkn, a_], [tmp])
            tt("dve", Bt[:], tmp[:], eneg[:], ALU.mult, [tmp, eneg], [Bt])
            ts("dve", a_[:], a_[:], -1.0, pc[:, PC_KA + c:PC_KA + c + 1], ALU.add, ALU.mult, [a_, pc], [a_])
            stt(a_[:], a_[:], 1.0, k_, ALU.add, ALU.mult, [a_, raw], [a_])
            tt("dve", Kt[:], a_[:], eneg[:], ALU.mult, [a_, eneg], [Kt])
            tt("dve", Rt[:], r_, epos[:], ALU.mult, [raw, epos], [Rt])
            cp("act", Vb[:], v_, [raw], [Vb])
            yield
            stt(tmp[:], r_, pc[:, PC_RK + c:PC_RK + c + 1], a_[:], ALU.mult, ALU.mult, [raw, pc, a_], [tmp])
            mm(DB[1][:], bonesf[:], tmp[:], True, True, [bonesf, tmp], [DB[1]])
            tt("dve", bonT[:], DB[1][:], v_, ALU.mult, [DB[1], raw], [bonT])
            yield
            for ti, (srcb, dstb) in enumerate(((Bt, Btok), (Kt, Ktok), (Vb, Vtok))):
                pbk = DB[ti % 2]
                pv = DBb[ti % 2]
                for n in range(NCH):
                    for hh in range(2):
                        trp(pv[64 * hh:64 * hh + 64, n * 64:(n + 1) * 64], srcb[64 * hh:64 * hh + 64, n * 64:(n + 1) * 64],
                            identb[64 * hh:64 * hh + 64, 64 * hh:64 * hh + 64], [srcb, identb], [pbk])
                cp("act" if ti == 1 else "dve", dstb[:].rearrange("p n t -> p (n t)"), pv[:, 0:512], [pbk], [dstb])
                yield
            prods = ((Bt, At, Mm, 0), (At, Bt, Nn, 1), (Kt, At, AKT, 0), (Bt, Rt, ARBT, 2), (Kt, Rt, ARKT, 2))
            for pi, (la, rb_, dst, mk) in enumerate(prods):
                bank = DB[pi % 2]
                for n in range(NCH):
                    for hh in range(2):
                        sl = slice(64 * hh, 64 * hh + 64)
                        mm(bank[sl, n * 64:(n + 1) * 64], la[sl, n * 64:(n + 1) * 64], rb_[sl, n * 64:(n + 1) * 64],
                           True, True, [la, rb_], [bank])
                tt("dve", fl(dst), bank[:], masks[:, mk, :], ALU.mult, [bank, masks], [dst])
                yield
            tt("dve", fl(Pm), fl(Mm), masks[:, 3, :], ALU.add, [Mm, masks], [Pm])
            Xc, XTc, Pc = Mm, Nn, Pm
            for lev in range(1, 6):
                lastlev = (lev == 5)
                Xn, XTn = Xa[lev % 2], XTa[lev % 2]
                Pn = Pf if lastlev else Pm
                for n in range(NCH):
                    for hh in range(2):
                        sl = slice(64 * hh, 64 * hh + 64)
                        cs = slice(n * 64, (n + 1) * 64)
                        mm(DB[0][sl, cs], Xc[sl, n, :], XTc[sl, n, :], True, True, [Xc, XTc], [DB[0]])
                        if not lastlev:
                            mm(DB[1][sl, cs], XTc[sl, n, :], Xc[sl, n, :], True, True, [Xc, XTc], [DB[1]])
                cp("act", fl(XTn), DB[0][:], [DB[0]], [XTn])
                if not lastlev:
                    cp("dve", fl(Xn), DB[1][:], [DB[1]], [Xn])
                yield
                for n in range(NCH):
                    for hh in range(2):
                        sl = slice(64 * hh, 64 * hh + 64)
                        cs = slice(n * 64, (n + 1) * 64)
                        mm(DB[0][sl, cs], XTn[sl, n, :], Pc[sl, n, :], True, True, [XTn, Pc], [DB[0]])
                tt("dve", fl(Pn), DB[0][:], fl(Pc), ALU.add, [DB[0], Pc], [Pn])
                yield
                Xc, XTc, Pc = Xn, XTn, Pn

        def gen_P2(c, X):
            st_ = X["set"]
            gst, rhsb, ub, s0w = X["gst"], X["rhsb"], X["ub"], X["s0w"]
            ysq = Buf(X["f_"][1][:].rearrange("p (n t) -> p n t", t=64), X["f_"][1].T)
            DB = X["DB"]
            At, Rt = st_["At"], st_["Rt"]
            Btok, Ktok, Vtok = st_["Btok"], st_["Ktok"], st_["Vtok"]
            gT, bonT, WC = st_["gT"], st_["bonT"], st_["WC"]
            Pf, AKT, ARBT, ARKT, ytok = st_["Pf"], st_["AKT"], st_["ARBT"], st_["ARKT"], st_["ytok"]
            if g == 0:
                P.op("dve", lambda e: e.memset(S32[:, c, :], 0.0), [], [S32T[c]])
                P.op("dve", lambda e: e.memset(Sbf[:, c, :], 0.0), [], [SbfT[c]])
            for n in range(NCH):
                cs = slice(n * 64, (n + 1) * 64)
                tb = DB[n % 2]
                pr, pu, pst, py = tb[:, 0:64], tb[:, 64:128], tb[:, 128:192], tb[:, 192:256]
                rb2, ub2 = rhsb[n % 2], ub[n % 2]
                for hh in range(2):
                    sl = slice(64 * hh, 64 * hh + 64)
                    mm(pr[sl, :], At[sl, cs], Sbf[sl, c, :], True, False, [At, SbfT[c]], [tb])
                    mm(pr[sl, :], AKT[sl, n, :], Vtok[sl, n, :], False, True, [AKT, Vtok], [tb])
                cp("act", rb2[:], pr, [tb], [rb2])
                ts("dve", s0w[:], S32[:, c, :], WC[:, n:n + 1], None, ALU.mult, None, [S32T[c], WC], [s0w])
                yield
                for hh in range(2):
                    sl = slice(64 * hh, 64 * hh + 64)
                    mm(pu[sl, :], Pf[sl, n, :], rb2[sl, :], True, True, [Pf, rb2], [tb])
                cp("act", ub2[:], pu, [tb], [ub2])
                yield
                for hh in range(2):
                    sl = slice(64 * hh, 64 * hh + 64)
                    mm(pst[sl, :], Btok[sl, n, :], ub2[sl, :], True, False, [Btok, ub2], [tb])
                    mm(pst[sl, :], Ktok[sl, n, :], Vtok[sl, n, :], False, True, [Ktok, Vtok], [tb])
                for hh in range(2):
                    sl = slice(64 * hh, 64 * hh + 64)
                    mm(py[sl, :], Rt[sl, cs], Sbf[sl, c, :], True, False, [Rt, SbfT[c]], [tb])
                    mm(py[sl, :], ARBT[sl, n, :], ub2[sl, :], False, False, [ARBT, ub2], [tb])
                    mm(py[sl, :], ARKT[sl, n, :], Vtok[sl, n, :], False, True, [ARKT, Vtok], [tb])
                stt(Sbf[:, c, :], pst, WC[:, n:n + 1], s0w[:], ALU.mult, ALU.add, [tb, WC, s0w], [SbfT[c]])
                stt(S32[:, c, :], pst, WC[:, n:n + 1], s0w[:], ALU.mult, ALU.add, [tb, WC, s0w], [S32T[c]])
                cp("act", ytok[:, n, :], py, [tb], [ytok])
                yield
            P.op("dve", lambda e: e.tensor_reduce(out=gst[:, 0, :], in_=ytok[:], axis=AX.X, op=ALU.add), [ytok], [gst])
            tt("dve", ysq[:], ytok[:], ytok[:], ALU.mult, [ytok], [ysq])
            P.op("dve", lambda e: e.tensor_reduce(out=gst[:, 1, :], in_=ysq[:], axis=AX.X, op=ALU.add), [ysq], [gst])
            ts("dve", gst[:, 2, :], gst[:, 0, :], 1.0 / 64.0, None, ALU.mult, None, [gst], [gst])
            tt("dve", gst[:, 0, :], gst[:, 2, :], gst[:, 2, :], ALU.mult, [gst], [gst])
            stt(gst[:, 3, :], gst[:, 1, :], 1.0 / 64.0, gst[:, 0, :], ALU.mult, ALU.subtract, [gst], [gst])
            act(gst[:, 3, :], gst[:, 3, :], AF.Ln, [gst, epsx], [gst], bias=epsx[:], scale=1.0)
            act(gst[:, 3, :], gst[:, 3, :], AF.Exp, [gst], [gst], scale=-0.5)
            yield
            tt("dve", ysq[:], ytok[:], bc_last(gst[:, 2, :], 64), ALU.subtract, [ytok, gst], [ysq])
            tt("dve", ysq[:], ysq[:], bc_last(gst[:, 3, :], 64), ALU.mult, [ysq, gst], [ysq])
            for n in range(NCH):
                for hh in range(2):
                    sl = slice(64 * hh, 64 * hh + 64)
                    mm(DB[0][sl, n * 64:(n + 1) * 64], ysq[sl, n, :], identf[sl, 64 * hh:64 * hh + 64], True, True,
                       [ysq, identf], [DB[0]])
            etmp = ysq[:].rearrange("p n t -> p (n t)")
            act(etmp, DB[0][:], AF.Identity, [DB[0], pc], [ysq], bias=pc[:, PC_LXB + c:PC_LXB + c + 1],
                scale=pc[:, PC_LXG + c:PC_LXG + c + 1])
            tt("dve", etmp, etmp, bonT[:], ALU.add, [ysq, bonT], [ysq])
            tt("dve", ybT_ap[:, c, :], etmp, gT[:], ALU.mult, [ysq, gT], [ybT_T])
            yield

        def d_thread(t):
            for c in (t, t + 2):
                for _ in gen_P1(c, ctx[t]):
                    yield
                for _ in gen_P2(c, ctx[t]):
                    yield

        gC = gen_C()
        _drain(_sched(P, [gC, gen_D_lora()], stop_when=1))
        _drain(_sched(P, [gC, d_thread(0), d_thread(1)], bias=[(25.0 if g >= 2 else (8.0 if g == 1 else 0.0)), 0.0, 0.0]))
        if dbg_here:
            dump("yaT", Buf(yaT_ap, yaT_T), [128, 4, TOK], BF16)
            dump("ybT", Buf(ybT_ap, ybT_T), [128, 4, TOK], BF16)

        RE = RegionAlloc(0)
        mT = RE([128, 8, TOK], BF16)
        sga = [RE([128, TOK], F32) for _ in range(2)]
        sgb = [RE([128, TOK], F32) for _ in range(2)]
        n1 = [RE([128, D], F32) for _ in range(2)]
        lnbuf = RE([128, 2048], F32)
        dma("sp", lnbuf[:], pbc_d[:, PB_LN1:PB_LN1 + 2048], [pbc_d], [lnbuf])
        for f in range(8):
            wf = ws_get()
            ga3 = wf[:, 0:1024].rearrange("p (c n) -> p c n", c=8)
            gb3 = wf[:, 1024:2048].rearrange("p (c n) -> p c n", c=8)
            ua3 = wf[:, 2048:2560].rearrange("p (c n) -> p c n", c=4)
            ub3 = wf[:, 2560:3072].rearrange("p (c n) -> p c n", c=4)
            o4 = 4 * (f % 2)
            pga, pgb, pua, pub = ps[o4], ps[o4 + 1], ps[o4 + 2], ps[o4 + 3]
            wr_ = [pub]
            for kc in range(8):
                mm(pga[:], ga3[:, kc, :], hT[:, kc, :], kc == 0, kc == 7, [wf, hT], [pga])
            for kc in range(8):
                mm(pgb[:], gb3[:, kc, :], hT[:, kc, :], kc == 0, kc == 7, [wf, hT], [pgb])
            for kc in range(4):
                mm(pua[:], ua3[:, kc, :], yaT_ap[:, kc, :], kc == 0, kc == 3, [wf, yaT_T], [pua])
            for kc in range(4):
                mm(pub[:], ub3[:, kc, :], ybT_ap[:, kc, :], kc == 0, kc == 3, [wf, ybT_T], wr_)
            ws_release()
            sa, sb_ = sga[f % 2], sgb[f % 2]
            act(sa[:], pga[:], AF.Sigmoid, [pga], [sa])
            act(sb_[:], pgb[:], AF.Sigmoid, [pgb], [sb_])
            tt("dve", sa[:], sa[:], pua[:], ALU.mult, [sa, pua], [sa])
            tt("dve", sb_[:], sb_[:], pub[:], ALU.mult, [sb_] + wr_, [sb_])
            tt("dve", mT[:, f, :], sa[:], sb_[:], ALU.add, [sa, sb_], [mT])
        if dbg_here:
            dump("mT", mT, [128, 8, TOK], BF16)
        for half in range(2):
            wo = ws_get()
            wo3 = wo[:, 0:4096].rearrange("p (c n) -> p c n", c=8)
            for j in range(4):
                pb = ps[(2 * half + j) % 4]
                for f in range(8):
                    mm(pb[:], mT[:, f, j * 128:(j + 1) * 128], wo3[:, f, :], f == 0, f == 7, [wo, mT], [pb])
                stt(hres_t[:, j, half * 512:(half + 1) * 512], hres_t[:, j, half * 512:(half + 1) * 512], ALPHA, pb[:],
                    ALU.mult, ALU.add, [hresT[j], pb], [hresT[j]])
            ws_release()
        h1T_T = T(init_reads=P.snapshot(), t0=P.now())
        for j in range(4):
            nb = n1[j % 2]
            mv, rs = layer_norm_stats(hres_t[:, j, :], hresT[j], j % 2)
            ts("dve", nb[:], hres_t[:, j, :], mv[:, 0:1], rs[:], ALU.subtract, ALU.mult, [hresT[j], mv, rs], [nb])
            tt("pool", hres_t[:, j, :], nb[:], lnbuf[:, 0:1024], ALU.mult, [nb, lnbuf], [hresT[j]])
            tt("pool", hres_t[:, j, :], hres_t[:, j, :], lnbuf[:, 1024:2048], ALU.add, [hresT[j], lnbuf], [hresT[j]])
            transposes_to_featmajor(nb, nb, yab, h1T_T, j, PC_LN1_G, PC_LN1_B, (ps[4], ps[5]))

        RF = RegionAlloc(0)
        actT = RF([128, NHC, TOK], BF16)
        sil = [RF([128, TOK], F32) for _ in range(2)]
        n2 = [RF([128, D], F32) for _ in range(2)]
        lnbuf = RF([128, 2048], F32)
        dma("sp", lnbuf[:], pbc_d[:, PB_LN2:PB_LN2 + 2048], [pbc_d], [lnbuf])
        for u in range(11):
            wf = ws_get()
            w4 = wf[:, 0:4096].rearrange("p (a c n) -> p a c n", a=2, c=8)
            for q in range(2):
                hc = 2 * u + q
                pg, pu_ = ps[2 * (hc % 3)], ps[2 * (hc % 3) + 1]
                for kc in range(8):
                    mm(pg[:], w4[:, 0, kc, q * 128:(q + 1) * 128], yab[:, kc, :], kc == 0, kc == 7, [wf, h1T_T], [pg])
                for kc in range(8):
                    mm(pu_[:], w4[:, 1, kc, q * 128:(q + 1) * 128], yab[:, kc, :], kc == 0, kc == 7, [wf, h1T_T], [pu_])
                sl_ = sil[hc % 2]
                act(sl_[:], pg[:], AF.Silu, [pg], [sl_])
                tt("dve", actT[:, hc, :], sl_[:], pu_[:], ALU.mult, [sl_, pu_], [actT])
            ws_release()
        for u in range(8):
            wdn = ws_get()
            wd3 = wdn[:, 0:NHC * 128].rearrange("p (c n) -> p c n", c=NHC)
            pb = ps[u % 2]
            for j in range(4):
                for hc in range(NHC):
                    mm(pb[:, j * 128:(j + 1) * 128], actT[:, hc, j * 128:(j + 1) * 128], wd3[:, hc, :], hc == 0, hc == NHC - 1,
                       [wdn, actT], [pb])
            ws_release()
            for j in range(4):
                stt(hres_t[:, j, u * 128:(u + 1) * 128], hres_t[:, j, u * 128:(u + 1) * 128], ALPHA, pb[:, j * 128:(j + 1) * 128],
                    ALU.mult, ALU.add, [hresT[j], pb], [hresT[j]])
        for j in range(4):
            nb = n2[j % 2]
            mv, rs = layer_norm_stats(hres_t[:, j, :], hresT[j], j % 2)
            ts("dve", nb[:], hres_t[:, j, :], mv[:, 0:1], rs[:], ALU.subtract, ALU.mult, [hresT[j], mv, rs], [nb])
            tt("dve", nb[:], nb[:], lnbuf[:, 0:1024], ALU.mult, [nb, lnbuf], [nb])
            tt("pool", nb[:], nb[:], lnbuf[:, 1024:2048], ALU.add, [nb, lnbuf], [nb])
            dma("sp", y_d[b, t0 + j * 128:t0 + (j + 1) * 128, :], nb[:], [nb], [y_d])

    P.final_wait("sp", [y_d] + list(dbg_outs.values()))
    P.emit()
    return nc, dbg_outs


def _t5_bucket_np(dist):
    d = np.maximum(dist, 1).astype(np.float32)
    large = 16 + (np.log(d / np.float32(16)) / np.float32(math.log(128 / 16)) * np.float32(16)).astype(np.int32)
    large = np.minimum(large, 31)
    return np.where(dist < 16, dist, large)


def _host_constants():
    ohu = np.zeros((33, 383), np.float32)
    for i in range(383):
        if i < 127:
            ohu[32, i] = MASKVAL
        else:
            bkt = int(_t5_bucket_np(np.array([i - 127], np.int32))[0])
            ohu[bkt, i] += 8.0
            ohu[31, i] -= 8.0
    p = np.arange(128)[:, None] % 64
    t = np.arange(64)[None, :]
    m = np.stack([(p < t), (t < p), (p <= t), (p == t)], 0).astype(np.float32)
    masks = np.ascontiguousarray(np.broadcast_to(m[:, :, None, :], (4, 128, 8, 64)).transpose(1, 0, 2, 3).reshape(128, 4, 512))
    ident = np.eye(128, dtype=np.float32)
    jmat = np.ascontiguousarray(ident[::-1])
    bones = np.zeros((128, 128), np.float32)
    bones[:64, :64] = 1.0
    bones[64:, 64:] = 1.0
    cm = np.ones((128, 512), np.float32)
    cm[:, ::64] = 0.0
    return dict(ohu=ohu, masks=masks, ident=ident, jmat=jmat, bones=bones, cmask=cm)


def _prep_inputs(inp):
    f = lambda a: np.ascontiguousarray(np.asarray(a, dtype=np.float32))
    row = lambda a: np.asarray(a, np.float32).reshape(-1)
    pbc_row = np.concatenate([row(inp["ln_in_g"]), row(inp["ln_in_b"]), row(inp["ln1_g"]), row(inp["ln1_b"]),
                              row(inp["ln2_g"]), row(inp["ln2_b"]), row(inp["diff_subln_g"]),
                              row(inp["diff_lam_q1"]), row(inp["diff_lam_k1"]), row(inp["diff_lam_q2"]),
                              row(inp["diff_lam_k2"]), row(np.asarray(inp["rel_bias"])[31])])
    assert pbc_row.shape[0] == PB_N
    pbc = np.ascontiguousarray(np.broadcast_to(pbc_row[None, :], (128, PB_N)))
    col = lambda a: row(a).reshape(-1, 128).T
    pcol = np.ascontiguousarray(np.concatenate(
        [col(inp["ln_in_g"]), col(inp["ln_in_b"]), col(inp["ln1_g"]), col(inp["ln1_b"]), col(inp["rwkv_mu"]),
         col(inp["rwkv_w0"]), col(inp["rwkv_a0"]), col(inp["rwkv_k_k"]), col(inp["rwkv_k_a"]), col(inp["rwkv_r_k"]),
         col(inp["rwkv_lnx_g"]), col(inp["rwkv_lnx_b"])], axis=1))
    assert pcol.shape == (128, PC_N)
    w2a2 = np.ascontiguousarray(np.concatenate([np.asarray(inp["rwkv_w2"], np.float32)[0],
                                                np.asarray(inp["rwkv_a2"], np.float32)[0]], 0))
    rbaug = np.ascontiguousarray(np.concatenate([np.asarray(inp["rel_bias"], np.float32), np.ones((1, 4), np.float32)], 0))
    shared = dict(w_in=f(inp["w_in"][0]), w_up_a=f(inp["w_up_a"][0]), w_up_b=f(inp["w_up_b"][0]), w_out=f(inp["w_out"][0]),
                  wg=f(inp["ffn_w_gate"][0]), wu=f(inp["ffn_w_up"][0]), wd=f(inp["ffn_w_down"][0]),
                  pbc=pbc, pcol=pcol, w2a2=w2a2, g2=f(inp["rwkv_g2"][0]), rbaug=rbaug)
    shared.update(_host_constants())
    return shared


_CACHE = {}


def kernel(**inputs):
    x = np.asarray(inputs["x"], np.float32)
    shared = _prep_inputs(inputs)
    if "nc" not in _CACHE:
        _CACHE["nc"] = build_program()[0]
    nc = _CACHE["nc"]
    in_maps = []
    for c in range(NCORES):
        m = dict(shared)
        m["x"] = np.ascontiguousarray(x[BL * c:BL * (c + 1)])
        in_maps.append(m)
    res = run_bass_kernel_spmd(nc, in_maps, core_ids=list(range(NCORES)))
    out = np.concatenate([np.asarray(r["y"], np.float32) for r in res.results], axis=0)
    return out
```

```python
import math
import numpy as np
import concourse.bass as bass
import concourse.mybir as mybir
from concourse.bass_utils import run_bass_kernel_spmd

F32 = mybir.dt.float32
BF16 = mybir.dt.bfloat16
ALU = mybir.AluOpType
AF = mybir.ActivationFunctionType
AX = mybir.AxisListType

NCORES = 8
D = 1024
SEQ = 2048
BL = 2
TOK = 512
NSLAB = SEQ // TOK
ALPHA = 2.0 ** 0.25
LAMBDA_INIT = 0.2
FFN = 2816
NHC = FFN // 128
C = 64
NCH = TOK // C
MASKVAL = -1.0e9
EXPM05 = math.exp(-0.5)

PB_LNIN, PB_LN1, PB_LN2, PB_SUB, PB_LAM, PB_RB31, PB_N = 0, 2048, 4096, 6144, 6272, 6528, 6532
PC_LNIN_G, PC_LNIN_B, PC_LN1_G, PC_LN1_B, PC_MU, PC_W0, PC_A0, PC_KK, PC_KA, PC_RK, PC_LXG, PC_LXB, PC_N = \
    0, 8, 16, 24, 32, 46, 50, 54, 58, 62, 66, 70, 74


class T:
    __slots__ = ("w", "r", "excl", "tw", "tr")

    def __init__(self, init_reads=None, excl=False, t0=0.0):
        self.w = None
        self.r = dict(init_reads) if init_reads else {}
        self.excl = excl
        self.tw = 0.0
        self.tr = t0


class Buf:
    def __init__(self, t, tr=None):
        self.t = t
        self.T = tr if tr is not None else T()

    def __getitem__(self, idx):
        return self.t[idx]


def _T(x):
    return x.T if isinstance(x, Buf) else x


class Prog:
    ENG = ("pe", "dve", "act", "pool", "sp")

    def __init__(self, nc, n_dma_sems=10):
        self.nc = nc
        self.sems = {}
        self.cnt = {}
        for e in ("pe", "dve", "act", "pool"):
            self.sems[e] = nc.alloc_semaphore("s_" + e)
            self.cnt[e] = 0
        self.known = {e: {} for e in self.ENG}
        self.lists = {e: [] for e in self.ENG}
        self.dsem = {}
        for q in ("sp", "pool", "poolc"):
            lst = []
            for i in range(n_dma_sems):
                k = "d_%s_%d" % (q, i)
                self.sems[k] = nc.alloc_semaphore(k)
                self.cnt[k] = 0
                lst.append(k)
            self.dsem[q] = lst
        self.dnext = {"sp": 0, "pool": 0, "poolc": 0}
        self.efree = {e: 0.0 for e in self.ENG}
        self.step_fin = 0.0

    def now(self):
        return max(self.efree.values())

    def _time(self, e, reads, writes, cost, issue=None):
        ready = 0.0
        for t in reads:
            ready = max(ready, t.tw)
        for t in writes:
            ready = max(ready, t.tw, t.tr)
        start = max(self.efree[e], ready + 0.2)
        if issue is None:
            fin = start + cost
            self.efree[e] = fin
        else:
            self.efree[e] = start + issue
            fin = start + cost
        for t in reads:
            t.tr = max(t.tr, fin)
        for t in writes:
            t.tw = fin
            t.tr = 0.0
        self.step_fin = max(self.step_fin, fin)

    def snapshot(self):
        return {k: v for k, v in self.cnt.items() if v > 0 and not k.startswith("d_poolc")}

    def _deps(self, e, reads, writes, skip_self=False):
        deps = {}

        def add(k, v):
            if deps.get(k, 0) < v:
                deps[k] = v
        for t in reads:
            if t.w is not None:
                add(*t.w)
        for t in writes:
            if t.w is not None:
                add(*t.w)
            for k, v in t.r.items():
                add(k, v)
        waits = []
        kn = self.known[e]
        for k, v in deps.items():
            if skip_self and k == e:
                continue
            if kn.get(k, 0) < v:
                kn[k] = v
                waits.append((k, v))
        return waits

    def _mark(self, ev, reads, writes):
        k, v = ev
        for t in reads:
            if t.r.get(k, 0) < v:
                t.r[k] = v
        for t in writes:
            t.w = ev
            t.r = {}

    def op(self, e, fn, reads=(), writes=(), cost=0.3):
        reads = [_T(x) for x in reads]
        writes = [_T(x) for x in writes]
        writes = writes + [t for t in reads if t.excl]
        reads = [t for t in reads if not t.excl]
        self._time(e, reads, writes, cost)
        waits = self._deps(e, reads, writes, skip_self=(e == "pe"))
        self.cnt[e] += 1
        ev = (e, self.cnt[e])
        self.lists[e].append((waits, fn, (e, 1)))
        self._mark(ev, reads, writes)

    def dma(self, q, fn, reads=(), writes=(), cost=4.0):
        reads = [_T(x) for x in reads]
        writes = [_T(x) for x in writes]
        eng = "pool" if q == "poolc" else q
        self._time(eng, reads, writes, cost, issue=(0.1 if eng == "sp" else 1.0))
        waits = self._deps(eng, reads, writes)
        lst = self.dsem[q]
        k = lst[self.dnext[q] % len(lst)]
        self.dnext[q] += 1
        prev = self.cnt[k]
        if prev > 0 and self.known[eng].get(k, 0) < prev:
            self.known[eng][k] = prev
            waits.append((k, prev))
        self.cnt[k] += 16
        ev = (k, self.cnt[k])
        self.lists[eng].append((waits, fn, (k, 16)))
        self._mark(ev, reads, writes)

    def final_wait(self, e, tiles):
        waits = self._deps(e, [_T(x) for x in tiles], ())
        self.lists[e].append((waits, None, None))

    def emit(self):
        nc = self.nc
        with nc.Block() as block:
            def run(name):
                def body(engh):
                    for waits, fn, inc in self.lists[name]:
                        for k, v in waits:
                            engh.wait_ge(self.sems[k], v)
                        if fn is not None:
                            fn(engh).then_inc(self.sems[inc[0]], inc[1])
                return body
            block.sync(run("sp"))
            block.scalar(run("act"))
            block.vector(run("dve"))
            block.gpsimd(run("pool"))
            block.tensor(run("pe"))


def bc_last(ap, n):
    return bass.AP(tensor=ap.tensor, offset=ap.offset, ap=[list(x) for x in ap.ap] + [[0, n]])


def _merge(ga, gb, na, nb):
    ia = ib = 0
    da = db = False
    while not (da and db):
        if not da and (db or ia * nb <= ib * na):
            try:
                next(ga)
                ia += 1
            except StopIteration:
                da = True
        elif not db:
            try:
                next(gb)
                ib += 1
            except StopIteration:
                db = True
        yield
    return


def _sched(P, gens, stop_when=None, bias=None):
    clocks = [0.0] * len(gens)
    live = list(range(len(gens)))
    while live:
        i = min(live, key=lambda j: (clocks[j] - (bias[j] if bias else 0.0), j))
        P.step_fin = 0.0
        try:
            next(gens[i])
        except StopIteration:
            live.remove(i)
            if stop_when is not None and i == stop_when:
                return
            continue
        if P.step_fin > 0.0:
            clocks[i] = P.step_fin
        yield


def _drain(g):
    n = 0
    for _ in g:
        n += 1
    return n


def _chain(*gens):
    for g_ in gens:
        for _ in g_:
            yield


NRING = 3


def build_program(n_slabs_total=BL * NSLAB, dbg=None):
    nc = bass.Bass("TRN2", target_bir_lowering=False)
    P = Prog(nc)

    def din(name, shape):
        return Buf(nc.dram_tensor(name, list(shape), F32, kind="ExternalInput").ap())

    x_d = din("x", [BL, SEQ, D])
    w_in_d = din("w_in", [D, 5376])
    w_upa_d = din("w_up_a", [512, D])
    w_upb_d = din("w_up_b", [512, D])
    w_out_d = din("w_out", [D, D])
    wg_d = din("wg", [D, FFN])
    wu_d = din("wu", [D, FFN])
    wd_d = din("wd", [FFN, D])
    pbc_d = din("pbc", [128, PB_N])
    pcol_d = din("pcol", [128, PC_N])
    w2a2_d = din("w2a2", [128, 512])
    g2_d = din("g2", [128, 512])
    rbaug_d = din("rbaug", [33, 4])
    ohu_d = din("ohu", [33, 383])
    masks_d = din("masks", [128, 4, 512])
    ident_d = din("ident", [128, 128])
    jmat_d = din("jmat", [128, 128])
    bones_d = din("bones", [128, 128])
    cmask_d = din("cmask", [128, 512])
    y_d = Buf(nc.dram_tensor("y", [BL, SEQ, D], F32, kind="ExternalOutput").ap())
    scr_d = Buf(nc.dram_tensor("scr", [4, 383], F32, kind="Internal").ap())
    dbg_outs = {}
    y_stores = []

    off = [16384 + 256]
    SB_END = 16384 + 212863 - 256

    def nbytes(shape, dt):
        nb = int(np.prod(shape[1:])) * (2 if dt == BF16 else 4)
        return (nb + 63) // 64 * 64

    def S(name, shape, dt):
        nb = nbytes(shape, dt)
        o = off[0]
        off[0] += nb
        assert o + nb <= SB_END, (name, o, nb)
        return Buf(nc.alloc_sbuf_tensor_at(name, list(shape), dt, offset=o))

    ps = [Buf(nc.alloc_psum_tensor("ps%d" % i, [128, 512], F32), T(excl=True)) for i in range(8)]

    def _n(ap):
        return int(np.prod(ap.shape[1:]))

    def mm(out, lhsT, rhs, start, stop, reads, writes):
        c_ = 0.035 + 0.00036 * _n(rhs)
        if rhs.tensor.dtype == F32:
            c_ *= 4.0
        P.op("pe", lambda e: e.matmul(out=out, lhsT=lhsT, rhs=rhs, start=start, stop=stop), reads, writes, cost=c_)

    def trp(out, in_, ident, reads, writes):
        P.op("pe", lambda e: e.transpose(out=out, in_=in_, identity=ident), reads, writes, cost=0.07)

    def act(out, in_, func, reads, writes, bias=None, scale=None):
        kw = {}
        if bias is not None:
            kw["bias"] = bias
        if scale is not None:
            kw["scale"] = scale
        P.op("act", lambda e: e.activation(out=out, in_=in_, func=func, **kw), reads, writes, cost=0.15 + 0.00105 * _n(out))

    def _vc(eng, out):
        return (0.3 + 0.002 * _n(out)) if eng == "pool" else (0.08 + 0.0012 * _n(out))

    def tt(eng, out, in0, in1, op, reads, writes):
        P.op(eng, lambda e: e.tensor_tensor(out=out, in0=in0, in1=in1, op=op), reads, writes, cost=_vc(eng, out))

    def ts(eng, out, in0, s1, s2, op0, op1, reads, writes):
        if s2 is None:
            P.op(eng, lambda e: e.tensor_scalar(out=out, in0=in0, scalar1=s1, scalar2=None, op0=op0), reads, writes,
                 cost=_vc(eng, out))
        else:
            P.op(eng, lambda e: e.tensor_scalar(out=out, in0=in0, scalar1=s1, scalar2=s2, op0=op0, op1=op1), reads, writes,
                 cost=_vc(eng, out))

    def stt(out, in0, scalar, in1, op0, op1, reads, writes):
        P.op("dve", lambda e: e.scalar_tensor_tensor(out=out, in0=in0, scalar=scalar, in1=in1, op0=op0, op1=op1), reads, writes,
             cost=_vc("dve", out))

    def cp(eng, out, in_, reads, writes):
        if eng == "act":
            P.op("act", lambda e: e.activation(out=out, in_=in_, func=AF.Copy), reads, writes, cost=0.15 + 0.00105 * _n(out))
        else:
            P.op(eng, lambda e: e.tensor_copy(out=out, in_=in_), reads, writes, cost=_vc(eng, out))

    def dma(q, out, in_, reads, writes):
        P.dma(q, lambda e: e.dma_start(out=out, in_=in_), reads, writes)

    def dump(name, buf, shape, dt=F32, reads=None):
        if dbg is None or name not in dbg:
            return
        d = Buf(nc.dram_tensor("dbg_" + name, list(shape), dt, kind="ExternalOutput").ap())
        dbg_outs[name] = d
        dma("sp", d[:], buf[:], reads if reads is not None else [buf], [d])

    pc = S("pc", [128, PC_N], F32)
    csm = S("csm", [128, 388], F32)
    identf = S("identf", [128, 128], F32)
    identb = S("identb", [128, 128], BF16)
    bonesf = S("bonesf", [128, 128], F32)
    jm = S("jm", [128, 128], F32)
    masks = S("masks", [128, 4, 512], BF16)
    cmask = S("cmask", [128, 512], F32)
    mb = S("mb", [128, 8, 128], BF16)
    w2a2b = S("w2a2b", [128, 512], BF16)
    g2b = S("g2b", [128, 512], BF16)
    zb = S("zb", [128, 264], BF16)
    neglam = S("neglam", [128, 1], F32)
    eps5 = S("eps5", [128, 1], F32)
    epsx = S("epsx", [128, 1], F32)
    one1 = S("one1", [128, 1], F32)
    mhalf = S("mhalf", [128, 1], F32)
    ln2x20 = S("ln2x20", [128, 1], F32)
    npc = S("npc", [128, PC_N], F32)
    vaug = S("vaug", [128, 16, 4, 129], BF16)
    kT = S("kT", [128, 4, SEQ], BF16)
    S32 = S("S32", [128, 4, 64], F32)
    Sbf = S("Sbf", [128, 4, 64], BF16)
    S32T = [T() for _ in range(4)]
    SbfT = [T() for _ in range(4)]
    prevcol = S("prevcol", [128, 14], F32)
    ring = [S("ring%d" % i, [128, 4096], BF16) for i in range(NRING)]
    hres_t = S("hres", [128, 4, D], F32)
    hresT = [T() for _ in range(4)]
    hT = S("hT", [128, 8, TOK], BF16)
    yab = S("yab", [128, 8, TOK], BF16)
    stt_ = [S("st%d" % i, [128, 2, 6], F32) for i in range(2)]
    mv_ = [S("mv%d" % i, [128, 2], F32) for i in range(2)]
    rs_ = [S("rs%d" % i, [128, 1], F32) for i in range(2)]
    RBYTES = (SB_END - off[0]) // 256 * 256
    arena = S("arena", [128, RBYTES // 4], F32)

    def RV(o, shape, dt, fence=True):
        nb = nbytes(shape, dt)
        assert o % 64 == 0 and o + nb <= RBYTES, (o, nb, RBYTES)
        a = arena.t[0:shape[0], o // 4:(o + nb) // 4]
        if dt == BF16:
            a = a.bitcast(BF16)
        n = int(np.prod(shape[1:]))
        a = a[:, 0:n]
        if len(shape) == 3:
            a = a.rearrange("p (a b) -> p a b", a=shape[1])
        elif len(shape) == 4:
            a = a.rearrange("p (a b c) -> p a b c", a=shape[1], b=shape[2])
        return Buf(a, T(init_reads=P.snapshot(), t0=P.now()) if fence else T())

    class RegionAlloc:
        def __init__(self, base):
            self.o = base

        def __call__(self, shape, dt):
            b_ = RV(self.o, shape, dt)
            self.o += nbytes(shape, dt)
            return b_

    units = []
    ws = {"load": 0, "use": 0}

    def ws_load_next():
        u = ws["load"]
        if u < len(units):
            units[u](ring[u % NRING])
            ws["load"] += 1

    def ws_get():
        u = ws["use"]
        assert u < ws["load"], "unit not loaded"
        return ring[u % NRING]

    def ws_release():
        ws["use"] += 1
        ws_load_next()

    released = set()

    def ws_release_unit(u):
        released.add(u)
        while ws["use"] in released:
            released.discard(ws["use"])
            ws["use"] += 1
            ws_load_next()

    w_in_r = w_in_d.t.rearrange("(c p) n -> p c n", p=128)
    upa_r = w_upa_d.t.rearrange("(c p) n -> p c n", p=128)
    upb_r = w_upb_d.t.rearrange("(c p) n -> p c n", p=128)
    wout_r = w_out_d.t.rearrange("(c p) n -> p c n", p=128)
    wg_r = wg_d.t.rearrange("(c p) n -> p c n", p=128)
    wu_r = wu_d.t.rearrange("(c p) n -> p c n", p=128)
    wd_r = wd_d.t.rearrange("(c p) n -> p c n", p=128)

    unit_parts = []

    def u_multi(parts):
        unit_parts.append(parts)

    def u_cols(src_r, wbuf, kc, col0, ncols):
        u_multi([(src_r, wbuf, kc, col0, ncols)])

    u_cols(w_in_r, w_in_d, 8, 0, 512)
    u_cols(w_in_r, w_in_d, 8, 512, 512)
    u_cols(w_in_r, w_in_d, 8, 1024, 512)
    u_cols(w_in_r, w_in_d, 8, 3072, 256)
    for c in range(4):
        u_multi([(w_in_r, w_in_d, 8, 1536 + 512 * i + 128 * c, 128) for i in range(3)])
    for f in range(8):
        u_multi([(w_in_r, w_in_d, 8, 3328 + 128 * f, 128),
                 (w_in_r, w_in_d, 8, 4352 + 128 * f, 128),
                 (upa_r, w_upa_d, 4, 128 * f, 128),
                 (upb_r, w_upb_d, 4, 128 * f, 128)])
    for half in range(2):
        u_cols(wout_r, w_out_d, 8, 512 * half, 512)
    for u in range(11):
        u_multi([(wg_r, wg_d, 8, 256 * u, 256), (wu_r, wu_d, 8, 256 * u, 256)])
    for u in range(8):
        u_cols(wd_r, wd_d, 22, 128 * u, 128)
    NU = len(unit_parts)
    wbf = nc.dram_tensor("wbf", [NU, 128, 4096], BF16, kind="Internal").ap()
    wbfT = [T() for _ in range(NU)]
    usize = []
    for u, parts in enumerate(unit_parts):
        usize.append(sum(kc * ncols for (_, _, kc, _, ncols) in parts))

    def emit_conversions():
        for u, parts in enumerate(unit_parts):
            o = 0
            for (src_r, wbuf, kc, col0, ncols) in parts:
                dst = wbf[u, :, o:o + kc * ncols].rearrange("p (c n) -> p c n", c=kc)
                dma("poolc", dst, src_r[:, :, col0:col0 + ncols], [wbuf], [wbfT[u]])
                o += kc * ncols

    def mk_loader(u):
        def loader(slot):
            dma("sp", slot[:, 0:usize[u]], wbf[u, :, 0:usize[u]], [wbfT[u]], [slot])
        return loader

    for _ in range(n_slabs_total):
        for u in range(NU):
            units.append(mk_loader(u))

    dma("sp", pc[:], pcol_d[:, :], [pcol_d], [pc])
    dma("sp", csm[:], pbc_d[:, PB_SUB:PB_N], [pbc_d], [csm])
    dma("sp", identf[:], ident_d[:, :], [ident_d], [identf])
    dma("sp", bonesf[:], bones_d[:, :], [bones_d], [bonesf])
    dma("sp", jm[:], jmat_d[:, :], [jmat_d], [jm])
    dma("sp", cmask[:], cmask_d[:, :], [cmask_d], [cmask])
    dma("pool", masks[:], masks_d[:, :, :], [masks_d], [masks])
    dma("pool", w2a2b[:], w2a2_d[:, :], [w2a2_d], [w2a2b])
    dma("pool", g2b[:], g2_d[:, :], [g2_d], [g2b])
    emit_conversions()
    for _ in range(NRING):
        ws_load_next()
    cp("dve", identb[:], identf[:], [identf], [identb])
    P.op("dve", lambda e: e.memset(eps5[:], 1e-5), [], [eps5])
    P.op("dve", lambda e: e.memset(epsx[:], 64e-5), [], [epsx])
    P.op("dve", lambda e: e.memset(one1[:], 1.0), [], [one1])
    P.op("dve", lambda e: e.memset(mhalf[:], -0.5), [], [mhalf])
    P.op("dve", lambda e: e.memset(ln2x20[:], 20.0 * math.log(2.0)), [], [ln2x20])
    ts("dve", npc[:], pc[:], -1.0, None, ALU.mult, None, [pc], [npc])
    P.op("dve", lambda e: e.memset(zb[:], 0.0), [], [zb])
    P.op("dve", lambda e: e.memset(vaug[:].rearrange("p a b c -> p (a b c)"), 1.0), [], [vaug])
    ts("dve", csm[:, 0:128], csm[:, 0:128], 1.0 - LAMBDA_INIT, None, ALU.mult, None, [csm], [csm])
    RA = RegionAlloc(0)
    lt = RA([128, 2, 64], F32)
    ls = RA([128, 2], F32)
    le = RA([128, 2], F32)
    tt("dve", lt[:, 0, :], csm[:, 128:192], csm[:, 192:256], ALU.mult, [csm], [lt])
    tt("dve", lt[:, 1, :], csm[:, 256:320], csm[:, 320:384], ALU.mult, [csm], [lt])
    P.op("dve", lambda e: e.tensor_reduce(out=ls[:], in_=lt[:], axis=AX.X, op=ALU.add), [lt], [ls])
    act(le[:], ls[:], AF.Exp, [ls], [le])
    tt("dve", neglam[:], le[:, 1:2], le[:, 0:1], ALU.subtract, [le], [neglam])
    ts("dve", neglam[:], neglam[:], -LAMBDA_INIT, None, ALU.add, None, [neglam], [neglam])
    def layer_norm_stats(src_ap, srcT, k):
        st, mv, rs = stt_[k], mv_[k], rs_[k]
        for i in range(2):
            P.op("dve", lambda e, i=i: e.bn_stats(out=st[:, i, :], in_=src_ap[:, i * 512:(i + 1) * 512]), [srcT], [st])
        P.op("dve", lambda e: e.bn_aggr(out=mv[:], in_=st[:].rearrange("p a b -> p (a b)")), [st], [mv])
        act(rs[:], mv[:, 1:2], AF.Ln, [mv, eps5], [rs], bias=eps5[:], scale=1.0)
        act(rs[:], rs[:], AF.Exp, [rs], [rs], scale=-0.5)
        return mv, rs

    def transposes_to_featmajor(src, srcT, dst_ap, dstT, j, gcol, bcol, pbanks):
        for half in range(2):
            pb = pbanks[half]
            for q in range(4):
                kc = half * 4 + q
                trp(pb[:, q * 128:(q + 1) * 128], src[:, kc * 128:(kc + 1) * 128], identf[:], [srcT, identf], [pb])
            for q in range(4):
                kc = half * 4 + q
                act(dst_ap[:, kc, j * 128:(j + 1) * 128], pb[:, q * 128:(q + 1) * 128], AF.Identity,
                    [pb, pc], [dstT], bias=pc[:, bcol + kc:bcol + kc + 1], scale=pc[:, gcol + kc:gcol + kc + 1])

    psb = [ps[i][:].bitcast(BF16) for i in range(8)]
    yaT_ap = yab[:, 0:4, :]
    ybT_ap = yab[:, 4:8, :]
    CREG = 13056
    counts = {}

    for slab_i in range(n_slabs_total):
        b = slab_i // NSLAB
        g = slab_i % NSLAB
        t0 = g * TOK
        dbg_here = (dbg is not None and slab_i == dbg.get("slab", 0))

        RA_ = RegionAlloc(0)
        xt = [RA_([128, D], F32) for _ in range(2)]
        lnbuf = RA_([128, 2048], F32)
        dma("sp", lnbuf[:], pbc_d[:, PB_LNIN:PB_LNIN + 2048], [pbc_d], [lnbuf])
        for j in range(4):
            xb = xt[j % 2]
            dma("sp", xb[:], x_d[b, t0 + j * 128:t0 + (j + 1) * 128, :], [x_d], [xb])
            mv, rs = layer_norm_stats(xb, xb, j % 2)
            ts("dve", xb[:], xb[:], mv[:, 0:1], rs[:], ALU.subtract, ALU.mult, [xb, mv, rs], [xb])
            he = "dve" if slab_i == 0 else "pool"
            tt(he, hres_t[:, j, :], xb[:], lnbuf[:, 0:1024], ALU.mult, [xb, lnbuf], [hresT[j]])
            tt(he, hres_t[:, j, :], hres_t[:, j, :], lnbuf[:, 1024:2048], ALU.add, [hresT[j], lnbuf], [hresT[j]])
            transposes_to_featmajor(xb, xb, hT, hT, j, PC_LNIN_G, PC_LNIN_B, (ps[0], ps[1]))
        if dbg_here:
            dump("hT", hT, [128, 8, TOK], BF16)

        RC = RegionAlloc(0)
        qT = RC([128, 4, TOK], BF16)
        wq = ws_get()
        wq3 = wq[:, 0:4096].rearrange("p (c n) -> p c n", c=8)
        for h in range(4):
            pb = ps[2 + h % 2]
            for kc in range(8):
                mm(pb[:], wq3[:, kc, h * 128:(h + 1) * 128], hT[:, kc, :], kc == 0, kc == 7, [wq, hT], [pb])
            cp("act" if h % 2 else "dve", qT[:, h, :], pb[:], [pb], [qT])
        ws_release()
        wk = ws_get()
        wk3 = wk[:, 0:4096].rearrange("p (c n) -> p c n", c=8)
        for h in range(4):
            pb = ps[2 + h % 2]
            for kc in range(8):
                mm(pb[:], wk3[:, kc, h * 128:(h + 1) * 128], hT[:, kc, :], kc == 0, kc == 7, [wk, hT], [pb])
            cp("act" if h % 2 else "dve", kT[:, h, t0:t0 + TOK], pb[:], [pb], [kT])
        ws_release()
        wv = ws_get()
        wv3 = wv[:, 0:4096].rearrange("p (c n) -> p c n", c=8)
        for j in range(4):
            pb = ps[2 + j % 2]
            for kc in range(8):
                mm(pb[:], hT[:, kc, j * 128:(j + 1) * 128], wv3[:, kc, :], kc == 0, kc == 7, [wv, hT], [pb])
            cp("act" if j % 2 else "dve", vaug[:, 4 * g + j, :, 0:128], pb[:].rearrange("p (a b) -> p a b", a=4), [pb], [vaug])
        ws_release()

        if slab_i == 0:
            RS = RegionAlloc(40 * 1024)
            rb = RS([33, 4], F32)
            oh = RS([33, 383], F32)
            u4 = RS([4, 383], F32)
            hk = RS([128, 8, 128], F32)
            dma("sp", rb[:], rbaug_d[:, :], [rbaug_d], [rb])
            dma("sp", oh[:], ohu_d[:, :], [ohu_d], [oh])
            mm(ps[0][0:4, 0:383], rb[:], oh[:], True, True, [rb, oh], [ps[0]])
            cp("dve", u4[:], ps[0][0:4, 0:383], [ps[0]], [u4])
            dma("sp", scr_d[:, :], u4[:], [u4], [scr_d])
            for h in range(4):
                for blk in range(2):
                    src = bass.AP(tensor=scr_d.t.tensor, offset=383 * h + 128 * blk, ap=[[1, 128], [1, 128]])
                    dma("sp", hk[:, 2 * h + blk, :], src, [scr_d], [hk])
            for i in range(8):
                mm(ps[1 + i // 4][:, (i % 4) * 128:(i % 4 + 1) * 128], jm[:], hk[:, i, :], True, True, [jm, hk], [ps[1 + i // 4]])
            for i in range(2):
                cp("dve", mb[:, 4 * i:4 * i + 4, :], ps[1 + i][:].rearrange("p (a b) -> p a b", a=4), [ps[1 + i]], [mb])

        yaT_T = T(init_reads=P.snapshot(), t0=P.now())
        ybT_T = T(init_reads=P.snapshot(), t0=P.now())

        def gen_C():
            pt = [RC([128, TOK], BF16) for _ in range(3)]
            osb = RC([128, 2, 4, 129], F32)
            rz = RC([128, 2, 4, 1], F32)
            dd = [RC([128, 128], F32) for _ in range(2)]
            ssq = [RC([128, 1], F32) for _ in range(2)]
            sqj = RC([128, 128], F32)
            assert RC.o <= CREG
            nkt = 4 * g + 4
            it = 0
            epi = []
            for h in range(4):
                for c in range(2):
                    lo, hi = 64 * c, 64 * c + 64
                    for ob in (ps[2], ps[3]):
                        mm(ob[:, 0:258], zb[:, 0:128], zb[:, 0:258], True, False, [zb], [ob])
                    def emit_st(kt):
                        j0 = max(0, kt - 4 * g)
                        stb = ps[kt % 2]
                        need_diag = kt >= 4 * g
                        need_off = (kt >= 4 * g and j0 + 1 <= 3) or (kt == 4 * g - 1)
                        mm(stb[:, j0 * 128:512], kT[lo:hi, h, kt * 128:(kt + 1) * 128], qT[lo:hi, h, j0 * 128:512],
                           True, not (need_diag or need_off), [kT, qT], [stb])
                        if need_diag:
                            mm(stb[:, j0 * 128:(j0 + 1) * 128], identb[:], mb[:, 2 * h, :], False, not (j0 + 1 <= 3),
                               [identb, mb], [stb])
                            if j0 + 1 <= 3:
                                mm(stb[:, (j0 + 1) * 128:(j0 + 2) * 128], identb[:], mb[:, 2 * h + 1, :], False, True,
                                   [identb, mb], [stb])
                        elif kt == 4 * g - 1:
                            mm(stb[:, 0:128], identb[:], mb[:, 2 * h + 1, :], False, True, [identb, mb], [stb])

                    emit_st(0)
                    for kt in range(nkt):
                        j0 = max(0, kt - 4 * g)
                        stb = ps[kt % 2]
                        ptb = pt[it % 3]
                        it += 1
                        act(ptb[:, j0 * 128:512], stb[:, j0 * 128:512], AF.Exp, [stb, csm], [ptb],
                            bias=csm[:, 384 + h:385 + h], scale=0.125)
                        if kt + 1 < nkt:
                            emit_st(kt + 1)
                        for j in range(j0, 4):
                            ob = ps[2 + j // 2]
                            oc = (j % 2) * 129
                            last = (j % 2 == 1) and (kt == 4 * g + j)
                            mm(ob[:, oc:oc + 129], ptb[:, j * 128:(j + 1) * 128], vaug[:, kt, h, :], False, last,
                               [ptb, vaug], [ob])
                        if epi:
                            epi.pop(0)(ps[kt % 2])
                        yield
                    for j in range(4):
                        ob = ps[2 + j // 2]
                        oc = (j % 2) * 129
                        cp("dve", osb[:, c, j, :], ob[:, oc:oc + 129], [ob], [osb])
                    yield
                assert not epi

                def mk_chunk(h, j):
                    def chunk(pb):
                        if j == 0:
                            P.op("dve", lambda e: e.reciprocal(out=rz[:], in_=osb[:, :, :, 128:129]), [osb], [rz])
                            ts("dve", rz[:, 1, :, :], rz[:, 1, :, :], neglam[:], None, ALU.mult, None, [rz, neglam], [rz])
                        d_ = dd[j % 2]
                        sq_ = ssq[j % 2]
                        ts("dve", d_[:], osb[:, 0, j, 0:128], rz[:, 0, j, :], None, ALU.mult, None, [osb, rz], [d_])
                        stt(d_[:], osb[:, 1, j, 0:128], rz[:, 1, j, :], d_[:], ALU.mult, ALU.add, [osb, rz, d_], [d_])
                        P.op("act", lambda e: e.activation(out=sqj[:], in_=d_[:], func=AF.Square, accum_out=sq_[:]),
                             [d_], [sqj, sq_])
                        act(sq_[:], sq_[:], AF.Ln, [sq_, eps5], [sq_], bias=eps5[:], scale=1.0 / 128.0)
                        act(sq_[:], sq_[:], AF.Exp, [sq_], [sq_], scale=-0.5)
                        stt(d_[:], d_[:], sq_[:], csm[:, 0:128], ALU.mult, ALU.mult, [d_, sq_, csm], [d_])
                        trp(pb[:, 0:128], d_[:], identf[:], [d_, identf], [pb])
                        cp("act", yaT_ap[:, h, j * 128:(j + 1) * 128], pb[:, 0:128], [pb], [yaT_T])
                    return chunk
                epi.extend(mk_chunk(h, j) for j in range(4))
            while epi:
                epi.pop(0)(ps[len(epi) % 2])
                yield

        RD = RegionAlloc(CREG)
        twb = RD([128, TOK], BF16)
        sgl = RD([128, TOK], BF16)

        def mk_ctx(t):
            X = {}
            X["raw"] = RD([128, 3, 513], F32)
            X["f_"] = [RD([128, TOK], F32) for i in range(6)]
            X["Vb"] = RD([128, TOK], BF16)
            X["Mm"] = RD([128, NCH, 64], BF16)
            X["Nn"] = RD([128, NCH, 64], BF16)
            X["Pm"] = RD([128, NCH, 64], BF16)
            X["Xa1"] = RD([128, NCH, 64], BF16)
            X["XTa1"] = RD([128, NCH, 64], BF16)
            X["gst"] = RD([128, 4, NCH], F32)
            X["rhsb"] = [RD([128, 64], BF16) for _ in range(2)]
            X["ub"] = [RD([128, 64], BF16) for _ in range(2)]
            X["s0w"] = RD([128, 64], F32)
            X["set"] = dict(
                At=RD([128, TOK], BF16), Bt=RD([128, TOK], BF16), Kt=RD([128, TOK], BF16), Rt=RD([128, TOK], BF16),
                Btok=RD([128, NCH, 64], BF16), Ktok=RD([128, NCH, 64], BF16), Vtok=RD([128, NCH, 64], BF16),
                gT=RD([128, TOK], F32), bonT=RD([128, TOK], F32), WC=RD([128, NCH], F32),
                Pf=RD([128, NCH, 64], BF16), AKT=RD([128, NCH, 64], BF16), ARBT=RD([128, NCH, 64], BF16),
                ARKT=RD([128, NCH, 64], BF16), ytok=RD([128, NCH, 64], F32))
            X["DB"] = (ps[4 + 2 * t], ps[5 + 2 * t])
            X["DBb"] = (psb[4 + 2 * t], psb[5 + 2 * t])
            return X
        ctx = [mk_ctx(0), mk_ctx(1)]
        lraw = ctx[0]["raw"]

        def lerp_chunk(pb, dst3, idx, pcidx, dtmp):
            cp("act", dst3[:, idx, 1:513], pb[:], [pb], [dst3])
            if g == 0:
                P.op("dve", lambda e: e.memset(dst3[:, idx, 0:1], 0.0), [], [dst3])
            else:
                cp("dve", dst3[:, idx, 0:1], prevcol[:, pcidx:pcidx + 1], [prevcol], [dst3])
            cp("dve", prevcol[:, pcidx:pcidx + 1], dst3[:, idx, 512:513], [dst3], [prevcol])
            tt("dve", dtmp[:], dst3[:, idx, 0:512], dst3[:, idx, 1:513], ALU.subtract, [dst3], [dtmp])
            stt(dst3[:, idx, 1:513], dtmp[:], pc[:, PC_MU + pcidx:PC_MU + pcidx + 1], dst3[:, idx, 1:513],
                ALU.mult, ALU.add, [dtmp, pc, dst3], [dst3])

        def gen_D_lora():
            wl_ = ws_get()
            wl3 = wl_[:, 0:2048].rearrange("p (c n) -> p c n", c=8)
            for i in range(2):
                pb = ctx[0]["DB"][i]
                for kc in range(8):
                    mm(pb[:], wl3[:, kc, i * 128:(i + 1) * 128], hT[:, kc, :], kc == 0, kc == 7, [wl_, hT], [pb])
                lerp_chunk(pb, lraw, i, 12 + i, ctx[0]["f_"][4])
                yield
            ws_release()
            cp("dve", twb[64:128, :], lraw[64:128, 0, 1:513], [lraw], [twb])
            act(lraw[0:64, 0, 1:513], lraw[0:64, 0, 1:513], AF.Exp, [lraw], [lraw], scale=-2.0)
            act(lraw[0:64, 0, 1:513], lraw[0:64, 0, 1:513], AF.Ln, [lraw, one1], [lraw], bias=one1[0:64, :], scale=1.0)
            act(lraw[0:64, 0, 1:513], lraw[0:64, 0, 1:513], AF.Exp, [lraw], [lraw], scale=-1.0)
            ts("dve", twb[0:64, :], lraw[0:64, 0, 1:513], 2.0, -1.0, ALU.mult, ALU.add, [lraw], [twb])
            act(lraw[:, 1, 1:513], lraw[:, 1, 1:513], AF.Exp, [lraw], [lraw], scale=-1.0)
            act(lraw[:, 1, 1:513], lraw[:, 1, 1:513], AF.Ln, [lraw, one1], [lraw], bias=one1[:], scale=1.0)
            act(sgl[:], lraw[:, 1, 1:513], AF.Exp, [lraw], [sgl], scale=-1.0)
            yield

        fl = lambda bf_: bf_[:].rearrange("p n t -> p (n t)")

        def gen_P1(c, X):
            st_ = X["set"]
            raw, f_, Vb, Mm, Nn, Pm = X["raw"], X["f_"], X["Vb"], X["Mm"], X["Nn"], X["Pm"]
            Xa = [Mm, X["Xa1"]]
            XTa = [Nn, X["XTa1"]]
            DB, DBb = X["DB"], X["DBb"]
            dtmp = f_[4]
            u_exp = slab_i * NU + 4 + c
            assert ws["load"] > u_exp
            At, Bt, Kt, Rt = st_["At"], st_["Bt"], st_["Kt"], st_["Rt"]
            Btok, Ktok, Vtok = st_["Btok"], st_["Ktok"], st_["Vtok"]
            gT, bonT, WC = st_["gT"], st_["bonT"], st_["WC"]
            Pf, AKT, ARBT, ARKT = st_["Pf"], st_["AKT"], st_["ARBT"], st_["ARKT"]
            wr = ring[u_exp % NRING]
            wr4 = wr[:, 0:3072].rearrange("p (i c n) -> p i c n", i=3, c=8)
            for i in range(3):
                pb = DB[i % 2]
                for kc in range(8):
                    mm(pb[:], wr4[:, i, kc, :], hT[:, kc, :], kc == 0, kc == 7, [wr, hT], [pb])
                lerp_chunk(pb, raw, i, 4 * i + c, dtmp)
                yield
            ws_release_unit(u_exp)
            r_ = raw[:, 0, 1:513]
            k_ = raw[:, 1, 1:513]
            v_ = raw[:, 2, 1:513]
            mm(DB[0][:], w2a2b[0:64, c * 128:(c + 1) * 128], twb[0:64, :], True, True, [w2a2b, twb], [DB[0]])
            mm(DB[1][:], w2a2b[64:128, c * 128:(c + 1) * 128], twb[64:128, :], True, True, [w2a2b, twb], [DB[1]])
            sigw, cl, epos, eneg, eprv, a_ = f_
            act(sigw[:], DB[0][:], AF.Exp, [DB[0], npc], [sigw], bias=npc[:, PC_W0 + c:PC_W0 + c + 1], scale=-1.0)
            act(a_[:], DB[1][:], AF.Exp, [DB[1], npc], [a_], bias=npc[:, PC_A0 + c:PC_A0 + c + 1], scale=-1.0)
            act(sigw[:], sigw[:], AF.Ln, [sigw, one1], [sigw], bias=one1[:], scale=1.0)
            act(a_[:], a_[:], AF.Ln, [a_, one1], [a_], bias=one1[:], scale=1.0)
            act(sigw[:], sigw[:], AF.Exp, [sigw, mhalf], [sigw], bias=mhalf[:], scale=-1.0)
            act(a_[:], a_[:], AF.Exp, [a_], [a_], scale=-1.0)
            mm(DB[0][:], g2b[:, c * 128:(c + 1) * 128], sgl[:], True, True, [g2b, sgl], [DB[0]])
            cp("act", gT[:], DB[0][:], [DB[0]], [gT])
            yield
            P.op("dve", lambda e: e.tensor_tensor_scan(out=cl[:], data0=cmask[:], data1=sigw[:], initial=0.0,
                                                      op0=ALU.mult, op1=ALU.add), [cmask, sigw], [cl])
            act(epos[:], cl[:], AF.Exp, [cl], [epos], scale=-1.0)
            act(eneg[:], cl[:], AF.Exp, [cl], [eneg])
            tt("dve", eprv[:], cl[:], sigw[:], ALU.subtract, [cl, sigw], [eprv])
            act(eprv[:], eprv[:], AF.Exp, [eprv], [eprv], scale=-1.0)
            cp("dve", WC[:], epos[:].rearrange("p (n t) -> p n t", t=64)[:, :, 63], [epos], [WC])
            yield
            kkn, tmp = cl, sigw
            ts("dve", kkn[:], k_, pc[:, PC_KK + c:PC_KK + c + 1], None, ALU.mult, None, [raw, pc], [kkn])
            tt("dve", tmp[:], kkn[:], kkn[:], ALU.mult, [kkn], [tmp])
            mm(DB[0][:], bonesf[:], tmp[:], True, True, [bonesf, tmp], [DB[0]])
            ts("dve", tmp[:], DB[0][:], 1e-24, None, ALU.max, None, [DB[0]], [tmp])
            act(tmp[:], tmp[:], AF.Ln, [tmp], [tmp], scale=float(2.0 ** 40))
            act(tmp[:], tmp[:], AF.Exp, [tmp, ln2x20], [tmp], bias=ln2x20[:], scale=-0.5)
            tt("dve", kkn[:], kkn[:], tmp[:], ALU.mult, [kkn, tmp], [kkn])
            yield
            stt(At[:], kkn[:], -1.0, eprv[:], ALU.mult, ALU.mult, [kkn, eprv], [At])
            tt("dve", tmp[:], kkn[:], a_[:], ALU.mult, [kkn, a_], [tmp])
            tt("dve", Bt[:], tmp[:], eneg[:], ALU.mult, [tmp, eneg], [Bt])
            ts("dve", a_[:], a_[:], -1.0, pc[:, PC_KA + c:PC_KA + c + 1], ALU.add, ALU.mult, [a_, pc], [a_])
            stt(a_[:], a_[:], 1.0, k_, ALU.add, ALU.mult, [a_, raw], [a_])
            tt("dve", Kt[:], a_[:], eneg[:], ALU.mult, [a_, eneg], [Kt])
            tt("dve", Rt[:], r_, epos[:], ALU.mult, [raw, epos], [Rt])
            cp("act", Vb[:], v_, [raw], [Vb])
            yield
            stt(tmp[:], r_, pc[:, PC_RK + c:PC_RK + c + 1], a_[:], ALU.mult, ALU.mult, [raw, pc, a_], [tmp])
            mm(DB[1][:], bonesf[:], tmp[:], True, True, [bonesf, tmp], [DB[1]])
            tt("dve", bonT[:], DB[1][:], v_, ALU.mult, [DB[1], raw], [bonT])
            yield
            for ti, (srcb, dstb) in enumerate(((Bt, Btok), (Kt, Ktok), (Vb, Vtok))):
                pbk = DB[ti % 2]
                pv = DBb[ti % 2]
                for n in range(NCH):
                    for hh in range(2):
                        trp(pv[64 * hh:64 * hh + 64, n * 64:(n + 1) * 64], srcb[64 * hh:64 * hh + 64, n * 64:(n + 1) * 64],
                            identb[64 * hh:64 * hh + 64, 64 * hh:64 * hh + 64], [srcb, identb], [pbk])
                cp("act" if ti == 1 else "dve", dstb[:].rearrange("p n t -> p (n t)"), pv[:, 0:512], [pbk], [dstb])
                yield
            prods = ((Bt, At, Mm, 0), (At, Bt, Nn, 1), (Kt, At, AKT, 0), (Bt, Rt, ARBT, 2), (Kt, Rt, ARKT, 2))
            for pi, (la, rb_, dst, mk) in enumerate(prods):
                bank = DB[pi % 2]
                for n in range(NCH):
                    for hh in range(2):
                        sl = slice(64 * hh, 64 * hh + 64)
                        mm(bank[sl, n * 64:(n + 1) * 64], la[sl, n * 64:(n + 1) * 64], rb_[sl, n * 64:(n + 1) * 64],
                           True, True, [la, rb_], [bank])
                tt("dve", fl(dst), bank[:], masks[:, mk, :], ALU.mult, [bank, masks], [dst])
                yield
            tt("dve", fl(Pm), fl(Mm), masks[:, 3, :], ALU.add, [Mm, masks], [Pm])
            Xc, XTc, Pc = Mm, Nn, Pm
            for lev in range(1, 6):
                lastlev = (lev == 5)
                Xn, XTn = Xa[lev % 2], XTa[lev % 2]
                Pn = Pf if lastlev else Pm
                for n in range(NCH):
                    for hh in range(2):
                        sl = slice(64 * hh, 64 * hh + 64)
                        cs = slice(n * 64, (n + 1) * 64)
                        mm(DB[0][sl, cs], Xc[sl, n, :], XTc[sl, n, :], True, True, [Xc, XTc], [DB[0]])
                        if not lastlev:
                            mm(DB[1][sl, cs], XTc[sl, n, :], Xc[sl, n, :], True, True, [Xc, XTc], [DB[1]])
                cp("act", fl(XTn), DB[0][:], [DB[0]], [XTn])
                if not lastlev:
                    cp("dve", fl(Xn), DB[1][:], [DB[1]], [Xn])
                yield
                for n in range(NCH):
                    for hh in range(2):
                        sl = slice(64 * hh, 64 * hh + 64)
                        cs = slice(n * 64, (n + 1) * 64)
                        mm(DB[0][sl, cs], XTn[sl, n, :], Pc[sl, n, :], True, True, [XTn, Pc], [DB[0]])
                tt("dve", fl(Pn), DB[0][:], fl(Pc), ALU.add, [DB[0], Pc], [Pn])
                yield
                Xc, XTc, Pc = Xn, XTn, Pn

        def gen_P2(c, X):
            st_ = X["set"]
            gst, rhsb, ub, s0w = X["gst"], X["rhsb"], X["ub"], X["s0w"]
            ysq = Buf(X["f_"][1][:].rearrange("p (n t) -> p n t", t=64), X["f_"][1].T)
            DB = X["DB"]
            At, Rt = st_["At"], st_["Rt"]
            Btok, Ktok, Vtok = st_["Btok"], st_["Ktok"], st_["Vtok"]
            gT, bonT, WC = st_["gT"], st_["bonT"], st_["WC"]
            Pf, AKT, ARBT, ARKT, ytok = st_["Pf"], st_["AKT"], st_["ARBT"], st_["ARKT"], st_["ytok"]
            if g == 0:
                P.op("dve", lambda e: e.memset(S32[:, c, :], 0.0), [], [S32T[c]])
                P.op("dve", lambda e: e.memset(Sbf[:, c, :], 0.0), [], [SbfT[c]])
            for n in range(NCH):
                cs = slice(n * 64, (n + 1) * 64)
                tb = DB[n % 2]
                pr, pu, pst, py = tb[:, 0:64], tb[:, 64:128], tb[:, 128:192], tb[:, 192:256]
                rb2, ub2 = rhsb[n % 2], ub[n % 2]
                for hh in range(2):
                    sl = slice(64 * hh, 64 * hh + 64)
                    mm(pr[sl, :], At[sl, cs], Sbf[sl, c, :], True, False, [At, SbfT[c]], [tb])
                    mm(pr[sl, :], AKT[sl, n, :], Vtok[sl, n, :], False, True, [AKT, Vtok], [tb])
                cp("act", rb2[:], pr, [tb], [rb2])
                ts("dve", s0w[:], S32[:, c, :], WC[:, n:n + 1], None, ALU.mult, None, [S32T[c], WC], [s0w])
                yield
                for hh in range(2):
                    sl = slice(64 * hh, 64 * hh + 64)
                    mm(pu[sl, :], Pf[sl, n, :], rb2[sl, :], True, True, [Pf, rb2], [tb])
                cp("act", ub2[:], pu, [tb], [ub2])
                yield
                for hh in range(2):
                    sl = slice(64 * hh, 64 * hh + 64)
                    mm(pst[sl, :], Btok[sl, n, :], ub2[sl, :], True, False, [Btok, ub2], [tb])
                    mm(pst[sl, :], Ktok[sl, n, :], Vtok[sl, n, :], False, True, [Ktok, Vtok], [tb])
                for hh in range(2):
                    sl = slice(64 * hh, 64 * hh + 64)
                    mm(py[sl, :], Rt[sl, cs], Sbf[sl, c, :], True, False, [Rt, SbfT[c]], [tb])
                    mm(py[sl, :], ARBT[sl, n, :], ub2[sl, :], False, False, [ARBT, ub2], [tb])
                    mm(py[sl, :], ARKT[sl, n, :], Vtok[sl, n, :], False, True, [ARKT, Vtok], [tb])
                stt(Sbf[:, c, :], pst, WC[:, n:n + 1], s0w[:], ALU.mult, ALU.add, [tb, WC, s0w], [SbfT[c]])
                stt(S32[:, c, :], pst, WC[:, n:n + 1], s0w[:], ALU.mult, ALU.add, [tb, WC, s0w], [S32T[c]])
                cp("act", ytok[:, n, :], py, [tb], [ytok])
                yield
            P.op("dve", lambda e: e.tensor_reduce(out=gst[:, 0, :], in_=ytok[:], axis=AX.X, op=ALU.add), [ytok], [gst])
            tt("dve", ysq[:], ytok[:], ytok[:], ALU.mult, [ytok], [ysq])
            P.op("dve", lambda e: e.tensor_reduce(out=gst[:, 1, :], in_=ysq[:], axis=AX.X, op=ALU.add), [ysq], [gst])
            ts("dve", gst[:, 2, :], gst[:, 0, :], 1.0 / 64.0, None, ALU.mult, None, [gst], [gst])
            tt("dve", gst[:, 0, :], gst[:, 2, :], gst[:, 2, :], ALU.mult, [gst], [gst])
            stt(gst[:, 3, :], gst[:, 1, :], 1.0 / 64.0, gst[:, 0, :], ALU.mult, ALU.subtract, [gst], [gst])
            act(gst[:, 3, :], gst[:, 3, :], AF.Ln, [gst, epsx], [gst], bias=epsx[:], scale=1.0)
            act(gst[:, 3, :], gst[:, 3, :], AF.Exp, [gst], [gst], scale=-0.5)
            yield
            tt("dve", ysq[:], ytok[:], bc_last(gst[:, 2, :], 64), ALU.subtract, [ytok, gst], [ysq])
            tt("dve", ysq[:], ysq[:], bc_last(gst[:, 3, :], 64), ALU.mult, [ysq, gst], [ysq])
            for n in range(NCH):
                for hh in range(2):
                    sl = slice(64 * hh, 64 * hh + 64)
                    mm(DB[0][sl, n * 64:(n + 1) * 64], ysq[sl, n, :], identf[sl, 64 * hh:64 * hh + 64], True, True,
                       [ysq, identf], [DB[0]])
            etmp = ysq[:].rearrange("p n t -> p (n t)")
            act(etmp, DB[0][:], AF.Identity, [DB[0], pc], [ysq], bias=pc[:, PC_LXB + c:PC_LXB + c + 1],
                scale=pc[:, PC_LXG + c:PC_LXG + c + 1])
            tt("dve", etmp, etmp, bonT[:], ALU.add, [ysq, bonT], [ysq])
            tt("dve", ybT_ap[:, c, :], etmp, gT[:], ALU.mult, [ysq, gT], [ybT_T])
            yield

        def d_thread(t):
            for c in (t, t + 2):
                for _ in gen_P1(c, ctx[t]):
                    yield
                for _ in gen_P2(c, ctx[t]):
                    yield

        gC = gen_C()
        _drain(_sched(P, [gC, gen_D_lora()], stop_when=1))
        _drain(_sched(P, [gC, d_thread(0), d_thread(1)], bias=[(25.0 if g >= 2 else (8.0 if g == 1 else 0.0)), 0.0, 0.0]))
        if dbg_here:
            dump("yaT", Buf(yaT_ap, yaT_T), [128, 4, TOK], BF16)
            dump("ybT", Buf(ybT_ap, ybT_T), [128, 4, TOK], BF16)

        RE = RegionAlloc(0)
        mT = RE([128, 8, TOK], BF16)
        sga = [RE([128, TOK], F32) for _ in range(2)]
        sgb = [RE([128, TOK], F32) for _ in range(2)]
        n1 = [RE([128, D], F32) for _ in range(2)]
        lnbuf = RE([128, 2048], F32)
        dma("sp", lnbuf[:], pbc_d[:, PB_LN1:PB_LN1 + 2048], [pbc_d], [lnbuf])
        for f in range(8):
            wf = ws_get()
            ga3 = wf[:, 0:1024].rearrange("p (c n) -> p c n", c=8)
            gb3 = wf[:, 1024:2048].rearrange("p (c n) -> p c n", c=8)
            ua3 = wf[:, 2048:2560].rearrange("p (c n) -> p c n", c=4)
            ub3 = wf[:, 2560:3072].rearrange("p (c n) -> p c n", c=4)
            o4 = 4 * (f % 2)
            pga, pgb, pua, pub = ps[o4], ps[o4 + 1], ps[o4 + 2], ps[o4 + 3]
            wr_ = [pub]
            for kc in range(8):
                mm(pga[:], ga3[:, kc, :], hT[:, kc, :], kc == 0, kc == 7, [wf, hT], [pga])
            for kc in range(8):
                mm(pgb[:], gb3[:, kc, :], hT[:, kc, :], kc == 0, kc == 7, [wf, hT], [pgb])
            for kc in range(4):
                mm(pua[:], ua3[:, kc, :], yaT_ap[:, kc, :], kc == 0, kc == 3, [wf, yaT_T], [pua])
            for kc in range(4):
                mm(pub[:], ub3[:, kc, :], ybT_ap[:, kc, :], kc == 0, kc == 3, [wf, ybT_T], wr_)
            ws_release()
            sa, sb_ = sga[f % 2], sgb[f % 2]
            act(sa[:], pga[:], AF.Sigmoid, [pga], [sa])
            act(sb_[:], pgb[:], AF.Sigmoid, [pgb], [sb_])
            tt("dve", sa[:], sa[:], pua[:], ALU.mult, [sa, pua], [sa])
            tt("dve", sb_[:], sb_[:], pub[:], ALU.mult, [sb_] + wr_, [sb_])
            tt("dve", mT[:, f, :], sa[:], sb_[:], ALU.add, [sa, sb_], [mT])
        if dbg_here:
            dump("mT", mT, [128, 8, TOK], BF16)
        for half in range(2):
            wo = ws_get()
            wo3 = wo[:, 0:4096].rearrange("p (c n) -> p c n", c=8)
            for j in range(4):
                pb = ps[(2 * half + j) % 4]
                for f in range(8):
                    mm(pb[:], mT[:, f, j * 128:(j + 1) * 128], wo3[:, f, :], f == 0, f == 7, [wo, mT], [pb])
                stt(hres_t[:, j, half * 512:(half + 1) * 512], hres_t[:, j, half * 512:(half + 1) * 512], ALPHA, pb[:],
                    ALU.mult, ALU.add, [hresT[j], pb], [hresT[j]])
            ws_release()
        h1T_T = T(init_reads=P.snapshot(), t0=P.now())
        for j in range(4):
            nb = n1[j % 2]
            mv, rs = layer_norm_stats(hres_t[:, j, :], hresT[j], j % 2)
            ts("dve", nb[:], hres_t[:, j, :], mv[:, 0:1], rs[:], ALU.subtract, ALU.mult, [hresT[j], mv, rs], [nb])
            tt("pool", hres_t[:, j, :], nb[:], lnbuf[:, 0:1024], ALU.mult, [nb, lnbuf], [hresT[j]])
            tt("pool", hres_t[:, j, :], hres_t[:, j, :], lnbuf[:, 1024:2048], ALU.add, [hresT[j], lnbuf], [hresT[j]])
            transposes_to_featmajor(nb, nb, yab, h1T_T, j, PC_LN1_G, PC_LN1_B, (ps[4], ps[5]))

        RF = RegionAlloc(0)
        actT = RF([128, NHC, TOK], BF16)
        sil = [RF([128, TOK], F32) for _ in range(2)]
        n2 = [RF([128, D], F32) for _ in range(2)]
        lnbuf = RF([128, 2048], F32)
        dma("sp", lnbuf[:], pbc_d[:, PB_LN2:PB_LN2 + 2048], [pbc_d], [lnbuf])
        for u in range(11):
            wf = ws_get()
            w4 = wf[:, 0:4096].rearrange("p (a c n) -> p a c n", a=2, c=8)
            for q in range(2):
                hc = 2 * u + q
                pg, pu_ = ps[2 * (hc % 3)], ps[2 * (hc % 3) + 1]
                for kc in range(8):
                    mm(pg[:], w4[:, 0, kc, q * 128:(q + 1) * 128], yab[:, kc, :], kc == 0, kc == 7, [wf, h1T_T], [pg])
                for kc in range(8):
                    mm(pu_[:], w4[:, 1, kc, q * 128:(q + 1) * 128], yab[:, kc, :], kc == 0, kc == 7, [wf, h1T_T], [pu_])
                sl_ = sil[hc % 2]
                act(sl_[:], pg[:], AF.Silu, [pg], [sl_])
                tt("dve", actT[:, hc, :], sl_[:], pu_[:], ALU.mult, [sl_, pu_], [actT])
            ws_release()
        for u in range(8):
            wdn = ws_get()
            wd3 = wdn[:, 0:NHC * 128].rearrange("p (c n) -> p c n", c=NHC)
            pb = ps[u % 2]
            for j in range(4):
                for hc in range(NHC):
                    mm(pb[:, j * 128:(j + 1) * 128], actT[:, hc, j * 128:(j + 1) * 128], wd3[:, hc, :], hc == 0, hc == NHC - 1,
                       [wdn, actT], [pb])
            ws_release()
            for j in range(4):
                stt(hres_t[:, j, u * 128:(u + 1) * 128], hres_t[:, j, u * 128:(u + 1) * 128], ALPHA, pb[:, j * 128:(j + 1) * 128],
                    ALU.mult, ALU.add, [hresT[j], pb], [hresT[j]])
        for j in range(4):
            nb = n2[j % 2]
            mv, rs = layer_norm_stats(hres_t[:, j, :], hresT[j], j % 2)
            ts("dve", nb[:], hres_t[:, j, :], mv[:, 0:1], rs[:], ALU.subtract, ALU.mult, [hresT[j], mv, rs], [nb])
            tt("dve", nb[:], nb[:], lnbuf[:, 0:1024], ALU.mult, [nb, lnbuf], [nb])
            tt("pool", nb[:], nb[:], lnbuf[:, 1024:2048], ALU.add, [nb, lnbuf], [nb])
            yT_ = T()
            y_stores.append(yT_)
            dma("sp", y_d[b, t0 + j * 128:t0 + (j + 1) * 128, :], nb[:], [nb], [yT_])

    P.final_wait("sp", y_stores + list(dbg_outs.values()))
    P.emit()
    return nc, dbg_outs


def _t5_bucket_np(dist):
    d = np.maximum(dist, 1).astype(np.float32)
    large = 16 + (np.log(d / np.float32(16)) / np.float32(math.log(128 / 16)) * np.float32(16)).astype(np.int32)
    large = np.minimum(large, 31)
    return np.where(dist < 16, dist, large)


def _host_constants():
    ohu = np.zeros((33, 383), np.float32)
    for i in range(383):
        if i < 127:
            ohu[32, i] = MASKVAL
        else:
            bkt = int(_t5_bucket_np(np.array([i - 127], np.int32))[0])
            ohu[bkt, i] += 8.0
            ohu[31, i] -= 8.0
    p = np.arange(128)[:, None] % 64
    t = np.arange(64)[None, :]
    m = np.stack([(p < t), (t < p), (p <= t), (p == t)], 0).astype(np.float32)
    masks = np.ascontiguousarray(np.broadcast_to(m[:, :, None, :], (4, 128, 8, 64)).transpose(1, 0, 2, 3).reshape(128, 4, 512))
    ident = np.eye(128, dtype=np.float32)
    jmat = np.ascontiguousarray(ident[::-1])
    bones = np.zeros((128, 128), np.float32)
    bones[:64, :64] = 1.0
    bones[64:, 64:] = 1.0
    cm = np.ones((128, 512), np.float32)
    cm[:, ::64] = 0.0
    return dict(ohu=ohu, masks=masks, ident=ident, jmat=jmat, bones=bones, cmask=cm)


def _prep_inputs(inp):
    f = lambda a: np.ascontiguousarray(np.asarray(a, dtype=np.float32))
    row = lambda a: np.asarray(a, np.float32).reshape(-1)
    pbc_row = np.concatenate([row(inp["ln_in_g"]), row(inp["ln_in_b"]), row(inp["ln1_g"]), row(inp["ln1_b"]),
                              row(inp["ln2_g"]), row(inp["ln2_b"]), row(inp["diff_subln_g"]),
                              row(inp["diff_lam_q1"]), row(inp["diff_lam_k1"]), row(inp["diff_lam_q2"]),
                              row(inp["diff_lam_k2"]), row(np.asarray(inp["rel_bias"])[31])])
    assert pbc_row.shape[0] == PB_N
    pbc = np.ascontiguousarray(np.broadcast_to(pbc_row[None, :], (128, PB_N)))
    col = lambda a: row(a).reshape(-1, 128).T
    pcol = np.ascontiguousarray(np.concatenate(
        [col(inp["ln_in_g"]), col(inp["ln_in_b"]), col(inp["ln1_g"]), col(inp["ln1_b"]), col(inp["rwkv_mu"]),
         col(inp["rwkv_w0"]), col(inp["rwkv_a0"]), col(inp["rwkv_k_k"]), col(inp["rwkv_k_a"]), col(inp["rwkv_r_k"]),
         col(inp["rwkv_lnx_g"]), col(inp["rwkv_lnx_b"])], axis=1))
    assert pcol.shape == (128, PC_N)
    w2a2 = np.ascontiguousarray(np.concatenate([np.asarray(inp["rwkv_w2"], np.float32)[0],
                                                np.asarray(inp["rwkv_a2"], np.float32)[0]], 0))
    rbaug = np.ascontiguousarray(np.concatenate([np.asarray(inp["rel_bias"], np.float32), np.ones((1, 4), np.float32)], 0))
    shared = dict(w_in=f(inp["w_in"][0]), w_up_a=f(inp["w_up_a"][0]), w_up_b=f(inp["w_up_b"][0]), w_out=f(inp["w_out"][0]),
                  wg=f(inp["ffn_w_gate"][0]), wu=f(inp["ffn_w_up"][0]), wd=f(inp["ffn_w_down"][0]),
                  pbc=pbc, pcol=pcol, w2a2=w2a2, g2=f(inp["rwkv_g2"][0]), rbaug=rbaug)
    shared.update(_host_constants())
    return shared


_CACHE = {}


def kernel(**inputs):
    x = np.asarray(inputs["x"], np.float32)
    shared = _prep_inputs(inputs)
    if "nc" not in _CACHE:
        _CACHE["nc"] = build_program()[0]
    nc = _CACHE["nc"]
    in_maps = []
    for c in range(NCORES):
        m = dict(shared)
        m["x"] = np.ascontiguousarray(x[BL * c:BL * (c + 1)])
        in_maps.append(m)
    res = run_bass_kernel_spmd(nc, in_maps, core_ids=list(range(NCORES)))
    out = np.concatenate([np.asarray(r["y"], np.float32) for r in res.results], axis=0)
    return out
```
